# Optimizing a Trainium2 kernel written in Bass

```python
import math
import jax, jax.numpy as jnp
from jax import lax
import numpy as np

D_MODEL = 2048
BATCH = 8
SEQ = 4096
DEPTH = 4
DEC_BATCH = 2
DEC_SEQ = 4096
PAST_LEN = 128

HEAD_DIM = 128
GRID_W = 64
A_HEADS = 8
A_KV = 2
A_WINDOW = 128
B_HEADS = 8
B_KV = 2
ROPE_THETA = 10000.0
QBLK = 128
C_HEADS = 8
C_KV = 2
NA_ROWS = 8
NA_COLS = 16
NA_QCOLS = 16
NA_KSPAN = 32
D_PAIRS = ((128, 1), (512, 4), (2048, 16))
D_SLOTS = 4
D_HEADS = D_SLOTS * len(D_PAIRS)
T5_BUCKETS = 32
T5_MAX_DIST = 2048
T5_COLS = A_HEADS + D_HEADS
D_FF = 5632
NORM_EPS = 1e-6
NEG_INF = -1e30
ATTN_SCALE = HEAD_DIM ** -0.5

AB_SIZES = (A_HEADS * HEAD_DIM, A_KV * HEAD_DIM, A_KV * HEAD_DIM,
            B_HEADS * HEAD_DIM, B_KV * HEAD_DIM, B_KV * HEAD_DIM)
CD_SIZES = (C_HEADS * HEAD_DIM, C_KV * HEAD_DIM, C_KV * HEAD_DIM,
            D_HEADS * HEAD_DIM, D_SLOTS * HEAD_DIM, D_SLOTS * HEAD_DIM)
AB_IN = sum(AB_SIZES)
AB_OUT = (A_HEADS + B_HEADS) * HEAD_DIM
CD_IN = sum(CD_SIZES)
CD_OUT = (C_HEADS + D_SLOTS) * HEAD_DIM
N_EVEN = (DEPTH + 1) // 2
N_ODD = DEPTH // 2

kernel_name = 'hybrid_bidir_encoder_macaron'


def _rms_norm(x, g):
    xf = x.astype(jnp.float32)
    y = xf * lax.rsqrt(jnp.mean(xf * xf, axis=-1, keepdims=True) + NORM_EPS)
    return (y * g.astype(jnp.float32)).astype(x.dtype)


def _swiglu(x, w_in, w_out):
    gate, up = jnp.split(x @ w_in, 2, axis=-1)
    return (jax.nn.silu(gate) * up) @ w_out


def _split(x, sizes):
    idx = [int(i) for i in np.cumsum(sizes)[:-1]]
    return jnp.split(x, idx, axis=-1)


def _t5_bucket(rel):
    half = T5_BUCKETS // 2
    max_exact = half // 2
    n = jnp.abs(rel)
    nf = jnp.maximum(n, 1).astype(jnp.float32)
    large = max_exact + (jnp.log(nf / max_exact) / math.log(T5_MAX_DIST / max_exact)
                         * (half - max_exact)).astype(jnp.int32)
    large = jnp.minimum(large, half - 1)
    return jnp.where(rel > 0, half, 0) + jnp.where(n < max_exact, n, large)


def _t5_bias(table_cols, rel, n_kv):
    tb = table_cols.astype(jnp.float32)[_t5_bucket(rel)]
    h = table_cols.shape[1]
    return jnp.moveaxis(tb, -1, 0).reshape((n_kv, h // n_kv) + rel.shape)


def _band_rel(blk):
    iq = jnp.arange(blk)[:, None]
    j = jnp.arange(3 * blk)[None, :]
    return j - blk - iq


def _banded_attention(q, k, v, bias, band, blk):
    n, L, kv, g, dh = q.shape
    nb = -(-L // blk)
    pad = nb * blk - L
    qp = jnp.pad(q, ((0, 0), (0, pad), (0, 0), (0, 0), (0, 0)))
    kp = jnp.pad(k, ((0, 0), (blk, blk + pad), (0, 0), (0, 0)))
    vp = jnp.pad(v, ((0, 0), (blk, blk + pad), (0, 0), (0, 0)))
    idx = jnp.arange(nb)[:, None] * blk + jnp.arange(3 * blk)[None, :]
    kpos = idx - blk
    valid = band[None] & ((kpos >= 0) & (kpos < L))[:, None, :]
    kb = kp[:, idx]
    vb = vp[:, idx]
    qb = qp.reshape(n, nb, blk, kv, g, dh)
    s = jnp.einsum('nbqkgd,nbjkd->nbkgqj', qb, kb).astype(jnp.float32) * ATTN_SCALE + bias
    s = jnp.where(valid[None, :, None, None], s, NEG_INF)
    m = jnp.max(s, axis=-1, keepdims=True)
    p = jnp.exp(s - m)
    l = jnp.sum(p, axis=-1, keepdims=True)
    o = jnp.einsum('nbkgqj,nbjkd->nbqkgd', (p / l).astype(v.dtype), vb).astype(jnp.float32)
    lse = jnp.transpose((m + jnp.log(l))[..., 0], (0, 1, 4, 2, 3))
    o = o.reshape(n, nb * blk, kv, g, dh)[:, :L]
    lse = lse.reshape(n, nb * blk, kv, g)[:, :L]
    return o, lse


def _axial_rope(x, t):
    n_pairs = HEAD_DIM // 2
    n_freq = n_pairs // 2
    pos = jnp.arange(t)
    row = (pos // GRID_W).astype(jnp.float32)
    col = (pos % GRID_W).astype(jnp.float32)
    omega = ROPE_THETA ** (-(jnp.arange(n_freq, dtype=jnp.float32) * 2.0 / n_pairs))
    ang = jnp.concatenate([row[:, None] * omega, col[:, None] * omega], axis=-1)
    bshape = (1, t) + (1,) * (x.ndim - 3) + (n_pairs,)
    cos = jnp.cos(ang).reshape(bshape)
    sin = jnp.sin(ang).reshape(bshape)
    xr = x.astype(jnp.float32).reshape(x.shape[:-1] + (n_pairs, 2))
    x0, x1 = xr[..., 0], xr[..., 1]
    out = jnp.stack([x0 * cos - x1 * sin, x0 * sin + x1 * cos], axis=-1)
    return out.reshape(x.shape).astype(x.dtype)


def _dense_blocked(q, k, v):
    b, t = q.shape[:2]
    nb = t // QBLK
    qb = jnp.moveaxis(q.reshape((b, nb, QBLK) + q.shape[2:]), 1, 0)

    def one_block(qblk):
        s = jnp.einsum('bqkgd,btkd->bkgqt', qblk, k).astype(jnp.float32) * ATTN_SCALE
        p = jax.nn.softmax(s, axis=-1)
        return jnp.einsum('bkgqt,btkd->bqkgd', p.astype(v.dtype), v)

    o = lax.map(one_block, qb)
    return jnp.moveaxis(o, 0, 1).reshape(b, t, -1)


def _neighbourhood_attention(q, k, v, rpb):
    b, t, kvh, g, dh = q.shape
    rows = t // GRID_W
    kr = min(NA_ROWS, rows)
    ncb = GRID_W // NA_QCOLS
    r = jnp.arange(rows)
    rs = jnp.clip(r - NA_ROWS // 2, 0, rows - kr)
    row_idx = rs[:, None] + jnp.arange(kr)[None, :]
    jb = jnp.arange(ncb)
    span0 = jnp.clip(jb * NA_QCOLS - NA_COLS // 2, 0, GRID_W - NA_KSPAN)
    col_idx = span0[:, None] + jnp.arange(NA_KSPAN)[None, :]
    qc = jb[:, None] * NA_QCOLS + jnp.arange(NA_QCOLS)[None, :]
    cs = jnp.clip(qc - NA_COLS // 2, 0, GRID_W - NA_COLS)
    colmask = (col_idx[:, None, :] >= cs[..., None]) & (col_idx[:, None, :] < cs[..., None] + NA_COLS)
    nkeys = kr * NA_KSPAN
    mask = jnp.broadcast_to(colmask[:, :, None, :], (ncb, NA_QCOLS, kr, NA_KSPAN)).reshape(ncb, NA_QCOLS, nkeys)
    ri = row_idx[:, None, :, None]
    ci = col_idx[None, :, None, :]
    kg = k.reshape(b, rows, GRID_W, kvh, dh)[:, ri, ci].reshape(b, rows, ncb, nkeys, kvh, dh)
    vg = v.reshape(b, rows, GRID_W, kvh, dh)[:, ri, ci].reshape(b, rows, ncb, nkeys, kvh, dh)
    qg = q.reshape(b, rows, ncb, NA_QCOLS, kvh, g, dh)
    dr_i = (row_idx - r[:, None]) + NA_ROWS - 1
    dc_i = jnp.clip(col_idx[:, None, :] - qc[..., None] + NA_COLS - 1, 0, 2 * NA_COLS - 2)
    rb = rpb.astype(jnp.float32)[:, dr_i[:, None, None, :, None], dc_i[None, :, :, None, :]]
    rb = jnp.transpose(rb.reshape(kvh, g, rows, ncb, NA_QCOLS, nkeys), (2, 3, 0, 1, 4, 5))
    s = jnp.einsum('brjqkgd,brjnkd->brjkgqn', qg, kg).astype(jnp.float32) * ATTN_SCALE
    s = jnp.where(mask[None, None, :, None, None], s + rb[None], NEG_INF)
    p = jax.nn.softmax(s, axis=-1)
    o = jnp.einsum('brjkgqn,brjnkd->brjqkgd', p.astype(v.dtype), vg)
    return o.reshape(b, t, kvh * g * dh)


def _to_sub(a, d):
    b, t = a.shape[:2]
    a = a.reshape((b, t // d, d) + a.shape[2:])
    return jnp.swapaxes(a, 1, 2).reshape((b * d, t // d) + a.shape[3:])


def _from_sub(a, b, d):
    n, u = a.shape[:2]
    a = a.reshape((b, d, u) + a.shape[2:])
    return jnp.swapaxes(a, 1, 2).reshape((b, u * d) + a.shape[3:])


def _mixer_ab(h, w_in, sink, q_gain, k_gain, w_out, t5_table):
    b, t, _ = h.shape
    qa, ka, va, qb, kb, vb = _split(h @ w_in, AB_SIZES)
    ga = A_HEADS // A_KV
    qa = qa.reshape(b, t, A_KV, ga, HEAD_DIM)
    ka = ka.reshape(b, t, A_KV, HEAD_DIM)
    va = va.reshape(b, t, A_KV, HEAD_DIM)
    rel = _band_rel(A_WINDOW)
    band = jnp.abs(rel) <= A_WINDOW
    bias_a = _t5_bias(t5_table[:, :A_HEADS], rel, A_KV)
    o_a, lse_a = _banded_attention(qa, ka, va, bias_a, band, A_WINDOW)
    keep = jax.nn.sigmoid(lse_a - sink.astype(jnp.float32).reshape(A_KV, ga))
    o_a = (o_a * keep[..., None]).astype(h.dtype).reshape(b, t, -1)
    gb = B_HEADS // B_KV
    qb = _axial_rope(_rms_norm(qb.reshape(b, t, B_KV, gb, HEAD_DIM), q_gain), t)
    kb = _axial_rope(_rms_norm(kb.reshape(b, t, B_KV, HEAD_DIM), k_gain), t)
    o_b = _dense_blocked(qb, kb, vb.reshape(b, t, B_KV, HEAD_DIM)).astype(h.dtype)
    return jnp.concatenate([o_a, o_b], axis=-1) @ w_out


def _mixer_cd(h, w_in, rpb, w_out, t5_table):
    b, t, _ = h.shape
    qc, kc, vc, qd, kd, vd = _split(h @ w_in, CD_SIZES)
    gc = C_HEADS // C_KV
    o_c = _neighbourhood_attention(qc.reshape(b, t, C_KV, gc, HEAD_DIM),
                                   kc.reshape(b, t, C_KV, HEAD_DIM),
                                   vc.reshape(b, t, C_KV, HEAD_DIM), rpb).astype(h.dtype)
    qd = qd.reshape(b, t, len(D_PAIRS), D_SLOTS, 1, HEAD_DIM)
    kd = kd.reshape(b, t, D_SLOTS, HEAD_DIM)
    vd = vd.reshape(b, t, D_SLOTS, HEAD_DIM)
    outs, lses = [], []
    for gi, (win, dil) in enumerate(D_PAIRS):
        hs = win // (2 * dil)
        rel_sub = _band_rel(hs)
        band = jnp.abs(rel_sub) <= hs
        col0 = A_HEADS + gi * D_SLOTS
        bias_g = _t5_bias(t5_table[:, col0:col0 + D_SLOTS], rel_sub * dil, D_SLOTS)
        o_g, lse_g = _banded_attention(_to_sub(qd[:, :, gi], dil), _to_sub(kd, dil),
                                       _to_sub(vd, dil), bias_g, band, hs)
        outs.append(_from_sub(o_g, b, dil))
        lses.append(_from_sub(lse_g, b, dil))
    wts = jax.nn.softmax(jnp.stack(lses), axis=0)
    o_d = jnp.sum(wts[..., None] * jnp.stack(outs), axis=0).astype(h.dtype).reshape(b, t, -1)
    return jnp.concatenate([o_c, o_d], axis=-1) @ w_out


def _trunk(x, p):
    for l in range(DEPTH):
        i = l // 2
        x = x + 0.5 * _swiglu(_rms_norm(x, p['norm_ffn1'][l]), p['ffn1_w_in'][l], p['ffn1_w_out'][l])
        hn = _rms_norm(x, p['norm_mix'][l])
        if l % 2 == 0:
            x = x + _mixer_ab(hn, p['ab_w_in'][i], p['ab_sink'][i], p['ab_q_gain'][i],
                              p['ab_k_gain'][i], p['ab_w_out'][i], p['t5_table'])
        else:
            x = x + _mixer_cd(hn, p['cd_w_in'][i], p['cd_rpb'][i], p['cd_w_out'][i], p['t5_table'])
        x = x + 0.5 * _swiglu(_rms_norm(x, p['norm_ffn2'][l]), p['ffn2_w_in'][l], p['ffn2_w_out'][l])
    return _rms_norm(x, p['final_norm'])


def setup_inputs(seed: int = 0) -> dict:
    key = jax.random.key(seed)
    ks = jax.random.split(key, 20)
    f32 = jnp.float32

    def w(k, shape, fan_in):
        return jax.random.normal(k, shape, f32) * fan_in ** -0.5

    def gain(k, shape):
        return 1.0 + 0.02 * jax.random.normal(k, shape, f32)

    return {
        'x_prompt': jax.random.normal(ks[0], (BATCH, SEQ, D_MODEL), f32),
        'x_sample': jax.random.normal(ks[1], (DEC_BATCH, DEC_SEQ, D_MODEL), f32),
        'norm_ffn1': gain(ks[2], (DEPTH, D_MODEL)),
        'ffn1_w_in': w(ks[3], (DEPTH, D_MODEL, 2 * D_FF), D_MODEL),
        'ffn1_w_out': w(ks[4], (DEPTH, D_FF, D_MODEL), D_FF),
        'norm_mix': gain(ks[5], (DEPTH, D_MODEL)),
        'ab_w_in': w(ks[6], (N_EVEN, D_MODEL, AB_IN), D_MODEL),
        'ab_sink': jax.random.normal(ks[7], (N_EVEN, A_HEADS), f32),
        'ab_q_gain': gain(ks[8], (N_EVEN, HEAD_DIM)),
        'ab_k_gain': gain(ks[9], (N_EVEN, HEAD_DIM)),
        'ab_w_out': w(ks[10], (N_EVEN, AB_OUT, D_MODEL), AB_OUT),
        'cd_w_in': w(ks[11], (N_ODD, D_MODEL, CD_IN), D_MODEL),
        'cd_rpb': 0.1 * jax.random.normal(ks[12], (N_ODD, C_HEADS, 2 * NA_ROWS - 1, 2 * NA_COLS - 1), f32),
        'cd_w_out': w(ks[13], (N_ODD, CD_OUT, D_MODEL), CD_OUT),
        'norm_ffn2': gain(ks[14], (DEPTH, D_MODEL)),
        'ffn2_w_in': w(ks[15], (DEPTH, D_MODEL, 2 * D_FF), D_MODEL),
        'ffn2_w_out': w(ks[16], (DEPTH, D_FF, D_MODEL), D_FF),
        't5_table': 0.1 * jax.random.normal(ks[17], (T5_BUCKETS, T5_COLS), f32),
        'final_norm': gain(ks[18], (D_MODEL,)),
    }


def reference(x_prompt, x_sample, norm_ffn1, ffn1_w_in, ffn1_w_out, norm_mix, ab_w_in, ab_sink,
              ab_q_gain, ab_k_gain, ab_w_out, cd_w_in, cd_rpb, cd_w_out, norm_ffn2, ffn2_w_in,
              ffn2_w_out, t5_table, final_norm):
    params = {
        'norm_ffn1': norm_ffn1, 'ffn1_w_in': ffn1_w_in, 'ffn1_w_out': ffn1_w_out,
        'norm_mix': norm_mix, 'ab_w_in': ab_w_in, 'ab_sink': ab_sink,
        'ab_q_gain': ab_q_gain, 'ab_k_gain': ab_k_gain, 'ab_w_out': ab_w_out,
        'cd_w_in': cd_w_in, 'cd_rpb': cd_rpb, 'cd_w_out': cd_w_out,
        'norm_ffn2': norm_ffn2, 'ffn2_w_in': ffn2_w_in, 'ffn2_w_out': ffn2_w_out,
        't5_table': t5_table, 'final_norm': final_norm,
    }
    y_prompt = _trunk(x_prompt, params)
    y_sample = _trunk(x_sample, params)
    return (y_prompt, y_sample)
```

```python
import contextlib
import math
import numpy as np
import concourse.bass as bass
import concourse.mybir as mybir
from concourse.bass_utils import run_bass_kernel_spmd

F32 = mybir.dt.float32
BF16 = mybir.dt.bfloat16
ALU = mybir.AluOpType
AF = mybir.ActivationFunctionType
ENG = ("pe", "act", "dve", "pool", "sp")

HD = 128
GRID_W = 64
EPS = 1e-6
SCALE = HD ** -0.5
N_CORES = 8


class Cfg:
    def __init__(self, D=2048, FF=5632, S=4096, NSEQ=2, DEPTH=4, T=512):
        self.D, self.FF, self.S, self.NSEQ, self.DEPTH, self.T = D, FF, S, NSEQ, DEPTH, T
        self.KC = D // 128
        self.FFC = FF // 128
        self.NT = S // T
        self.TT = T // 128
        self.n_even = (DEPTH + 1) // 2
        self.n_odd = DEPTH // 2
        off = 0
        self.woff = []
        self.lbase = []
        for l in range(DEPTH):
            d = {}
            self.lbase.append(off)
            inc = 24 if l % 2 == 0 else 32
            okc = 16 if l % 2 == 0 else 12
            for name, nblk, L in (("f1in", self.FFC, self.KC * 256), ("f1out", self.KC, self.FFC * 128),
                                  ("min", inc, self.KC * 128), ("mout", self.KC, okc * 128),
                                  ("f2in", self.FFC, self.KC * 256), ("f2out", self.KC, self.FFC * 128)):
                d[name] = (off - self.lbase[l], nblk, L)
                off += nblk * 128 * L
            self.woff.append(d)
        self.NW = off
        self.lbase.append(off)
        self.SLOT = max(self.KC * 256, self.FFC * 128, 16 * 128)


class Buf:
    __slots__ = ("w", "r", "const")

    def __init__(self, const=False):
        self.w = None
        self.r = {}
        self.const = const


class Bld:
    def __init__(self, nc, es, nsets=4, ndma=7):
        self.nc = nc
        self.q = {e: [] for e in ENG}
        self.nsets, self.ndma, self.cur = nsets, ndma, 0
        self.psem = [{e: es.enter_context(nc.semaphore(f"p{s}{e}")) for e in ENG} for s in range(nsets)]
        self.pcnt = [{e: 0 for e in ENG} for s in range(nsets)]
        self.dq = ("sp", "pool")
        self.dsem = [{qn: [es.enter_context(nc.semaphore(f"d{s}{qn}{i}")) for i in range(ndma)] for qn in self.dq}
                     for s in range(nsets)]
        self.dcnt = [{qn: [0] * ndma for qn in self.dq} for s in range(nsets)]
        self.drr = {qn: 0 for qn in self.dq}
        self.last = {e: None for e in ENG}
        self.dlast = {}
        self.nops = 0

    def seg(self):
        self.cur = (self.cur + 1) % self.nsets

    @staticmethod
    def _add(deps, t):
        if t is None:
            return
        k = t[2]
        if k not in deps or deps[k][1] < t[1]:
            deps[k] = t

    def _deps(self, reads, writes):
        deps = {}
        for b in reads:
            self._add(deps, b.w)
        for b in writes:
            self._add(deps, b.w)
            for t in b.r.values():
                self._add(deps, t)
        return deps

    def _commit(self, tok, reads, writes):
        for b in reads:
            if not b.const:
                b.r[tok[2]] = tok
        for b in writes:
            b.w = tok
            b.r = {}

    def op(self, eng, fn, reads=(), writes=()):
        deps = self._deps(reads, writes)
        s = self.cur
        key = ("p", s, eng)
        if eng == "pe":
            for k in [k for k in deps if k[0] == "p" and k[2] == "pe"]:
                del deps[k]
        self.pcnt[s][eng] += 1
        tok = (self.psem[s][eng], self.pcnt[s][eng], key)
        self.q[eng].append((fn, list(deps.values()), tok[0], 1))
        self._commit(tok, reads, writes)
        self.last[eng] = tok
        self.nops += 1
        return tok

    def dma(self, qn, out, in_, reads=(), writes=()):
        deps = self._deps(reads, writes)
        s = self.cur
        i = self.drr[qn]
        self.drr[qn] = (i + 1) % self.ndma
        key = ("d", s, qn, i)
        self._add(deps, self.dlast.get(key))
        self.dcnt[s][qn][i] += 16
        tok = (self.dsem[s][qn][i], self.dcnt[s][qn][i], key)
        self.dlast[key] = tok
        self.q[qn].append((lambda e, o=out, i_=in_: e.dma_start(out=o, in_=i_), list(deps.values()), tok[0], 16))
        self._commit(tok, reads, writes)
        self.nops += 1
        return tok

    def barrier(self):
        toks = [t for t in self.last.values() if t is not None] + list(self.dlast.values())
        for e in ENG:
            self.q[e].append((None, list(toks), None, 0))

    def emit(self, block):
        def mk(name):
            def f(e):
                seen = {}
                for fn, deps, sem, inc in self.q[name]:
                    for t in deps:
                        if seen.get(t[2], 0) < t[1]:
                            e.wait_ge(t[0], t[1])
                            seen[t[2]] = t[1]
                    if fn is not None:
                        fn(e).then_inc(sem, inc)
            return f
        block.tensor(mk("pe"))
        block.scalar(mk("act"))
        block.vector(mk("dve"))
        block.gpsimd(mk("pool"))
        block.sync(mk("sp"))


class Arena:
    def __init__(self, t, nbytes):
        self.t, self.n, self.p = t, nbytes, 0

    def reset(self, p=0):
        self.p = p

    def get(self, dtype, *free):
        esz = 4 if dtype == F32 else 2
        n = int(np.prod(free)) * esz
        n_al = (n + 63) // 64 * 64
        assert self.p + n_al <= self.n, f"arena overflow {self.p}+{n_al}>{self.n}"
        a = self.t[:, self.p // 2:(self.p + n) // 2]
        self.p += n_al
        if dtype == F32:
            a = a.bitcast(F32)
        if len(free) == 2:
            a = a.rearrange("p (a b) -> p a b", b=free[1])
        elif len(free) == 3:
            a = a.rearrange("p (a b c) -> p a b c", b=free[1], c=free[2])
        return a


class WStream:
    def __init__(self, B, wbf, slots, entries):
        self.B, self.wbf, self.slots, self.entries = B, wbf, slots, entries
        self.bufs = [Buf() for _ in slots]
        self.issued = 0

    def _load(self, e):
        off, L = self.entries[e]
        k = e % len(self.slots)
        src = self.wbf[off:off + 128 * L].rearrange("(p l) -> p l", p=128)
        self.B.dma("sp", self.slots[k][:, 0:L], src, writes=[self.bufs[k]])

    def get(self, e):
        n = len(self.slots)
        while self.issued < min(len(self.entries), e + n):
            self._load(self.issued)
            self.issued += 1
        k = e % n
        return self.bufs[k], self.slots[k]


def build(cfg, phases=None):
    c = cfg
    D, FF, S, NSEQ, KC, FFC, T, TT, NT = c.D, c.FF, c.S, c.NSEQ, c.KC, c.FFC, c.T, c.TT, c.NT
    NB128 = S // 128
    ROWS = S // GRID_W
    nc = bass.Bass("TRN2", target_bir_lowering=False)
    dt = nc.dram_tensor
    x_in = dt("x", [NSEQ, S, D], F32, kind="ExternalInput").ap()
    wpack = dt("wpack", [c.NW], F32, kind="ExternalInput").ap()
    gains = dt("gains", [128, 3 * c.DEPTH * KC + KC], F32, kind="ExternalInput").ap()
    qkg = dt("qkg", [128, 2 * c.n_even], F32, kind="ExternalInput").ap()
    sink = dt("sink", [c.n_even * 8], F32, kind="ExternalInput").ap()
    t5 = dt("t5", [32 * 20], F32, kind="ExternalInput").ap()
    rpb = dt("rpb", [max(c.n_odd, 1), 8 * 15 * 31], F32, kind="ExternalInput").ap()
    cosT = dt("cosT", [128, S], F32, kind="ExternalInput").ap()
    sinT = dt("sinT", [128, S], F32, kind="ExternalInput").ap()
    piT = dt("piT", [128, 128], F32, kind="ExternalInput").ap()
    ohA = dt("ohA", [128, 32, 384], F32, kind="ExternalInput").ap()
    mA = dt("mA", [128, 384], F32, kind="ExternalInput").ap()
    ohD = dt("ohD", [64, 3, 32, 192], F32, kind="ExternalInput").ap()
    mD = dt("mD", [64, 192], F32, kind="ExternalInput").ap()
    ohC = dt("ohC", [64, 31, 64], F32, kind="ExternalInput").ap()
    mC = dt("mC", [64, 64], F32, kind="ExternalInput").ap()
    y_out = dt("y", [NSEQ, S, D], F32, kind="ExternalOutput").ap()

    wbfs = [dt(f"wbf{l}", [c.lbase[l + 1] - c.lbase[l]], BF16, kind="Internal").ap() for l in range(c.DEPTH)]
    xT = dt("xT", [NSEQ, D, S], F32, kind="Internal").ap()
    qk = dt("qk", [NSEQ, 26 * 128, S], BF16, kind="Internal").ap()
    vt = dt("vt", [NSEQ, S, 6 * 128], BF16, kind="Internal").ap()
    oT = dt("oT", [NSEQ, 2048, S], BF16, kind="Internal").ap()
    EA = dt("EA", [8, 128, 384], F32, kind="Internal").ap()
    ED = dt("ED", [12, 64, 192], F32, kind="Internal").ap()
    EC = dt("EC", [max(c.n_odd, 1), 8, 15, 64, 64], F32, kind="Internal").ap()

    ARENA_BYTES = 178 * 1024
    CONST_BYTES = 10 * 1024
    with contextlib.ExitStack() as es:
        arena_t = es.enter_context(nc.sbuf_tensor("arena", [128, ARENA_BYTES // 2], BF16))
        const_t = es.enter_context(nc.sbuf_tensor("consts", [128, CONST_BYTES // 2], BF16))
        banks = [es.enter_context(nc.psum_tensor(f"bank{i}", [128, 512], F32)) for i in range(8)]
        B = Bld(nc, es)
        A = Arena(arena_t, ARENA_BYTES)
        CA = Arena(const_t, CONST_BYTES)
        bankb = [Buf() for _ in range(8)]

        ident = CA.get(F32, 128)
        onesD = CA.get(F32, 128)
        ones128 = CA.get(F32, 128)
        ones_bf = CA.get(BF16, 128)
        pi_sb = CA.get(F32, 128)
        gains_sb = CA.get(F32, 3 * c.DEPTH * KC + KC)
        qkg_sb = CA.get(F32, 2 * c.n_even)
        esink = CA.get(F32, c.n_even * 8)
        cb = Buf(const=True)
        B.op("pool", lambda e: e.memset(ident, 0.0), writes=[cb])
        B.op("pool", lambda e: e.affine_select(out=ident, in_=ident, pattern=[[-1, 128]], compare_op=ALU.not_equal,
                                               fill=1.0, base=0, channel_multiplier=1), writes=[cb])
        B.op("pool", lambda e: e.memset(onesD, 1.0 / D), writes=[cb])
        B.op("pool", lambda e: e.memset(ones128, 1.0 / 128), writes=[cb])
        B.op("pool", lambda e: e.memset(ones_bf, 1.0), writes=[cb])
        B.dma("sp", pi_sb, piT, writes=[cb])
        B.dma("sp", gains_sb, gains, writes=[cb])
        B.dma("sp", qkg_sb, qkg, writes=[cb])
        B.dma("sp", esink, sink.partition_broadcast(128), writes=[cb])
        B.barrier()
        B.op("act", lambda e: e.activation(out=esink, in_=esink, func=AF.Exp), writes=[cb])
        B.barrier()

        def gcol(idx):
            return gains_sb[:, idx * KC:(idx + 1) * KC]

        def precast():
            wb = Buf()
            for l in range(c.DEPTH):
                n = c.lbase[l + 1] - c.lbase[l]
                ncol = n // 128
                wv = wpack[c.lbase[l]:c.lbase[l + 1]].rearrange("(p n) -> p n", p=128)
                bv = wbfs[l].rearrange("(p n) -> p n", p=128)
                step = 16384
                for a in range(0, ncol, step):
                    b = min(ncol, a + step)
                    B.dma("pool", bv[:, a:b], wv[:, a:b], writes=[wb])
            B.barrier()

        def etables():
            etab_a()
            etab_d()
            for li in range(c.n_odd):
                etab_c(li)

        def etab_a():
            A.reset()
            tb = A.get(F32, 640)
            oh = A.get(F32, 32, 384)
            acc = A.get(F32, 8, 384)
            msk = A.get(F32, 384)
            bT, bO, bA, bM = Buf(), Buf(), Buf(), Buf()
            B.dma("sp", tb, t5.partition_broadcast(128), writes=[bT])
            B.dma("sp", oh, ohA, writes=[bO])
            B.dma("sp", msk, mA, writes=[bM])
            for h in range(8):
                for b in range(32):
                    sc = tb[:, b * 20 + h:b * 20 + h + 1]
                    if b == 0:
                        B.op("dve", lambda e, h=h, b=b, sc=sc: e.tensor_scalar(
                            out=acc[:, h, :], in0=oh[:, b, :], scalar1=sc, scalar2=None, op0=ALU.mult),
                            reads=[bT, bO], writes=[bA])
                    else:
                        B.op("dve", lambda e, h=h, b=b, sc=sc: e.scalar_tensor_tensor(
                            out=acc[:, h, :], in0=oh[:, b, :], scalar=sc, in1=acc[:, h, :], op0=ALU.mult, op1=ALU.add),
                            reads=[bT, bO], writes=[bA])
            B.op("act", lambda e: e.activation(out=acc, in_=acc, func=AF.Exp), writes=[bA])
            for h in range(8):
                B.op("dve", lambda e, h=h: e.tensor_tensor(out=acc[:, h, :], in0=acc[:, h, :], in1=msk, op=ALU.mult),
                     reads=[bM], writes=[bA])
            B.dma("sp", EA.rearrange("h p f -> p h f"), acc, reads=[bA])
            B.barrier()

        def etab_d():
            A.reset()
            tb = A.get(F32, 640)
            oh = A.get(F32, 3, 32, 192)
            acc = A.get(F32, 12, 192)
            msk = A.get(F32, 192)
            bT, bO, bA, bM = Buf(), Buf(), Buf(), Buf()
            B.dma("sp", tb, t5.partition_broadcast(128), writes=[bT])
            B.dma("sp", oh[0:64], ohD, writes=[bO])
            B.dma("sp", msk[0:64], mD, writes=[bM])
            for g in range(3):
                for s_ in range(4):
                    hh = g * 4 + s_
                    col = 8 + hh
                    for b in range(32):
                        sc = tb[0:64, b * 20 + col:b * 20 + col + 1]
                        if b == 0:
                            B.op("dve", lambda e, hh=hh, g=g, b=b, sc=sc: e.tensor_scalar(
                                out=acc[0:64, hh, :], in0=oh[0:64, g, b, :], scalar1=sc, scalar2=None, op0=ALU.mult),
                                reads=[bT, bO], writes=[bA])
                        else:
                            B.op("dve", lambda e, hh=hh, g=g, b=b, sc=sc: e.scalar_tensor_tensor(
                                out=acc[0:64, hh, :], in0=oh[0:64, g, b, :], scalar=sc, in1=acc[0:64, hh, :],
                                op0=ALU.mult, op1=ALU.add), reads=[bT, bO], writes=[bA])
            B.op("act", lambda e: e.activation(out=acc[0:64], in_=acc[0:64], func=AF.Exp), writes=[bA])
            for hh in range(12):
                B.op("dve", lambda e, hh=hh: e.tensor_tensor(out=acc[0:64, hh, :], in0=acc[0:64, hh, :],
                                                             in1=msk[0:64], op=ALU.mult), reads=[bM], writes=[bA])
            B.dma("sp", ED.rearrange("h p f -> p h f"), acc[0:64], reads=[bA])
            B.barrier()

        def etab_c(li):
            if True:
                A.reset()
                rb = A.get(F32, 8 * 15 * 31)
                oh = A.get(F32, 31, 64)
                acc = A.get(F32, 120, 64)
                msk = A.get(F32, 64)
                bT, bO, bA, bM = Buf(), Buf(), Buf(), Buf()
                B.dma("sp", rb, rpb[li].partition_broadcast(128), writes=[bT])
                B.dma("sp", oh[0:64], ohC, writes=[bO])
                B.dma("sp", msk[0:64], mC, writes=[bM])
                for dc in range(31):
                    for hd in range(120):
                        sc = rb[0:64, hd * 31 + dc:hd * 31 + dc + 1]
                        if dc == 0:
                            B.op("dve", lambda e, hd=hd, dc=dc, sc=sc: e.tensor_scalar(
                                out=acc[0:64, hd, :], in0=oh[0:64, dc, :], scalar1=sc, scalar2=None, op0=ALU.mult),
                                reads=[bT, bO], writes=[bA])
                        else:
                            B.op("dve", lambda e, hd=hd, dc=dc, sc=sc: e.scalar_tensor_tensor(
                                out=acc[0:64, hd, :], in0=oh[0:64, dc, :], scalar=sc, in1=acc[0:64, hd, :],
                                op0=ALU.mult, op1=ALU.add), reads=[bT, bO], writes=[bA])
                B.op("act", lambda e: e.activation(out=acc[0:64], in_=acc[0:64], func=AF.Exp), writes=[bA])
                for hd in range(120):
                    B.op("dve", lambda e, hd=hd: e.tensor_tensor(out=acc[0:64, hd, :], in0=acc[0:64, hd, :],
                                                                 in1=msk[0:64], op=ALU.mult), reads=[bM], writes=[bA])
                B.dma("sp", EC[li].rearrange("h d k q -> k (h d) q"), acc[0:64], reads=[bA])
                B.barrier()

        def rmsnorm(x_ap, xb, g_ap, xn_ap, xnb, sq2, sqb, pss, pssb, rs, rsb, width=T):
            G = min(2, KC)
            ng = KC // G
            for g in range(ng):
                sq = sq2[g % 2]
                B.op("act", lambda e, g=g, sq=sq: e.activation(out=sq[:, 0:G, 0:width], in_=x_ap[:, g * G:(g + 1) * G, :],
                                                               func=AF.Square), reads=[xb], writes=[sqb[g % 2]])

                def mm(e, g=g, sq=sq):
                    for j in range(G):
                        ins = e.matmul(pss[:, 0:width], onesD, sq[:, j, 0:width], start=(g == 0 and j == 0),
                                       stop=(g == ng - 1 and j == G - 1))
                    return ins
                B.op("pe", mm, reads=[sqb[g % 2]], writes=([pssb] if g in (0, ng - 1) else []))
            B.op("act", lambda e: e.activation(out=rs[:, 0:width], in_=pss[:, 0:width], func=AF.Sqrt, bias=EPS, scale=1.0),
                 reads=[pssb], writes=[rsb])
            B.op("dve", lambda e: e.reciprocal(out=rs[:, 0:width], in_=rs[:, 0:width]), writes=[rsb])
            for kc in range(KC):
                B.op("dve", lambda e, kc=kc: e.scalar_tensor_tensor(
                    out=xn_ap[:, kc, :], in0=x_ap[:, kc, :], scalar=g_ap[:, kc:kc + 1], in1=rs[:, 0:width],
                    op0=ALU.mult, op1=ALU.mult), reads=[xb, rsb], writes=[xnb])

        def xT_tile(s, t):
            return xT[s].rearrange("(kc p) t -> p kc t", p=128)[:, :, t * T:(t + 1) * T]

        def in_transpose():
            A.reset()
            xin = [A.get(F32, TT, D) for _ in range(2)]
            xo = [A.get(F32, KC, T) for _ in range(2)]
            xinb = [Buf(), Buf()]
            xob = [Buf(), Buf()]
            G = min(4, KC)
            it = 0
            for s in range(NSEQ):
                for t in range(NT):
                    p = it % 2
                    B.dma("sp", xin[p], x_in[s, t * T:(t + 1) * T, :].rearrange("(tt p) d -> p tt d", p=128),
                          writes=[xinb[p]])
                    n = 0
                    for tt in range(TT):
                        for g in range(KC // G):
                            bk = n % 4
                            n += 1

                            def tr(e, tt=tt, g=g, bk=bk, p=p):
                                for j in range(G):
                                    kc = g * G + j
                                    ins = e.transpose(banks[bk][:, j * 128:(j + 1) * 128],
                                                      xin[p][:, tt, kc * 128:(kc + 1) * 128], ident)
                                return ins
                            B.op("pe", tr, reads=[xinb[p], cb], writes=[bankb[bk]])
                            eng = "dve" if n % 2 else "act"

                            def cp(e, tt=tt, g=g, bk=bk, p=p, eng=eng):
                                o = xo[p][:, g * G:(g + 1) * G, tt * 128:(tt + 1) * 128]
                                i_ = banks[bk][:, 0:G * 128].rearrange("p (a b) -> p a b", b=128)
                                return e.tensor_copy(out=o, in_=i_) if eng == "dve" else e.copy(out=o, in_=i_)
                            B.op(eng, cp, reads=[bankb[bk]], writes=[xob[p]])
                    B.dma("pool", xT_tile(s, t), xo[p], reads=[xob[p]])
                    it += 1
            B.barrier()

        def ffn(l, which):
            A.reset()
            xs = [A.get(F32, KC, T) for _ in range(2)]
            xn = A.get(BF16, KC, T)
            hT = A.get(BF16, FFC, T)
            sq2 = [A.get(F32, min(2, KC), T) for _ in range(2)]
            rs = A.get(F32, T)
            sg = [A.get(F32, T) for _ in range(2)]
            slots = [A.get(BF16, c.SLOT) for _ in range(3)]
            xb = [Buf(), Buf()]
            xnb, hb, rsb = Buf(), Buf(), Buf()
            sqb = [Buf(), Buf()]
            sgb = [Buf(), Buf()]
            o_in, n_in, L_in = c.woff[l]["f%din" % which]
            o_out, n_out, L_out = c.woff[l]["f%dout" % which]
            entries = []
            tiles = [(s, t) for s in range(NSEQ) for t in range(NT)]
            for _ in tiles:
                entries += [(o_in + j * 128 * L_in, L_in) for j in range(n_in)]
                entries += [(o_out + j * 128 * L_out, L_out) for j in range(n_out)]
            ws = WStream(B, wbfs[l], slots, entries)
            g_ap = gcol((0 if which == 1 else 2) * c.DEPTH + l)
            pss, pssb = banks[6], bankb[6]
            e_i = 0

            def do_norm(i):
                rmsnorm(xs[i % 2], xb[i % 2], g_ap, xn, xnb, sq2, sqb, pss, pssb, rs, rsb)

            B.dma("sp", xs[0], xT_tile(*tiles[0]), writes=[xb[0]])
            do_norm(0)
            for i, (s, t) in enumerate(tiles):
                p = i % 2
                if i + 1 < len(tiles):
                    B.dma("sp", xs[1 - p], xT_tile(*tiles[i + 1]), writes=[xb[1 - p]])
                for jj in range(FFC):
                    wbuf, wsl = ws.get(e_i)
                    e_i += 1
                    w3 = wsl[:, 0:L_in].rearrange("p (k n) -> p k n", n=256)
                    par = jj % 2
                    bg, bu = banks[par * 2], banks[par * 2 + 1]

                    def mm(e, w3=w3, bg=bg, bu=bu):
                        for kc in range(KC):
                            e.matmul(bg[:, 0:T], w3[:, kc, 0:128], xn[:, kc, :], start=(kc == 0), stop=(kc == KC - 1))
                        for kc in range(KC):
                            ins = e.matmul(bu[:, 0:T], w3[:, kc, 128:256], xn[:, kc, :], start=(kc == 0), stop=(kc == KC - 1))
                        return ins
                    B.op("pe", mm, reads=[wbuf, xnb], writes=[bankb[par * 2], bankb[par * 2 + 1]])
                    B.op("act", lambda e, bg=bg, par=par: e.activation(out=sg[par], in_=bg[:, 0:T], func=AF.Silu),
                         reads=[bankb[par * 2]], writes=[sgb[par]])
                    B.op("dve", lambda e, bu=bu, par=par, jj=jj: e.tensor_tensor(out=hT[:, jj, :], in0=sg[par], in1=bu[:, 0:T],
                                                                                 op=ALU.mult),
                         reads=[sgb[par], bankb[par * 2 + 1]], writes=[hb])
                if i + 1 < len(tiles):
                    do_norm(i + 1)
                for oc in range(KC):
                    wbuf, wsl = ws.get(e_i)
                    e_i += 1
                    w3 = wsl[:, 0:L_out].rearrange("p (k n) -> p k n", n=128)
                    by, byb = banks[4 + oc % 2], bankb[4 + oc % 2]

                    def mm2(e, w3=w3, by=by):
                        for kc in range(FFC):
                            ins = e.matmul(by[:, 0:T], w3[:, kc, :], hT[:, kc, :], start=(kc == 0), stop=(kc == FFC - 1))
                        return ins
                    B.op("pe", mm2, reads=[wbuf, hb], writes=[byb])
                    B.op("dve", lambda e, by=by, oc=oc, p=p: e.scalar_tensor_tensor(
                        out=xs[p][:, oc, :], in0=by[:, 0:T], scalar=0.5, in1=xs[p][:, oc, :], op0=ALU.mult, op1=ALU.add),
                        reads=[byb], writes=[xb[p]])
                B.dma("pool", xT_tile(s, t), xs[p], reads=[xb[p]])
            B.barrier()

        def in_proj(l):
            even = (l % 2 == 0)
            i2 = l // 2
            A.reset()
            nqk, nv = (20, 4) if even else (26, 6)
            if even:
                plan = [("f", j, j, None) for j in range(10)] + [("v", 10 + j, j, None) for j in range(2)] + \
                       [("r", 12 + j, 10 + j, 2 * i2) for j in range(8)] + [("r", 20 + j, 18 + j, 2 * i2 + 1) for j in range(2)] + \
                       [("v", 22 + j, 2 + j, None) for j in range(2)]
            else:
                plan = [("f", j, j, None) for j in range(10)] + [("v", 10 + j, j, None) for j in range(2)] + \
                       [("f", 12 + j, 10 + j, None) for j in range(16)] + [("v", 28 + j, 2 + j, None) for j in range(4)]
            xs1 = A.get(F32, KC, T)
            xs = [xs1, xs1]
            xn = A.get(BF16, KC, T)
            sq2 = [A.get(F32, min(2, KC), T) for _ in range(2)]
            rs = A.get(F32, T)
            qkt = [A.get(BF16, nqk, T) for _ in range(2)]
            vtt = [A.get(BF16, TT, nv * 128) for _ in range(2)]
            slots = [A.get(BF16, KC * 128) for _ in range(4)]
            if even:
                cs = [[A.get(F32, T) for _ in range(2)] for _ in range(2)]
                tmp = [[A.get(F32, T) for _ in range(5)] for _ in range(2)]
            xb1 = Buf()
            xb, qkb, vtb, csb = [xb1, xb1], [Buf(), Buf()], [Buf(), Buf()], [Buf(), Buf()]
            tmpb = [[Buf() for _ in range(5)] for _ in range(2)]
            xnb, rsb = Buf(), Buf()
            sqb = [Buf(), Buf()]
            o_w, n_w, L = c.woff[l]["min"]
            tiles = [(s, t) for s in range(NSEQ) for t in range(NT)]
            entries = []
            for _ in tiles:
                entries += [(o_w + pl[1] * 128 * L, L) for pl in plan]
            ws = WStream(B, wbfs[l], slots, entries)
            g_ap = gcol(c.DEPTH + l)
            pss, pssb = banks[7], bankb[7]
            e_i = 0
            B.dma("sp", xs[0], xT_tile(*tiles[0]), writes=[xb[0]])
            for i, (s, t) in enumerate(tiles):
                p = i % 2
                if even:
                    B.dma("sp", cs[p][0], cosT[:, t * T:(t + 1) * T], writes=[csb[p]])
                    B.dma("sp", cs[p][1], sinT[:, t * T:(t + 1) * T], writes=[csb[p]])
                rmsnorm(xs[p], xb[p], g_ap, xn, xnb, sq2, sqb, pss, pssb, rs, rsb)
                if i + 1 < len(tiles):
                    B.dma("sp", xs[1 - p], xT_tile(*tiles[i + 1]), writes=[xb[1 - p]])
                nf = 0
                nvv = 0
                nr = 0
                for kind, wc, oi, gi in plan:
                    wbuf, wsl = ws.get(e_i)
                    e_i += 1
                    w3 = wsl[:, 0:L].rearrange("p (k n) -> p k n", n=128)
                    if kind in ("f", "r"):
                        bk = nf % 2
                        nf += 1
                        bank = banks[bk]

                        def mm(e, w3=w3, bank=bank):
                            for kc in range(KC):
                                ins = e.matmul(bank[:, 0:T], w3[:, kc, :], xn[:, kc, :], start=(kc == 0), stop=(kc == KC - 1))
                            return ins
                        B.op("pe", mm, reads=[wbuf, xnb], writes=[bankb[bk]])
                        if kind == "f":
                            B.op("act", lambda e, bank=bank, oi=oi, p=p: e.copy(out=qkt[p][:, oi, :], in_=bank[:, 0:T]),
                                 reads=[bankb[bk]], writes=[qkb[p]])
                        else:
                            q2 = nr % 2
                            nr += 1
                            sqh, rr, qn, t1, t2 = tmp[q2]
                            bsq, brr, bqn, bt1, bt2 = tmpb[q2]
                            B.op("act", lambda e, bank=bank, sqh=sqh: e.activation(out=sqh, in_=bank[:, 0:T], func=AF.Square),
                                 reads=[bankb[bk]], writes=[bsq])
                            B.op("pe", lambda e, sqh=sqh: e.matmul(banks[4][:, 0:T], ones128, sqh, start=True, stop=True),
                                 reads=[bsq, cb], writes=[bankb[4]])
                            B.op("act", lambda e, rr=rr: e.activation(out=rr, in_=banks[4][:, 0:T], func=AF.Sqrt, bias=EPS, scale=1.0),
                                 reads=[bankb[4]], writes=[brr])
                            B.op("dve", lambda e, rr=rr: e.reciprocal(out=rr, in_=rr), writes=[brr])
                            B.op("dve", lambda e, bank=bank, qn=qn, rr=rr, gi=gi: e.scalar_tensor_tensor(
                                out=qn, in0=bank[:, 0:T], scalar=qkg_sb[:, gi:gi + 1], in1=rr, op0=ALU.mult, op1=ALU.mult),
                                reads=[bankb[bk], brr, cb], writes=[bqn])
                            B.op("pe", lambda e, qn=qn: e.matmul(banks[5][:, 0:T], pi_sb, qn, start=True, stop=True),
                                 reads=[bqn, cb], writes=[bankb[5]])
                            B.op("dve", lambda e, qn=qn, t1=t1, p=p: e.tensor_tensor(out=t1, in0=qn, in1=cs[p][0], op=ALU.mult),
                                 reads=[bqn, csb[p]], writes=[bt1])
                            B.op("dve", lambda e, t2=t2, p=p: e.tensor_tensor(out=t2, in0=banks[5][:, 0:T], in1=cs[p][1], op=ALU.mult),
                                 reads=[bankb[5], csb[p]], writes=[bt2])
                            B.op("pool", lambda e, t1=t1, t2=t2, oi=oi, p=p: e.tensor_tensor(out=qkt[p][:, oi, :], in0=t1, in1=t2, op=ALU.add),
                                 reads=[bt1, bt2], writes=[qkb[p]])
                    else:
                        bk = 2 + nvv % 2
                        nvv += 1
                        bank = banks[bk]

                        def mmv(e, w3=w3, bank=bank):
                            for tt in range(TT):
                                for kc in range(KC):
                                    ins = e.matmul(bank[:, tt * 128:(tt + 1) * 128], xn[:, kc, tt * 128:(tt + 1) * 128], w3[:, kc, :],
                                                   start=(kc == 0), stop=(kc == KC - 1))
                            return ins
                        B.op("pe", mmv, reads=[wbuf, xnb], writes=[bankb[bk]])
                        B.op("dve", lambda e, bank=bank, oi=oi, p=p: e.tensor_copy(
                            out=vtt[p][:, :, oi * 128:(oi + 1) * 128], in_=bank[:, 0:T].rearrange("p (a b) -> p a b", b=128)),
                            reads=[bankb[bk]], writes=[vtb[p]])
                B.dma("pool", qk[s, 0:nqk * 128, t * T:(t + 1) * T].rearrange("(c p) t -> p c t", p=128), qkt[p], reads=[qkb[p]])
                B.dma("pool", vt[s, t * T:(t + 1) * T, 0:nv * 128].rearrange("(tt p) f -> p tt f", p=128), vtt[p], reads=[vtb[p]])
            B.barrier()

        def load_v(dst, s, col, buf):
            step = max(1, NB128 // 4)
            for a in range(0, NB128, step):
                b = min(NB128, a + step)
                B.dma("sp", dst[:, a:b, :], vt[s, a * 128:b * 128, col * 128:(col + 1) * 128].rearrange("(c p) d -> p c d", p=128),
                      writes=[buf])

        def attn_a(l):
            i2 = l // 2
            A.reset()
            E = A.get(F32, 8, 384)
            kT = A.get(BF16, S)
            V = A.get(BF16, NB128, 128)
            qT = [A.get(BF16, S) for _ in range(2)]
            ot = [A.get(BF16, S) for _ in range(2)]
            pex = [A.get(F32, 384) for _ in range(2)]
            pbf = [A.get(BF16, 384) for _ in range(2)]
            rl = [A.get(F32, 128) for _ in range(2)]
            Eb, kb, vb = Buf(), Buf(), Buf()
            qb, ob, pexb, pbfb, rlb = [Buf(), Buf()], [Buf(), Buf()], [Buf(), Buf()], [Buf(), Buf()], [Buf(), Buf()]
            B.dma("sp", E, EA.rearrange("h p f -> p h f"), writes=[Eb])
            hh = 0
            n = 0
            for s in range(NSEQ):
                for kvh in range(2):
                    B.dma("sp", kT, qk[s, (8 + kvh) * 128:(9 + kvh) * 128, :], writes=[kb])
                    load_v(V, s, kvh, vb)
                    for g in range(4):
                        h = kvh * 4 + g
                        hp = hh % 2
                        hh += 1
                        B.dma("sp", qT[hp], qk[s, h * 128:(h + 1) * 128, :], writes=[qb[hp]])
                        for blk in range(NB128):
                            cs_ = [cc for cc in (blk - 1, blk, blk + 1) if 0 <= cc < NB128]
                            c0 = cs_[0] - (blk - 1)
                            lo, hi = c0 * 128, (c0 + len(cs_)) * 128
                            sb_i = n % 3
                            o_i = 3 + n % 2
                            pp = n % 2
                            n += 1
                            sbank, obank = banks[sb_i], banks[o_i]

                            def mm(e, cs_=cs_, c0=c0, sbank=sbank, hp=hp, blk=blk):
                                for ci, cch in enumerate(cs_):
                                    ins = e.matmul(sbank[:, (c0 + ci) * 128:(c0 + ci + 1) * 128], kT[:, cch * 128:(cch + 1) * 128],
                                                   qT[hp][:, blk * 128:(blk + 1) * 128], start=True, stop=True)
                                return ins
                            B.op("pe", mm, reads=[kb, qb[hp]], writes=[bankb[sb_i]])
                            B.op("act", lambda e, sbank=sbank, pp=pp, lo=lo, hi=hi: e.activation(
                                out=pex[pp][:, lo:hi], in_=sbank[:, lo:hi], func=AF.Exp, scale=SCALE),
                                reads=[bankb[sb_i]], writes=[pexb[pp]])
                            B.op("dve", lambda e, pp=pp, lo=lo, hi=hi, h=h: e.tensor_tensor(
                                out=pbf[pp][:, lo:hi], in0=pex[pp][:, lo:hi], in1=E[:, h, lo:hi], op=ALU.mult),
                                reads=[pexb[pp], Eb], writes=[pbfb[pp]])

                            def pv(e, cs_=cs_, c0=c0, obank=obank, pp=pp):
                                nl = len(cs_)
                                for ci, cch in enumerate(cs_):
                                    e.matmul(obank[:, 0:128], V[:, cch, :], pbf[pp][:, (c0 + ci) * 128:(c0 + ci + 1) * 128],
                                             start=(ci == 0), stop=(ci == nl - 1))
                                for ci, cch in enumerate(cs_):
                                    ins = e.matmul(obank[:, 128:256], ones_bf, pbf[pp][:, (c0 + ci) * 128:(c0 + ci + 1) * 128],
                                                   start=(ci == 0), stop=(ci == nl - 1))
                                return ins
                            B.op("pe", pv, reads=[vb, pbfb[pp], cb], writes=[bankb[o_i]])
                            sk = esink[:, i2 * 8 + h:i2 * 8 + h + 1]
                            B.op("dve", lambda e, obank=obank, pp=pp, sk=sk: e.tensor_scalar(
                                out=rl[pp], in0=obank[:, 128:256], scalar1=sk, scalar2=None, op0=ALU.add),
                                reads=[bankb[o_i], cb], writes=[rlb[pp]])
                            B.op("dve", lambda e, pp=pp: e.reciprocal(out=rl[pp], in_=rl[pp]), writes=[rlb[pp]])
                            B.op("dve", lambda e, obank=obank, pp=pp, hp=hp, blk=blk: e.tensor_tensor(
                                out=ot[hp][:, blk * 128:(blk + 1) * 128], in0=obank[:, 0:128], in1=rl[pp], op=ALU.mult),
                                reads=[bankb[o_i], rlb[pp]], writes=[ob[hp]])
                        B.dma("pool", oT[s, h * 128:(h + 1) * 128, :], ot[hp], reads=[ob[hp]])
            B.barrier()

        def attn_b(l):
            A.reset()
            kT = A.get(BF16, S)
            V = A.get(BF16, NB128, 128)
            qT = [A.get(BF16, S) for _ in range(2)]
            ot = [A.get(BF16, S) for _ in range(2)]
            pbf = [A.get(BF16, 512) for _ in range(3)]
            rl = [A.get(F32, 512) for _ in range(2)]
            kb, vb = Buf(), Buf()
            qb, ob, rlb = [Buf(), Buf()], [Buf(), Buf()], [Buf(), Buf()]
            pbfb = [Buf(), Buf(), Buf()]
            hh = 0
            nq = 0
            QT = 512
            for s in range(NSEQ):
                for kvh in range(2):
                    B.dma("sp", kT, qk[s, (18 + kvh) * 128:(19 + kvh) * 128, :], writes=[kb])
                    load_v(V, s, 2 + kvh, vb)
                    for g in range(4):
                        h = kvh * 4 + g
                        hp = hh % 2
                        hh += 1
                        B.dma("sp", qT[hp], qk[s, (10 + h) * 128:(11 + h) * 128, :], writes=[qb[hp]])
                        for qt in range(S // QT):
                            o_i, l_i = 4 + nq % 2, 6 + nq % 2
                            rp = nq % 2
                            nq += 1
                            obank, lbank = banks[o_i], banks[l_i]
                            qsl = qT[hp][:, qt * QT:(qt + 1) * QT]

                            def emit_s(kc, qsl=qsl):
                                sb_i = kc % 3
                                B.op("pe", lambda e, kc=kc, sb_i=sb_i, qsl=qsl: e.matmul(
                                    banks[sb_i][:, 0:QT], kT[:, kc * 128:(kc + 1) * 128], qsl, start=True, stop=True),
                                    reads=[kb, qb[hp]], writes=[bankb[sb_i]])
                            emit_s(0)
                            for kc in range(NB128):
                                if kc + 1 < NB128:
                                    emit_s(kc + 1)
                                sb_i = kc % 3
                                B.op("act", lambda e, sb_i=sb_i: e.activation(out=pbf[sb_i], in_=banks[sb_i][:, 0:QT], func=AF.Exp, scale=SCALE),
                                     reads=[bankb[sb_i]], writes=[pbfb[sb_i]])

                                def pv(e, kc=kc, sb_i=sb_i, obank=obank, lbank=lbank):
                                    e.matmul(obank[:, 0:QT], V[:, kc, :], pbf[sb_i], start=(kc == 0), stop=(kc == NB128 - 1))
                                    return e.matmul(lbank[:, 0:QT], ones_bf, pbf[sb_i], start=(kc == 0), stop=(kc == NB128 - 1))
                                edge = kc in (0, NB128 - 1)
                                B.op("pe", pv, reads=[vb, pbfb[sb_i], cb], writes=([bankb[o_i], bankb[l_i]] if edge else []))
                            B.op("dve", lambda e, lbank=lbank, rp=rp: e.reciprocal(out=rl[rp], in_=lbank[:, 0:QT]),
                                 reads=[bankb[l_i]], writes=[rlb[rp]])
                            B.op("dve", lambda e, obank=obank, rp=rp, hp=hp, qt=qt: e.tensor_tensor(
                                out=ot[hp][:, qt * QT:(qt + 1) * QT], in0=obank[:, 0:QT], in1=rl[rp], op=ALU.mult),
                                reads=[bankb[o_i], rlb[rp]], writes=[ob[hp]])
                        B.dma("pool", oT[s, (8 + h) * 128:(9 + h) * 128, :], ot[hp], reads=[ob[hp]])
            B.barrier()

        def attn_c(l):
            li = l // 2
            A.reset()
            PT = [A.get(F32, 14, 64) for _ in range(2)]
            kT = A.get(BF16, S)
            V0 = A.get(BF16, NB128, 128)
            V1 = A.get(BF16, NB128, 128)
            qT = [A.get(BF16, S) for _ in range(2)]
            ot = [A.get(BF16, S) for _ in range(2)]
            pex = [A.get(F32, 256) for _ in range(2)]
            pbf = [A.get(BF16, 256) for _ in range(2)]
            rl = [A.get(F32, 64) for _ in range(2)]
            kb, vb = Buf(), Buf()
            ptb, qb, ob, pexb, pbfb, rlb = ([Buf(), Buf()] for _ in range(6))
            hh = 0
            n = 0
            for s in range(NSEQ):
                for kvh in range(2):
                    B.dma("sp", kT, qk[s, (8 + kvh) * 128:(9 + kvh) * 128, :], writes=[kb])
                    load_v(V0, s, kvh, vb)
                    B.dma("sp", V1[:, 0:NB128 - 1, :],
                          vt[s, 64:64 + (NB128 - 1) * 128, kvh * 128:(kvh + 1) * 128].rearrange("(c p) d -> p c d", p=128), writes=[vb])
                    for g in range(4):
                        h = kvh * 4 + g
                        hp = hh % 2
                        hh += 1
                        B.dma("sp", qT[hp], qk[s, h * 128:(h + 1) * 128, :], writes=[qb[hp]])
                        B.dma("sp", PT[hp][0:64], EC[li, h, 0:14].rearrange("d k q -> k d q"), writes=[ptb[hp]])
                        B.dma("sp", PT[hp][64:128], EC[li, h, 1:15].rearrange("d k q -> k d q"), writes=[ptb[hp]])
                        for r in range(ROWS):
                            rs_ = min(max(r - 4, 0), ROWS - 8)
                            d0 = rs_ - r + 7
                            sb_i = n % 3
                            o_i = 3 + n % 2
                            pp = n % 2
                            n += 1
                            sbank, obank = banks[sb_i], banks[o_i]

                            def mm(e, rs_=rs_, r=r, sbank=sbank, hp=hp):
                                for m in range(4):
                                    k0 = rs_ * 64 + m * 128
                                    ins = e.matmul(sbank[:, m * 64:(m + 1) * 64], kT[:, k0:k0 + 128], qT[hp][:, r * 64:(r + 1) * 64],
                                                   start=True, stop=True)
                                return ins
                            B.op("pe", mm, reads=[kb, qb[hp]], writes=[bankb[sb_i]])
                            B.op("act", lambda e, sbank=sbank, pp=pp: e.activation(out=pex[pp], in_=sbank[:, 0:256], func=AF.Exp, scale=SCALE),
                                 reads=[bankb[sb_i]], writes=[pexb[pp]])
                            B.op("dve", lambda e, pp=pp, hp=hp, d0=d0: e.tensor_tensor(
                                out=pbf[pp].rearrange("p (a b) -> p a b", b=64), in0=pex[pp].rearrange("p (a b) -> p a b", b=64),
                                in1=PT[hp][:, d0:d0 + 7:2, :], op=ALU.mult), reads=[pexb[pp], ptb[hp]], writes=[pbfb[pp]])

                            def pv(e, rs_=rs_, obank=obank, pp=pp):
                                for m in range(4):
                                    vv = V0[:, rs_ // 2 + m, :] if rs_ % 2 == 0 else V1[:, (rs_ - 1) // 2 + m, :]
                                    e.matmul(obank[:, 0:64], vv, pbf[pp][:, m * 64:(m + 1) * 64], start=(m == 0), stop=(m == 3))
                                for m in range(4):
                                    ins = e.matmul(obank[:, 64:128], ones_bf, pbf[pp][:, m * 64:(m + 1) * 64], start=(m == 0), stop=(m == 3))
                                return ins
                            B.op("pe", pv, reads=[vb, pbfb[pp], cb], writes=[bankb[o_i]])
                            B.op("dve", lambda e, obank=obank, pp=pp: e.reciprocal(out=rl[pp], in_=obank[:, 64:128]),
                                 reads=[bankb[o_i]], writes=[rlb[pp]])
                            B.op("dve", lambda e, obank=obank, pp=pp, hp=hp, r=r: e.tensor_tensor(
                                out=ot[hp][:, r * 64:(r + 1) * 64], in0=obank[:, 0:64], in1=rl[pp], op=ALU.mult),
                                reads=[bankb[o_i], rlb[pp]], writes=[ob[hp]])
                        B.dma("pool", oT[s, h * 128:(h + 1) * 128, :], ot[hp], reads=[ob[hp]])
            B.barrier()

        def attn_d(l):
            A.reset()
            E = A.get(F32, 12, 192)
            kT = A.get(BF16, S)
            qT = [A.get(BF16, S) for _ in range(3)]
            Vg = [A.get(BF16, S // 64, 128) for _ in range(3)]
            ol = A.get(F32, 2, S)
            ot = A.get(BF16, S)
            pex = [A.get(F32, 192) for _ in range(2)]
            pbf = [A.get(BF16, 192) for _ in range(2)]
            Eb, kb, qb, vb, olb, ob = Buf(), Buf(), Buf(), Buf(), Buf(), Buf()
            pexb, pbfb = [Buf(), Buf()], [Buf(), Buf()]
            B.dma("sp", E[0:64], ED.rearrange("h p f -> p h f"), writes=[Eb])
            n = 0
            for s in range(NSEQ):
                for slot in range(4):
                    B.dma("sp", kT, qk[s, (22 + slot) * 128:(23 + slot) * 128, :], writes=[kb])
                    for g, dil in enumerate((1, 4, 16)):
                        B.dma("sp", qT[g], qk[s, (10 + g * 4 + slot) * 128:(11 + g * 4 + slot) * 128, :], writes=[qb])
                        nb = S // dil // 64
                        for rho in range(dil):
                            B.dma("sp", Vg[g][0:64, rho * nb:(rho + 1) * nb, :],
                                  vt[s, rho:S:dil, (2 + slot) * 128:(3 + slot) * 128].rearrange("(c k) d -> k c d", k=64), writes=[vb])
                    for g, dil in enumerate((1, 4, 16)):
                        nb = S // dil // 64
                        hh = g * 4 + slot
                        for rho in range(dil):
                            for i in range(nb):
                                cs_ = [cc for cc in (i - 1, i, i + 1) if 0 <= cc < nb]
                                c0 = cs_[0] - (i - 1)
                                lo, hi = c0 * 64, (c0 + len(cs_)) * 64
                                sb_i = n % 3
                                o_i = 3 + n % 2
                                pp = n % 2
                                n += 1
                                sbank, obank = banks[sb_i], banks[o_i]

                                def tsl(blk, dil=dil, rho=rho):
                                    a = blk * 64 * dil + rho
                                    return slice(a, a + 63 * dil + 1, dil) if dil > 1 else slice(a, a + 64)
                                qs = tsl(i)

                                def mm(e, cs_=cs_, c0=c0, sbank=sbank, g=g, qs=qs, tsl=tsl):
                                    for ci, cch in enumerate(cs_):
                                        ins = e.matmul(sbank[0:64, (c0 + ci) * 64:(c0 + ci + 1) * 64], kT[:, tsl(cch)], qT[g][:, qs],
                                                       start=True, stop=True)
                                    return ins
                                B.op("pe", mm, reads=[kb, qb], writes=[bankb[sb_i]])
                                B.op("act", lambda e, sbank=sbank, pp=pp, lo=lo, hi=hi: e.activation(
                                    out=pex[pp][0:64, lo:hi], in_=sbank[0:64, lo:hi], func=AF.Exp, scale=SCALE),
                                    reads=[bankb[sb_i]], writes=[pexb[pp]])
                                B.op("dve", lambda e, pp=pp, lo=lo, hi=hi, hh=hh: e.tensor_tensor(
                                    out=pbf[pp][0:64, lo:hi], in0=pex[pp][0:64, lo:hi], in1=E[0:64, hh, lo:hi], op=ALU.mult),
                                    reads=[pexb[pp], Eb], writes=[pbfb[pp]])

                                def pv(e, cs_=cs_, c0=c0, obank=obank, pp=pp, g=g, rho=rho, nb=nb):
                                    nl = len(cs_)
                                    for ci, cch in enumerate(cs_):
                                        e.matmul(obank[:, 0:64], Vg[g][0:64, rho * nb + cch, :], pbf[pp][0:64, (c0 + ci) * 64:(c0 + ci + 1) * 64],
                                                 start=(ci == 0), stop=(ci == nl - 1))
                                    for ci, cch in enumerate(cs_):
                                        ins = e.matmul(obank[:, 64:128], ones_bf[0:64, :], pbf[pp][0:64, (c0 + ci) * 64:(c0 + ci + 1) * 64],
                                                       start=(ci == 0), stop=(ci == nl - 1))
                                    return ins
                                B.op("pe", pv, reads=[vb, pbfb[pp], cb], writes=[bankb[o_i]])
                                src = obank[:, 0:128].rearrange("p (a b) -> p a b", b=64)
                                if g == 0:
                                    B.op("act", lambda e, src=src, qs=qs: e.copy(out=ol[:, :, qs], in_=src),
                                         reads=[bankb[o_i]], writes=[olb])
                                else:
                                    B.op("dve", lambda e, src=src, qs=qs: e.tensor_tensor(out=ol[:, :, qs], in0=src, in1=ol[:, :, qs], op=ALU.add),
                                         reads=[bankb[o_i]], writes=[olb])
                    B.op("dve", lambda e: e.reciprocal(out=ol[:, 1, :], in_=ol[:, 1, :]), writes=[olb])
                    B.op("dve", lambda e: e.tensor_tensor(out=ot, in0=ol[:, 0, :], in1=ol[:, 1, :], op=ALU.mult), reads=[olb], writes=[ob])
                    B.dma("pool", oT[s, (8 + slot) * 128:(9 + slot) * 128, :], ot, reads=[ob])
            B.barrier()

        def out_proj(l):
            OKC = 16 if l % 2 == 0 else 12
            A.reset()
            xs = [A.get(F32, KC, T) for _ in range(2)]
            ots = [A.get(BF16, OKC, T) for _ in range(2)]
            slots = [A.get(BF16, c.SLOT) for _ in range(3)]
            xb, otb = [Buf(), Buf()], [Buf(), Buf()]
            o_w, n_w, L = c.woff[l]["mout"]
            tiles = [(s, t) for s in range(NSEQ) for t in range(NT)]
            entries = []
            for _ in tiles:
                entries += [(o_w + j * 128 * L, L) for j in range(n_w)]
            ws = WStream(B, wbfs[l], slots, entries)
            e_i = 0

            def ld(i):
                s, t = tiles[i]
                B.dma("sp", xs[i % 2], xT_tile(s, t), writes=[xb[i % 2]])
                B.dma("sp", ots[i % 2], oT[s, 0:OKC * 128, t * T:(t + 1) * T].rearrange("(c p) t -> p c t", p=128), writes=[otb[i % 2]])
            ld(0)
            for i, (s, t) in enumerate(tiles):
                p = i % 2
                if i + 1 < len(tiles):
                    ld(i + 1)
                for oc in range(KC):
                    wbuf, wsl = ws.get(e_i)
                    e_i += 1
                    w3 = wsl[:, 0:L].rearrange("p (k n) -> p k n", n=128)
                    by, byb = banks[oc % 2], bankb[oc % 2]

                    def mm(e, w3=w3, by=by, p=p):
                        for kc in range(OKC):
                            ins = e.matmul(by[:, 0:T], w3[:, kc, :], ots[p][:, kc, :], start=(kc == 0), stop=(kc == OKC - 1))
                        return ins
                    B.op("pe", mm, reads=[wbuf, otb[p]], writes=[byb])
                    B.op("dve", lambda e, by=by, oc=oc, p=p: e.tensor_tensor(out=xs[p][:, oc, :], in0=by[:, 0:T], in1=xs[p][:, oc, :], op=ALU.add),
                         reads=[byb], writes=[xb[p]])
                B.dma("pool", xT_tile(s, t), xs[p], reads=[xb[p]])
            B.barrier()

        def final():
            A.reset()
            xs = [A.get(F32, KC, T) for _ in range(2)]
            xn = A.get(F32, KC, T)
            yo = [A.get(F32, TT, D) for _ in range(2)]
            sq2 = [A.get(F32, min(2, KC), T) for _ in range(2)]
            rs = A.get(F32, T)
            xb, yb = [Buf(), Buf()], [Buf(), Buf()]
            xnb, rsb = Buf(), Buf()
            sqb = [Buf(), Buf()]
            g_ap = gcol(3 * c.DEPTH)
            G = min(4, KC)
            tiles = [(s, t) for s in range(NSEQ) for t in range(NT)]
            B.dma("sp", xs[0], xT_tile(*tiles[0]), writes=[xb[0]])
            for i, (s, t) in enumerate(tiles):
                p = i % 2
                if i + 1 < len(tiles):
                    B.dma("sp", xs[1 - p], xT_tile(*tiles[i + 1]), writes=[xb[1 - p]])
                rmsnorm(xs[p], xb[p], g_ap, xn, xnb, sq2, sqb, banks[6], bankb[6], rs, rsb)
                n = 0
                for tt in range(TT):
                    for g in range(KC // G):
                        bk = n % 4
                        n += 1

                        def tr(e, tt=tt, g=g, bk=bk):
                            for j in range(G):
                                kc = g * G + j
                                ins = e.transpose(banks[bk][:, j * 128:(j + 1) * 128], xn[:, kc, tt * 128:(tt + 1) * 128], ident)
                            return ins
                        B.op("pe", tr, reads=[xnb, cb], writes=[bankb[bk]])
                        eng = "dve" if n % 2 else "act"

                        def cp(e, tt=tt, g=g, bk=bk, p=p, eng=eng):
                            o = yo[p][:, tt, g * G * 128:(g + 1) * G * 128]
                            i_ = banks[bk][:, 0:G * 128]
                            return e.tensor_copy(out=o, in_=i_) if eng == "dve" else e.copy(out=o, in_=i_)
                        B.op(eng, cp, reads=[bankb[bk]], writes=[yb[p]])
                B.dma("pool", y_out[s, t * T:(t + 1) * T, :].rearrange("(tt p) d -> p tt d", p=128), yo[p], reads=[yb[p]])
            B.barrier()

        ph = phases or ("precast", "etab", "in", "layers", "mix", "final")
        if "precast" in ph:
            precast()
        if "etab" in ph:
            etables()
        if "in" in ph:
            in_transpose()
        if "layers" in ph:
            for l in range(c.DEPTH):
                B.seg()
                if "noffn" not in ph:
                    ffn(l, 1)
                if "mix" in ph:
                    B.seg()
                    in_proj(l)
                    B.seg()
                    if l % 2 == 0:
                        attn_a(l)
                        B.seg()
                        attn_b(l)
                    else:
                        attn_c(l)
                        B.seg()
                        attn_d(l)
                    B.seg()
                    out_proj(l)
                B.seg()
                if "noffn" not in ph:
                    ffn(l, 2)
        if "final" in ph:
            final()
        B.barrier()
        block = es.enter_context(nc.Block())
        B.emit(block)
    nc._nops = B.nops
    return nc


def _t5_bucket_np(rel):
    half, max_exact = 16, 8
    n = np.abs(rel)
    nf = np.maximum(n, 1).astype(np.float32)
    large = max_exact + (np.log(nf / np.float32(max_exact)) / np.float32(math.log(2048 / max_exact))
                         * np.float32(half - max_exact)).astype(np.int32)
    large = np.minimum(large, half - 1)
    return np.where(rel > 0, half, 0) + np.where(n < max_exact, n, large)


def make_consts(cfg):
    S = cfg.S
    out = {}
    n_pairs, n_freq = 64, 32
    pos = np.arange(S)
    row = (pos // GRID_W).astype(np.float32)
    col = (pos % GRID_W).astype(np.float32)
    omega = (np.float32(10000.0) ** (-(np.arange(n_freq, dtype=np.float32) * np.float32(2.0) / np.float32(n_pairs)))).astype(np.float32)
    ang = np.concatenate([row[:, None] * omega, col[:, None] * omega], axis=-1).astype(np.float32)
    cos = np.cos(ang).astype(np.float32)
    sin = np.sin(ang).astype(np.float32)
    out["cosT"] = np.ascontiguousarray(np.repeat(cos, 2, axis=1).T)
    out["sinT"] = np.ascontiguousarray(np.repeat(sin, 2, axis=1).T)
    pi = np.zeros((128, 128), np.float32)
    for i in range(64):
        pi[2 * i + 1, 2 * i] = -1.0
        pi[2 * i, 2 * i + 1] = 1.0
    out["piT"] = pi
    kk = np.arange(128)[:, None, None]
    cc = np.arange(3)[None, :, None]
    qq = np.arange(128)[None, None, :]
    rel = (cc - 1) * 128 + kk - qq
    bk = _t5_bucket_np(rel)
    m = (np.abs(rel) <= 128)
    ohA = np.zeros((128, 32, 3, 128), np.float32)
    for b in range(32):
        ohA[:, b] = ((bk == b) & m)
    out["ohA"] = ohA.reshape(128, 32, 384)
    out["mA"] = m.astype(np.float32).reshape(128, 384)
    kk = np.arange(64)[:, None, None]
    qq = np.arange(64)[None, None, :]
    rs = (cc - 1) * 64 + kk - qq
    m = (np.abs(rs) <= 64)
    ohD = np.zeros((64, 3, 32, 3, 64), np.float32)
    for g, dil in enumerate((1, 4, 16)):
        bk = _t5_bucket_np(rs * dil)
        for b in range(32):
            ohD[:, g, b] = ((bk == b) & m)
    out["ohD"] = ohD.reshape(64, 3, 32, 192)
    out["mD"] = m.astype(np.float32).reshape(64, 192)
    kc_ = np.arange(64)[:, None]
    qc_ = np.arange(64)[None, :]
    dci = np.clip(kc_ - qc_ + 15, 0, 30)
    cs = np.clip(qc_ - 8, 0, 48)
    m = (kc_ >= cs) & (kc_ < cs + 16)
    ohC = np.zeros((64, 31, 64), np.float32)
    for d_ in range(31):
        ohC[:, d_, :] = ((dci == d_) & m)
    out["ohC"] = ohC
    out["mC"] = m.astype(np.float32)
    return out


def pack_weights(cfg, inp):
    c = cfg
    KC, FFC, FF = c.KC, c.FFC, c.FF
    wp = np.empty((c.NW,), np.float32)

    def put(off, arr):
        a = np.ascontiguousarray(arr, dtype=np.float32).reshape(-1)
        wp[off:off + a.size] = a

    def chunks(W, kc_n, oc_n):
        return W.reshape(kc_n, 128, oc_n, 128).transpose(2, 1, 0, 3)

    for l in range(c.DEPTH):
        i = l // 2
        d = {k: (v[0] + c.lbase[l], v[1], v[2]) for k, v in c.woff[l].items()}
        for which in (1, 2):
            Win = np.asarray(inp["ffn%d_w_in" % which][l])
            g = chunks(Win[:, :FF], KC, FFC)
            u = chunks(Win[:, FF:], KC, FFC)
            put(d["f%din" % which][0], np.concatenate([g, u], axis=3))
            Wout = np.asarray(inp["ffn%d_w_out" % which][l])
            put(d["f%dout" % which][0], chunks(Wout, FFC, KC))
        if l % 2 == 0:
            put(d["min"][0], chunks(np.asarray(inp["ab_w_in"][i]), KC, 24))
            put(d["mout"][0], chunks(np.asarray(inp["ab_w_out"][i]), 16, KC))
        else:
            put(d["min"][0], chunks(np.asarray(inp["cd_w_in"][i]), KC, 32))
            put(d["mout"][0], chunks(np.asarray(inp["cd_w_out"][i]), 12, KC))
    return wp


def host_inputs(cfg, inp):
    c = cfg
    KC = c.KC
    shared = make_consts(c)
    shared["wpack"] = pack_weights(c, inp)

    def fm(v):
        return np.asarray(v, np.float32).reshape(KC, 128).T

    cols = [fm(inp["norm_ffn1"][l]) for l in range(c.DEPTH)] + [fm(inp["norm_mix"][l]) for l in range(c.DEPTH)] + \
           [fm(inp["norm_ffn2"][l]) for l in range(c.DEPTH)] + [fm(inp["final_norm"])]
    shared["gains"] = np.ascontiguousarray(np.concatenate(cols, axis=1))
    qkg = np.zeros((128, 2 * c.n_even), np.float32)
    for i in range(c.n_even):
        qkg[:, 2 * i] = np.asarray(inp["ab_q_gain"][i])
        qkg[:, 2 * i + 1] = np.asarray(inp["ab_k_gain"][i])
    shared["qkg"] = qkg
    shared["sink"] = np.ascontiguousarray(np.asarray(inp["ab_sink"], np.float32).reshape(-1))
    shared["t5"] = np.ascontiguousarray(np.asarray(inp["t5_table"], np.float32).reshape(-1))
    r = np.asarray(inp["cd_rpb"], np.float32)
    shared["rpb"] = np.ascontiguousarray(r.reshape(r.shape[0], -1)) if r.shape[0] else np.zeros((1, 8 * 15 * 31), np.float32)
    return shared


_NC_CACHE = {}


def kernel(**inputs):
    cfg = Cfg()
    shared = host_inputs(cfg, inputs)
    xp = np.asarray(inputs["x_prompt"], np.float32)
    xsm = np.asarray(inputs["x_sample"], np.float32)
    zero = np.zeros_like(xp[0])
    in_maps = []
    for cid in range(N_CORES):
        second = xsm[cid] if cid < xsm.shape[0] else zero
        m = dict(shared)
        m["x"] = np.stack([xp[cid], second], axis=0)
        in_maps.append(m)
    if "nc" not in _NC_CACHE:
        _NC_CACHE["nc"] = build(cfg)
    res = run_bass_kernel_spmd(_NC_CACHE["nc"], in_maps, core_ids=list(range(N_CORES)))
    y_prompt = np.stack([res.results[cid]["y"][0] for cid in range(N_CORES)], axis=0)
    y_sample = np.stack([res.results[cid]["y"][1] for cid in range(xsm.shape[0])], axis=0)
    return (y_prompt.astype(np.float32), y_sample.astype(np.float32))
```

```python
import contextlib
import math
import numpy as np
import concourse.bass as bass
import concourse.mybir as mybir
from concourse.bass_utils import run_bass_kernel_spmd

F32 = mybir.dt.float32
BF16 = mybir.dt.bfloat16
ALU = mybir.AluOpType
AF = mybir.ActivationFunctionType
ENG = ("pe", "act", "dve", "pool", "sp")

HD = 128
GRID_W = 64
EPS = 1e-6
SCALE = HD ** -0.5
N_CORES = 8


class Cfg:
    def __init__(self, D=2048, FF=5632, S=4096, NSEQ=1, DEPTH=4, T=512, BAL=True, NCORES=8):
        self.D, self.FF, self.S, self.NSEQ, self.DEPTH, self.T = D, FF, S, NSEQ, DEPTH, T
        self.KC = D // 128
        self.BAL = BAL
        self.NCORES = NCORES
        self.SQ = S // 4 if BAL else 0
        self.FFC = FF // 128
        self.NT = S // T
        self.TT = T // 128
        self.n_even = (DEPTH + 1) // 2
        self.n_odd = DEPTH // 2
        off = 0
        self.woff = []
        self.lbase = []
        for l in range(DEPTH):
            d = {}
            self.lbase.append(off)
            inc = 24 if l % 2 == 0 else 32
            okc = 16 if l % 2 == 0 else 12
            for name, nblk, L in (("f1in", self.FFC, self.KC * 256), ("f1out", self.KC, self.FFC * 128),
                                  ("min", inc, self.KC * 128), ("mout", self.KC, okc * 128),
                                  ("f2in", self.FFC, self.KC * 256), ("f2out", self.KC, self.FFC * 128)):
                d[name] = (off - self.lbase[l], nblk, L)
                off += nblk * 128 * L
            self.woff.append(d)
        self.NW = off
        self.lbase.append(off)
        off = 0
        self.woff_s = []
        for l in range(DEPTH):
            nin, okc = (8, 4) if l % 2 == 0 else (9, 3)
            d = {"min": (off, nin, self.KC * 128)}
            off += nin * 128 * self.KC * 128
            d["mout"] = (off, self.KC, okc * 128)
            off += self.KC * 128 * okc * 128
            self.woff_s.append(d)
        self.NWS = off
        self.SLOT = max(self.KC * 256, self.FFC * 128, 16 * 128)


class Buf:
    __slots__ = ("w", "r", "const")

    def __init__(self, const=False):
        self.w = None
        self.r = {}
        self.const = const


class Bld:
    def __init__(self, nc, es, nsets=4, ndma=7):
        self.nc = nc
        self.q = {e: [] for e in ENG}
        self.nsets, self.ndma, self.cur = nsets, ndma, 0
        self.psem = [{e: es.enter_context(nc.semaphore(f"p{s}{e}")) for e in ENG} for s in range(nsets)]
        self.pcnt = [{e: 0 for e in ENG} for s in range(nsets)]
        self.dq = ("sp", "pool")
        self.dsem = [{qn: [es.enter_context(nc.semaphore(f"d{s}{qn}{i}")) for i in range(ndma)] for qn in self.dq}
                     for s in range(nsets)]
        self.dcnt = [{qn: [0] * ndma for qn in self.dq} for s in range(nsets)]
        self.drr = {qn: 0 for qn in self.dq}
        self.last = {e: None for e in ENG}
        self.dlast = {}
        self.nops = 0

    def seg(self):
        self.cur = (self.cur + 1) % self.nsets

    @staticmethod
    def _add(deps, t):
        if t is None:
            return
        k = t[2]
        if k not in deps or deps[k][1] < t[1]:
            deps[k] = t

    def _deps(self, reads, writes):
        deps = {}
        for b in reads:
            self._add(deps, b.w)
        for b in writes:
            self._add(deps, b.w)
            for t in b.r.values():
                self._add(deps, t)
        return deps

    def _commit(self, tok, reads, writes):
        for b in reads:
            if not b.const:
                b.r[tok[2]] = tok
        for b in writes:
            b.w = tok
            b.r = {}

    def op(self, eng, fn, reads=(), writes=()):
        deps = self._deps(reads, writes)
        s = self.cur
        key = ("p", s, eng)
        if eng == "pe":
            for k in [k for k in deps if k[0] == "p" and k[2] == "pe"]:
                del deps[k]
        self.pcnt[s][eng] += 1
        tok = (self.psem[s][eng], self.pcnt[s][eng], key)
        self.q[eng].append((fn, list(deps.values()), tok[0], 1))
        self._commit(tok, reads, writes)
        self.last[eng] = tok
        self.nops += 1
        return tok

    def dma(self, qn, out, in_, reads=(), writes=(), extra=()):
        deps = self._deps(reads, writes)
        for t in extra:
            self._add(deps, t)
        s = self.cur
        i = self.drr[qn]
        self.drr[qn] = (i + 1) % self.ndma
        key = ("d", s, qn, i)
        self._add(deps, self.dlast.get(key))
        self.dcnt[s][qn][i] += 16
        tok = (self.dsem[s][qn][i], self.dcnt[s][qn][i], key)
        self.dlast[key] = tok
        self.q[qn].append((lambda e, o=out, i_=in_: e.dma_start(out=o, in_=i_), list(deps.values()), tok[0], 16))
        self._commit(tok, reads, writes)
        self.nops += 1
        return tok

    def cc(self, es_sem, kind, op, rg, ins, outs):
        self.barrier()
        self.cccnt = getattr(self, "cccnt", 0) + 1
        tok = (es_sem, self.cccnt, ("c",))
        self.q["pool"].append((lambda e: e.collective_compute(kind, op, replica_groups=rg, ins=ins, outs=outs), [], es_sem, 1))
        self.dlast[("c",)] = tok
        self.barrier()

    def barrier(self):
        toks = [t for t in self.last.values() if t is not None] + list(self.dlast.values())
        for e in ENG:
            self.q[e].append((None, list(toks), None, 0))

    def emit(self, block):
        def mk(name):
            def f(e):
                seen = {}
                for fn, deps, sem, inc in self.q[name]:
                    for t in deps:
                        if seen.get(t[2], 0) < t[1]:
                            e.wait_ge(t[0], t[1])
                            seen[t[2]] = t[1]
                    if fn is not None:
                        fn(e).then_inc(sem, inc)
            return f
        block.tensor(mk("pe"))
        block.scalar(mk("act"))
        block.vector(mk("dve"))
        block.gpsimd(mk("pool"))
        block.sync(mk("sp"))


class Arena:
    def __init__(self, t, nbytes):
        self.t, self.n, self.p = t, nbytes, 0

    def reset(self, p=0):
        self.p = p

    def get(self, dtype, *free):
        esz = 4 if dtype == F32 else 2
        n = int(np.prod(free)) * esz
        n_al = (n + 63) // 64 * 64
        assert self.p + n_al <= self.n, f"arena overflow {self.p}+{n_al}>{self.n}"
        a = self.t[:, self.p // 2:(self.p + n) // 2]
        self.p += n_al
        if dtype == F32:
            a = a.bitcast(F32)
        if len(free) == 2:
            a = a.rearrange("p (a b) -> p a b", b=free[1])
        elif len(free) == 3:
            a = a.rearrange("p (a b c) -> p a b c", b=free[1], c=free[2])
        return a


class WStream:
    def __init__(self, B, wbf, slots, entries, tok=None):
        self.B, self.wbf, self.slots, self.entries = B, wbf, slots, entries
        self.extra = [tok] if tok is not None else []
        self.bufs = [Buf() for _ in slots]
        self.issued = 0

    def _load(self, e):
        off, L = self.entries[e]
        k = e % len(self.slots)
        src = self.wbf[off:off + 128 * L].rearrange("(p l) -> p l", p=128)
        self.B.dma("sp", self.slots[k][:, 0:L], src, writes=[self.bufs[k]], extra=self.extra)

    def get(self, e):
        n = len(self.slots)
        while self.issued < min(len(self.entries), e + n):
            self._load(self.issued)
            self.issued += 1
        k = e % n
        return self.bufs[k], self.slots[k]


def build(cfg, phases=None):
    c = cfg
    D, FF, S, NSEQ, KC, FFC, T, TT, NT = c.D, c.FF, c.S, c.NSEQ, c.KC, c.FFC, c.T, c.TT, c.NT
    NB128 = S // 128
    ROWS = S // GRID_W
    nc = bass.Bass("TRN2", target_bir_lowering=False)
    dt = nc.dram_tensor
    x_in = dt("x", [NSEQ, S, D], F32, kind="ExternalInput").ap()
    wpack = dt("wpack", [c.NW], F32, kind="ExternalInput").ap()
    gains = dt("gains", [128, 3 * c.DEPTH * KC + KC], F32, kind="ExternalInput").ap()
    qkg = dt("qkg", [128, 2 * c.n_even], F32, kind="ExternalInput").ap()
    sink = dt("sink", [c.n_even * 8], F32, kind="ExternalInput").ap()
    t5 = dt("t5", [32 * 20], F32, kind="ExternalInput").ap()
    rpb = dt("rpb", [max(c.n_odd, 1), 31, 120], F32, kind="ExternalInput").ap()
    cosT = dt("cosT", [128, S], F32, kind="ExternalInput").ap()
    sinT = dt("sinT", [128, S], F32, kind="ExternalInput").ap()
    piT = dt("piT", [128, 128], F32, kind="ExternalInput").ap()
    ohA = dt("ohA", [128, 32, 384], F32, kind="ExternalInput").ap()
    mA = dt("mA", [128, 384], F32, kind="ExternalInput").ap()
    ohD = dt("ohD", [64, 3, 32, 192], F32, kind="ExternalInput").ap()
    mD = dt("mD", [64, 192], F32, kind="ExternalInput").ap()
    ohC = dt("ohC", [31, 64, 64], F32, kind="ExternalInput").ap()
    mC = dt("mC", [64, 64], F32, kind="ExternalInput").ap()
    y_out = dt("y", [NSEQ, S, D], F32, kind="ExternalOutput").ap()
    SQ = c.SQ
    SQn = max(SQ, T)
    xs_in = dt("xs", [SQn, D], F32, kind="ExternalInput").ap()
    ys_out = dt("ys", [SQn, D], F32, kind="ExternalOutput").ap()
    wpack_s = dt("wpack_s", [c.NWS], F32, kind="ExternalInput").ap()
    t5m = dt("t5m", [32 * 5], F32, kind="ExternalInput").ap()
    sinkm = dt("sinkm", [c.n_even * 2], F32, kind="ExternalInput").ap()
    rpbm = dt("rpbm", [max(c.n_odd, 1), 31, 30], F32, kind="ExternalInput").ap()
    wbf_s = dt("wbf_s", [c.NWS], BF16, kind="Internal").ap()
    xTs = dt("xTs", [D, SQn], F32, kind="Internal").ap()
    GP = min(D, 256)
    NGP = D // GP
    xg = dt("xg", [NGP, 4 * GP, SQn], F32, kind="Internal").ap()
    part = dt("part", [4 * D, SQn], F32, kind="Internal").ap()
    red = dt("red", [D, SQn], F32, kind="Internal").ap()
    qks = dt("qks", [7 * 128, S], BF16, kind="Internal").ap()
    vts = dt("vts", [S, 2 * 128], BF16, kind="Internal").ap()
    oTs = dt("oTs", [4 * 128, S], BF16, kind="Internal").ap()
    EAs = dt("EAs", [2, 128, 384], F32, kind="Internal").ap()
    EDs = dt("EDs", [3, 64, 192], F32, kind="Internal").ap()
    ECs = dt("ECs", [max(c.n_odd, 1), 2, 15, 64, 64], F32, kind="Internal").ap()
    TILES = [(0, t) for t in range(NT)] + [(1, t) for t in range(SQ // T)]
    RG = [list(range(g * 4, g * 4 + 4)) for g in range(c.NCORES // 4)]

    wbfs = [dt(f"wbf{l}", [c.lbase[l + 1] - c.lbase[l]], BF16, kind="Internal").ap() for l in range(c.DEPTH)]
    xT = dt("xT", [NSEQ, D, S], F32, kind="Internal").ap()
    qk = dt("qk", [NSEQ, 26 * 128, S], BF16, kind="Internal").ap()
    vt = dt("vt", [NSEQ, S, 6 * 128], BF16, kind="Internal").ap()
    oT = dt("oT", [NSEQ, 2048, S], BF16, kind="Internal").ap()
    EA = dt("EA", [8, 128, 384], F32, kind="Internal").ap()
    ED = dt("ED", [12, 64, 192], F32, kind="Internal").ap()
    EC = dt("EC", [max(c.n_odd, 1), 8, 15, 64, 64], F32, kind="Internal").ap()

    ARENA_BYTES = 178 * 1024
    CONST_BYTES = 10 * 1024
    with contextlib.ExitStack() as es:
        arena_t = es.enter_context(nc.sbuf_tensor("arena", [128, ARENA_BYTES // 2], BF16))
        const_t = es.enter_context(nc.sbuf_tensor("consts", [128, CONST_BYTES // 2], BF16))
        banks = [es.enter_context(nc.psum_tensor(f"bank{i}", [128, 512], F32)) for i in range(8)]
        B = Bld(nc, es)
        ccsem = es.enter_context(nc.semaphore("ccsem"))
        A = Arena(arena_t, ARENA_BYTES)
        CA = Arena(const_t, CONST_BYTES)
        bankb = [Buf() for _ in range(8)]

        ident = CA.get(F32, 128)
        onesD = CA.get(BF16, 128)
        ones128 = CA.get(BF16, 128)
        ones_bf = CA.get(BF16, 128)
        pi_sb = CA.get(F32, 128)
        gains_sb = CA.get(F32, 3 * c.DEPTH * KC + KC)
        qkg_sb = CA.get(F32, 2 * c.n_even)
        esink = CA.get(F32, c.n_even * 10)
        cb = Buf(const=True)
        B.op("pool", lambda e: e.memset(ident, 0.0), writes=[cb])
        B.op("pool", lambda e: e.affine_select(out=ident, in_=ident, pattern=[[-1, 128]], compare_op=ALU.not_equal,
                                               fill=1.0, base=0, channel_multiplier=1), writes=[cb])
        B.op("pool", lambda e: e.memset(onesD, 1.0 / D), writes=[cb])
        B.op("pool", lambda e: e.memset(ones128, 1.0 / 128), writes=[cb])
        B.op("pool", lambda e: e.memset(ones_bf, 1.0), writes=[cb])
        B.dma("sp", pi_sb, piT, writes=[cb])
        B.dma("sp", gains_sb, gains, writes=[cb])
        B.dma("sp", qkg_sb, qkg, writes=[cb])
        B.dma("sp", esink[:, 0:c.n_even * 8], sink.partition_broadcast(128), writes=[cb])
        B.dma("sp", esink[:, c.n_even * 8:c.n_even * 10], sinkm.partition_broadcast(128), writes=[cb])
        B.barrier()
        B.op("act", lambda e: e.activation(out=esink, in_=esink, func=AF.Exp), writes=[cb])
        B.barrier()

        def gcol(idx):
            return gains_sb[:, idx * KC:(idx + 1) * KC]

        pcsem = [es.enter_context(nc.semaphore(f"pc{l}")) for l in range(c.DEPTH + 1)]
        wtok = {}

        def precast():
            jobs = []
            if c.BAL:
                jobs.append((c.DEPTH, wpack_s, wbf_s, c.NWS))
            for l in range(c.DEPTH):
                jobs.append((l, wpack[c.lbase[l]:c.lbase[l + 1]], wbfs[l], c.lbase[l + 1] - c.lbase[l]))
            jobs.sort(key=lambda j: (j[0] != 0 and j[0] != c.DEPTH, j[0]))
            for l, src, dst, n in jobs:
                ncol = n // 128
                wv = src.rearrange("(p n) -> p n", p=128)
                bv = dst.rearrange("(p n) -> p n", p=128)
                cnt = 0
                for a_ in range(0, ncol, 16384):
                    b_ = min(ncol, a_ + 16384)
                    B.q["pool"].append((lambda e, o=bv[:, a_:b_], i_=wv[:, a_:b_]: e.dma_start(out=o, in_=i_), [], pcsem[l], 16))
                    cnt += 16
                wtok[l] = (pcsem[l], cnt, ("pc", l))

        def etables():
            etab_a(t5, 20, list(range(8)), EA)
            etab_d(t5, 20, [(g, 8 + g * 4 + s_) for g in range(3) for s_ in range(4)], ED)
            for li in range(c.n_odd):
                etab_c(li)
            if c.BAL:
                etab_a(t5m, 5, [0, 1], EAs)
                etab_d(t5m, 5, [(g, 2 + g) for g in range(3)], EDs)

        def accum(acc_h, oh_b, sc, first, bufs_r, buf_w):
            if first:
                B.op("dve", lambda e: e.tensor_scalar(out=acc_h, in0=oh_b, scalar1=sc, scalar2=None, op0=ALU.mult),
                     reads=bufs_r, writes=[buf_w])
            else:
                B.op("dve", lambda e: e.scalar_tensor_tensor(out=acc_h, in0=oh_b, scalar=sc, in1=acc_h, op0=ALU.mult, op1=ALU.add),
                     reads=bufs_r, writes=[buf_w])

        def etab_a(tsrc, nct, cols, out_d):
            A.reset()
            nh = len(cols)
            tb = A.get(F32, 32 * nct)
            oh = A.get(F32, 32, 384)
            acc = A.get(F32, nh, 384)
            msk = A.get(F32, 384)
            bT, bO, bA, bM = Buf(), Buf(), Buf(), Buf()
            B.dma("sp", tb, tsrc.partition_broadcast(128), writes=[bT])
            B.dma("sp", oh, ohA, writes=[bO])
            B.dma("sp", msk, mA, writes=[bM])
            for h, col in enumerate(cols):
                for b_ in range(32):
                    accum(acc[:, h, :], oh[:, b_, :], tb[:, b_ * nct + col:b_ * nct + col + 1], b_ == 0, [bT, bO], bA)
            B.op("act", lambda e: e.activation(out=acc, in_=acc, func=AF.Exp), writes=[bA])
            for h in range(nh):
                B.op("dve", lambda e, h=h: e.tensor_tensor(out=acc[:, h, :], in0=acc[:, h, :], in1=msk, op=ALU.mult),
                     reads=[bM], writes=[bA])
            B.dma("sp", out_d.rearrange("h p f -> p h f"), acc, reads=[bA])
            B.barrier()

        def etab_d(tsrc, nct, gcols, out_d):
            A.reset()
            nh = len(gcols)
            tb = A.get(F32, 32 * nct)
            oh = A.get(F32, 3, 32, 192)
            acc = A.get(F32, nh, 192)
            msk = A.get(F32, 192)
            bT, bO, bA, bM = Buf(), Buf(), Buf(), Buf()
            B.dma("sp", tb, tsrc.partition_broadcast(128), writes=[bT])
            B.dma("sp", oh[0:64], ohD, writes=[bO])
            B.dma("sp", msk[0:64], mD, writes=[bM])
            for hh, (g, col) in enumerate(gcols):
                for b_ in range(32):
                    accum(acc[0:64, hh, :], oh[0:64, g, b_, :], tb[0:64, b_ * nct + col:b_ * nct + col + 1], b_ == 0, [bT, bO], bA)
            B.op("act", lambda e: e.activation(out=acc[0:64], in_=acc[0:64], func=AF.Exp), writes=[bA])
            for hh in range(nh):
                B.op("dve", lambda e, hh=hh: e.tensor_tensor(out=acc[0:64, hh, :], in0=acc[0:64, hh, :], in1=msk[0:64], op=ALU.mult),
                     reads=[bM], writes=[bA])
            B.dma("sp", out_d.rearrange("h p f -> p h f"), acc[0:64], reads=[bA])
            B.barrier()

        def etab_c(li):
            A.reset()
            NH = 150 if c.BAL else 120
            R = A.get(F32, NH)
            M = A.get(F32, 64, 64)
            acc = A.get(F32, NH, 64)
            msk = A.get(F32, 64)
            bR, bM, bA, bK = Buf(), Buf(), Buf(), Buf()
            B.dma("sp", R[0:31, 0:120], rpb[li], writes=[bR])
            if c.BAL:
                B.dma("sp", R[0:31, 120:150], rpbm[li], writes=[bR])
            B.dma("sp", M[0:31], ohC, writes=[bM])
            B.dma("sp", msk[0:64], mC, writes=[bK])
            for qc in range(64):
                bk = qc % 4
                B.op("pe", lambda e, qc=qc, bk=bk: e.matmul(banks[bk][0:64, 0:NH], M[0:31, qc, :], R[0:31, 0:NH], start=True, stop=True),
                     reads=[bR, bM], writes=[bankb[bk]])
                eng = "dve" if qc % 2 else "act"
                B.op(eng, lambda e, qc=qc, bk=bk, eng=eng: (e.tensor_copy(out=acc[0:64, :, qc], in_=banks[bk][0:64, 0:NH]) if eng == "dve"
                                                         else e.copy(out=acc[0:64, :, qc], in_=banks[bk][0:64, 0:NH])),
                     reads=[bankb[bk]], writes=[bA])
            B.op("act", lambda e: e.activation(out=acc[0:64], in_=acc[0:64], func=AF.Exp), writes=[bA])
            m64 = msk[0:64]
            mb = bass.AP(m64.tensor, m64.offset, [list(m64.ap[0]), [0, NH], [1, 64]])
            B.op("dve", lambda e: e.tensor_tensor(out=acc[0:64], in0=acc[0:64], in1=mb, op=ALU.mult), reads=[bK], writes=[bA])
            B.dma("sp", EC[li].rearrange("h d k q -> k (h d) q"), acc[0:64, 0:120, :], reads=[bA])
            if c.BAL:
                B.dma("sp", ECs[li].rearrange("h d k q -> k (h d) q"), acc[0:64, 120:150, :], reads=[bA])
            B.barrier()

        def rmsnorm(x_ap, xb, g_ap, xn_ap, xnb, sq2, sqb, pss, pssb, rs, rsb, width=T):
            G = min(2, KC)
            ng = KC // G
            for g in range(ng):
                sq = sq2[g % 2]
                B.op("act", lambda e, g=g, sq=sq: e.activation(out=sq[:, 0:G, 0:width], in_=x_ap[:, g * G:(g + 1) * G, :],
                                                               func=AF.Square), reads=[xb], writes=[sqb[g % 2]])

                def mm(e, g=g, sq=sq):
                    for j in range(G):
                        ins = e.matmul(pss[:, 0:width], onesD, sq[:, j, 0:width], start=(g == 0 and j == 0),
                                       stop=(g == ng - 1 and j == G - 1))
                    return ins
                B.op("pe", mm, reads=[sqb[g % 2]], writes=([pssb] if g in (0, ng - 1) else []))
            B.op("act", lambda e: e.activation(out=rs[:, 0:width], in_=pss[:, 0:width], func=AF.Sqrt, bias=EPS, scale=1.0),
                 reads=[pssb], writes=[rsb])
            B.op("dve", lambda e: e.reciprocal(out=rs[:, 0:width], in_=rs[:, 0:width]), writes=[rsb])
            for kc in range(KC):
                B.op("dve", lambda e, kc=kc: e.scalar_tensor_tensor(
                    out=xn_ap[:, kc, :], in0=x_ap[:, kc, :], scalar=g_ap[:, kc:kc + 1], in1=rs[:, 0:width],
                    op0=ALU.mult, op1=ALU.mult), reads=[xb, rsb], writes=[xnb])

        def xT_tile(s, t):
            src = xT[0] if s == 0 else xTs
            return src.rearrange("(kc p) t -> p kc t", p=128)[:, :, t * T:(t + 1) * T]

        def in_transpose():
            A.reset()
            xin = [A.get(F32, TT, D) for _ in range(2)]
            xo = [A.get(F32, KC, T) for _ in range(2)]
            xinb = [Buf(), Buf()]
            xob = [Buf(), Buf()]
            G = min(4, KC)
            it = 0
            for s, t in TILES:
                if True:
                    p = it % 2
                    srcx = x_in[0] if s == 0 else xs_in
                    B.dma("sp", xin[p], srcx[t * T:(t + 1) * T, :].rearrange("(tt p) d -> p tt d", p=128),
                          writes=[xinb[p]])
                    n = 0
                    for tt in range(TT):
                        for g in range(KC // G):
                            bk = n % 4
                            n += 1

                            def tr(e, tt=tt, g=g, bk=bk, p=p):
                                for j in range(G):
                                    kc = g * G + j
                                    ins = e.transpose(banks[bk][:, j * 128:(j + 1) * 128],
                                                      xin[p][:, tt, kc * 128:(kc + 1) * 128], ident)
                                return ins
                            B.op("pe", tr, reads=[xinb[p], cb], writes=[bankb[bk]])
                            eng = "dve" if n % 2 else "act"

                            def cp(e, tt=tt, g=g, bk=bk, p=p, eng=eng):
                                o = xo[p][:, g * G:(g + 1) * G, tt * 128:(tt + 1) * 128]
                                i_ = banks[bk][:, 0:G * 128].rearrange("p (a b) -> p a b", b=128)
                                return e.tensor_copy(out=o, in_=i_) if eng == "dve" else e.copy(out=o, in_=i_)
                            B.op(eng, cp, reads=[bankb[bk]], writes=[xob[p]])
                    B.dma("pool", xT_tile(s, t), xo[p], reads=[xob[p]])
                    it += 1
            B.barrier()

        def ffn(l, which):
            A.reset()
            xs = [A.get(F32, KC, T) for _ in range(2)]
            xn = A.get(BF16, KC, T)
            hT = A.get(BF16, FFC, T)
            sq2 = [A.get(BF16, min(2, KC), T) for _ in range(2)]
            rs = A.get(F32, T)
            sg = [A.get(F32, T) for _ in range(2)]
            slots = [A.get(BF16, c.SLOT) for _ in range(3)]
            xb = [Buf(), Buf()]
            xnb, hb, rsb = Buf(), Buf(), Buf()
            sqb = [Buf(), Buf()]
            sgb = [Buf(), Buf()]
            o_in, n_in, L_in = c.woff[l]["f%din" % which]
            o_out, n_out, L_out = c.woff[l]["f%dout" % which]
            entries = []
            tiles = list(TILES)
            for _ in tiles:
                entries += [(o_in + j * 128 * L_in, L_in) for j in range(n_in)]
                entries += [(o_out + j * 128 * L_out, L_out) for j in range(n_out)]
            ws = WStream(B, wbfs[l], slots, entries, wtok.get(l))
            g_ap = gcol((0 if which == 1 else 2) * c.DEPTH + l)
            pss, pssb = banks[6], bankb[6]
            e_i = 0

            def do_norm(i):
                rmsnorm(xs[i % 2], xb[i % 2], g_ap, xn, xnb, sq2, sqb, pss, pssb, rs, rsb)

            B.dma("sp", xs[0], xT_tile(*tiles[0]), writes=[xb[0]])
            do_norm(0)
            for i, (s, t) in enumerate(tiles):
                p = i % 2
                if i + 1 < len(tiles):
                    B.dma("sp", xs[1 - p], xT_tile(*tiles[i + 1]), writes=[xb[1 - p]])
                for jj in range(FFC):
                    wbuf, wsl = ws.get(e_i)
                    e_i += 1
                    w3 = wsl[:, 0:L_in].rearrange("p (k n) -> p k n", n=256)
                    par = jj % 2
                    bg, bu = banks[par * 2], banks[par * 2 + 1]

                    def mm(e, w3=w3, bg=bg, bu=bu):
                        for kc in range(KC):
                            e.matmul(bg[:, 0:T], w3[:, kc, 0:128], xn[:, kc, :], start=(kc == 0), stop=(kc == KC - 1))
                        for kc in range(KC):
                            ins = e.matmul(bu[:, 0:T], w3[:, kc, 128:256], xn[:, kc, :], start=(kc == 0), stop=(kc == KC - 1))
                        return ins
                    B.op("pe", mm, reads=[wbuf, xnb], writes=[bankb[par * 2], bankb[par * 2 + 1]])
                    B.op("act", lambda e, bg=bg, par=par: e.activation(out=sg[par], in_=bg[:, 0:T], func=AF.Silu),
                         reads=[bankb[par * 2]], writes=[sgb[par]])
                    B.op("dve", lambda e, bu=bu, par=par, jj=jj: e.tensor_tensor(out=hT[:, jj, :], in0=sg[par], in1=bu[:, 0:T],
                                                                                 op=ALU.mult),
                         reads=[sgb[par], bankb[par * 2 + 1]], writes=[hb])
                if i + 1 < len(tiles):
                    do_norm(i + 1)
                for oc in range(KC):
                    wbuf, wsl = ws.get(e_i)
                    e_i += 1
                    w3 = wsl[:, 0:L_out].rearrange("p (k n) -> p k n", n=128)
                    by, byb = banks[4 + oc % 2], bankb[4 + oc % 2]

                    def mm2(e, w3=w3, by=by):
                        for kc in range(FFC):
                            ins = e.matmul(by[:, 0:T], w3[:, kc, :], hT[:, kc, :], start=(kc == 0), stop=(kc == FFC - 1))
                        return ins
                    B.op("pe", mm2, reads=[wbuf, hb], writes=[byb])
                    B.op("dve", lambda e, by=by, oc=oc, p=p: e.scalar_tensor_tensor(
                        out=xs[p][:, oc, :], in0=by[:, 0:T], scalar=0.5, in1=xs[p][:, oc, :], op0=ALU.mult, op1=ALU.add),
                        reads=[byb], writes=[xb[p]])
                B.dma("pool", xT_tile(s, t), xs[p], reads=[xb[p]])
            B.barrier()

        def in_proj(l, samp=False):
            even = (l % 2 == 0)
            i2 = l // 2
            A.reset()
            nqk, nv = (20, 4) if even else (26, 6)
            if samp:
                nqk, nv = (6, 2) if even else (7, 2)
            if even:
                plan = [("f", j, j, None) for j in range(10)] + [("v", 10 + j, j, None) for j in range(2)] + \
                       [("r", 12 + j, 10 + j, 2 * i2) for j in range(8)] + [("r", 20 + j, 18 + j, 2 * i2 + 1) for j in range(2)] + \
                       [("v", 22 + j, 2 + j, None) for j in range(2)]
            else:
                plan = [("f", j, j, None) for j in range(10)] + [("v", 10 + j, j, None) for j in range(2)] + \
                       [("f", 12 + j, 10 + j, None) for j in range(16)] + [("v", 28 + j, 2 + j, None) for j in range(4)]
            if samp:
                if even:
                    plan = [("f", 0, 0, None), ("f", 1, 1, None), ("f", 2, 2, None), ("r", 3, 3, 2 * i2), ("r", 4, 4, 2 * i2),
                            ("r", 5, 5, 2 * i2 + 1), ("v", 6, 0, None), ("v", 7, 1, None)]
                else:
                    plan = [("f", j, j, None) for j in range(7)] + [("v", 7, 0, None), ("v", 8, 1, None)]
            xs1 = A.get(F32, KC, T)
            xs = [xs1, xs1]
            xn = A.get(BF16, KC, T)
            sq2 = [A.get(BF16, min(2, KC), T) for _ in range(2)]
            rs = A.get(F32, T)
            qkt = [A.get(BF16, nqk, T) for _ in range(2)]
            vtt = [A.get(BF16, TT, nv * 128) for _ in range(2)]
            slots = [A.get(BF16, KC * 128) for _ in range(4)]
            if even:
                cs = [[A.get(F32, T) for _ in range(2)] for _ in range(2)]
                tmp = [[A.get(BF16, T)] + [A.get(F32, T) for _ in range(4)] for _ in range(2)]
            xb1 = Buf()
            xb, qkb, vtb, csb = [xb1, xb1], [Buf(), Buf()], [Buf(), Buf()], [Buf(), Buf()]
            tmpb = [[Buf() for _ in range(5)] for _ in range(2)]
            xnb, rsb = Buf(), Buf()
            sqb = [Buf(), Buf()]
            o_w, n_w, L = (c.woff_s if samp else c.woff)[l]["min"]
            tiles = [(2, t) for t in range(NT)] if samp else [tl for tl in TILES if tl[0] == 0]

            def xsrc(s, t):
                if s != 2:
                    return xT_tile(s, t)
                qd_, col = (t * T) // SQ, (t * T) % SQ
                return xg.rearrange("k (r h p) t -> r p k h t", r=4, p=128)[qd_][:, :, :, col:col + T]
            entries = []
            for _ in tiles:
                entries += [(o_w + pl[1] * 128 * L, L) for pl in plan]
            ws = WStream(B, wbf_s if samp else wbfs[l], slots, entries, wtok.get(c.DEPTH if samp else l))
            g_ap = gcol(c.DEPTH + l)
            pss, pssb = banks[7], bankb[7]
            e_i = 0
            def xload(j, s_, t_):
                if s_ != 2:
                    B.dma("sp", xs[j], xT_tile(s_, t_), writes=[xb[j]])
                    return
                qd_, col = (t_ * T) // SQ, (t_ * T) % SQ
                hh_ = GP // 128
                for k in range(NGP):
                    B.dma("sp", xs[j][:, k * hh_:(k + 1) * hh_, :],
                          xg[k, qd_ * GP:(qd_ + 1) * GP, col:col + T].rearrange("(h p) t -> p h t", p=128), writes=[xb[j]])
            xload(0, *tiles[0])
            for i, (s, t) in enumerate(tiles):
                p = i % 2
                if even:
                    B.dma("sp", cs[p][0], cosT[:, t * T:(t + 1) * T], writes=[csb[p]])
                    B.dma("sp", cs[p][1], sinT[:, t * T:(t + 1) * T], writes=[csb[p]])
                rmsnorm(xs[p], xb[p], g_ap, xn, xnb, sq2, sqb, pss, pssb, rs, rsb)
                if i + 1 < len(tiles):
                    xload(1 - p, *tiles[i + 1])
                nf = 0
                nvv = 0
                nr = 0
                for kind, wc, oi, gi in plan:
                    wbuf, wsl = ws.get(e_i)
                    e_i += 1
                    w3 = wsl[:, 0:L].rearrange("p (k n) -> p k n", n=128)
                    if kind in ("f", "r"):
                        bk = nf % 2
                        nf += 1
                        bank = banks[bk]

                        def mm(e, w3=w3, bank=bank):
                            for kc in range(KC):
                                ins = e.matmul(bank[:, 0:T], w3[:, kc, :], xn[:, kc, :], start=(kc == 0), stop=(kc == KC - 1))
                            return ins
                        B.op("pe", mm, reads=[wbuf, xnb], writes=[bankb[bk]])
                        if kind == "f":
                            B.op("act", lambda e, bank=bank, oi=oi, p=p: e.copy(out=qkt[p][:, oi, :], in_=bank[:, 0:T]),
                                 reads=[bankb[bk]], writes=[qkb[p]])
                        else:
                            q2 = nr % 2
                            nr += 1
                            sqh, rr, qn, t1, t2 = tmp[q2]
                            bsq, brr, bqn, bt1, bt2 = tmpb[q2]
                            B.op("act", lambda e, bank=bank, sqh=sqh: e.activation(out=sqh, in_=bank[:, 0:T], func=AF.Square),
                                 reads=[bankb[bk]], writes=[bsq])
                            B.op("pe", lambda e, sqh=sqh: e.matmul(banks[4][:, 0:T], ones128, sqh, start=True, stop=True),
                                 reads=[bsq, cb], writes=[bankb[4]])
                            B.op("act", lambda e, rr=rr: e.activation(out=rr, in_=banks[4][:, 0:T], func=AF.Sqrt, bias=EPS, scale=1.0),
                                 reads=[bankb[4]], writes=[brr])
                            B.op("dve", lambda e, rr=rr: e.reciprocal(out=rr, in_=rr), writes=[brr])
                            B.op("dve", lambda e, bank=bank, qn=qn, rr=rr, gi=gi: e.scalar_tensor_tensor(
                                out=qn, in0=bank[:, 0:T], scalar=qkg_sb[:, gi:gi + 1], in1=rr, op0=ALU.mult, op1=ALU.mult),
                                reads=[bankb[bk], brr, cb], writes=[bqn])
                            B.op("pe", lambda e, qn=qn: e.matmul(banks[5][:, 0:T], pi_sb, qn, start=True, stop=True),
                                 reads=[bqn, cb], writes=[bankb[5]])
                            B.op("dve", lambda e, qn=qn, t1=t1, p=p: e.tensor_tensor(out=t1, in0=qn, in1=cs[p][0], op=ALU.mult),
                                 reads=[bqn, csb[p]], writes=[bt1])
                            B.op("dve", lambda e, t2=t2, p=p: e.tensor_tensor(out=t2, in0=banks[5][:, 0:T], in1=cs[p][1], op=ALU.mult),
                                 reads=[bankb[5], csb[p]], writes=[bt2])
                            B.op("pool", lambda e, t1=t1, t2=t2, oi=oi, p=p: e.tensor_tensor(out=qkt[p][:, oi, :], in0=t1, in1=t2, op=ALU.add),
                                 reads=[bt1, bt2], writes=[qkb[p]])
                    else:
                        bk = 2 + nvv % 2
                        nvv += 1
                        bank = banks[bk]

                        def mmv(e, w3=w3, bank=bank):
                            for tt in range(TT):
                                for kc in range(KC):
                                    ins = e.matmul(bank[:, tt * 128:(tt + 1) * 128], xn[:, kc, tt * 128:(tt + 1) * 128], w3[:, kc, :],
                                                   start=(kc == 0), stop=(kc == KC - 1))
                            return ins
                        B.op("pe", mmv, reads=[wbuf, xnb], writes=[bankb[bk]])
                        B.op("dve", lambda e, bank=bank, oi=oi, p=p: e.tensor_copy(
                            out=vtt[p][:, :, oi * 128:(oi + 1) * 128], in_=bank[:, 0:T].rearrange("p (a b) -> p a b", b=128)),
                            reads=[bankb[bk]], writes=[vtb[p]])
                if samp:
                    qk_d, vt_d = qks, vts
                    t0 = t * T
                else:
                    assert s == 0 or not c.BAL or True
                    qk_d, vt_d = qk[0], vt[0]
                    t0 = t * T
                if (not samp) and s == 1:
                    continue
                B.dma("pool", qk_d[0:nqk * 128, t0:t0 + T].rearrange("(c p) t -> p c t", p=128), qkt[p], reads=[qkb[p]])
                B.dma("pool", vt_d[t0:t0 + T, 0:nv * 128].rearrange("(tt p) f -> p tt f", p=128), vtt[p], reads=[vtb[p]])
            B.barrier()

        def load_v(dst, vt_a, col, buf):
            step = max(1, NB128 // 4)
            for a in range(0, NB128, step):
                b = min(NB128, a + step)
                B.dma("sp", dst[:, a:b, :], vt_a[a * 128:b * 128, col * 128:(col + 1) * 128].rearrange("(c p) d -> p c d", p=128),
                      writes=[buf])

        def attn_a(l):
            i2 = l // 2
            A.reset()
            E = A.get(F32, 10, 384)
            kT = A.get(BF16, S)
            V = A.get(BF16, NB128, 128)
            qT = [A.get(BF16, S) for _ in range(2)]
            ot = [A.get(BF16, S) for _ in range(2)]
            pex = [A.get(F32, 384) for _ in range(2)]
            pbf = [A.get(BF16, 384) for _ in range(2)]
            rl = [A.get(F32, 128) for _ in range(2)]
            Eb, kb, vb = Buf(), Buf(), Buf()
            qb, ob, pexb, pbfb, rlb = [Buf(), Buf()], [Buf(), Buf()], [Buf(), Buf()], [Buf(), Buf()], [Buf(), Buf()]
            B.dma("sp", E[:, 0:8, :], EA.rearrange("h p f -> p h f"), writes=[Eb])
            SK = c.n_even * 8
            jobs = [(qk[0], vt[0], oT[0], [(8 + kvh, kvh, [(kvh * 4 + g, kvh * 4 + g, i2 * 8 + kvh * 4 + g, kvh * 4 + g) for g in range(4)])
                                           for kvh in range(2)])]
            if c.BAL:
                B.dma("sp", E[:, 8:10, :], EAs.rearrange("h p f -> p h f"), writes=[Eb])
                jobs.append((qks, vts, oTs, [(2, 0, [(0, 8, SK + i2 * 2, 0), (1, 9, SK + i2 * 2 + 1, 1)])]))
            pend = []

            def flush():
                for a_, k_ in pend:
                    B.op(*a_, **k_)
                pend.clear()
            hh = 0
            n = 0
            for qk_a, vt_a, oT_a, groups in jobs:
                for kch, vcol, heads in groups:
                    B.dma("sp", kT, qk_a[kch * 128:(kch + 1) * 128, :], writes=[kb])
                    load_v(V, vt_a, vcol, vb)
                    for qch, h, skc, och in heads:
                        hp = hh % 2
                        hh += 1
                        B.dma("sp", qT[hp], qk_a[qch * 128:(qch + 1) * 128, :], writes=[qb[hp]])
                        for blk in range(NB128):
                            cs_ = [cc for cc in (blk - 1, blk, blk + 1) if 0 <= cc < NB128]
                            c0 = cs_[0] - (blk - 1)
                            lo, hi = c0 * 128, (c0 + len(cs_)) * 128
                            sb_i = n % 3
                            o_i = 3 + n % 2
                            pp = n % 2
                            n += 1
                            sbank, obank = banks[sb_i], banks[o_i]

                            def mm(e, cs_=cs_, c0=c0, sbank=sbank, hp=hp, blk=blk):
                                for ci, cch in enumerate(cs_):
                                    ins = e.matmul(sbank[:, (c0 + ci) * 128:(c0 + ci + 1) * 128], kT[:, cch * 128:(cch + 1) * 128],
                                                   qT[hp][:, blk * 128:(blk + 1) * 128], start=True, stop=True)
                                return ins
                            B.op("pe", mm, reads=[kb, qb[hp]], writes=[bankb[sb_i]])
                            B.op("act", lambda e, sbank=sbank, pp=pp, lo=lo, hi=hi: e.activation(
                                out=pex[pp][:, lo:hi], in_=sbank[:, lo:hi], func=AF.Exp, scale=SCALE),
                                reads=[bankb[sb_i]], writes=[pexb[pp]])
                            B.op("dve", lambda e, pp=pp, lo=lo, hi=hi, h=h: e.tensor_tensor(
                                out=pbf[pp][:, lo:hi], in0=pex[pp][:, lo:hi], in1=E[:, h, lo:hi], op=ALU.mult),
                                reads=[pexb[pp], Eb], writes=[pbfb[pp]])

                            def pv(e, cs_=cs_, c0=c0, obank=obank, pp=pp):
                                nl = len(cs_)
                                for ci, cch in enumerate(cs_):
                                    e.matmul(obank[:, 0:128], V[:, cch, :], pbf[pp][:, (c0 + ci) * 128:(c0 + ci + 1) * 128],
                                             start=(ci == 0), stop=(ci == nl - 1))
                                for ci, cch in enumerate(cs_):
                                    ins = e.matmul(obank[:, 128:256], ones_bf, pbf[pp][:, (c0 + ci) * 128:(c0 + ci + 1) * 128],
                                                   start=(ci == 0), stop=(ci == nl - 1))
                                return ins
                            flush()
                            pend.append((("pe", pv), dict(reads=[vb, pbfb[pp], cb], writes=[bankb[o_i]])))
                            sk = esink[:, skc:skc + 1]
                            pend.append((("dve", lambda e, obank=obank, pp=pp, sk=sk: e.tensor_scalar(
                                out=rl[pp], in0=obank[:, 128:256], scalar1=sk, scalar2=None, op0=ALU.add)),
                                dict(reads=[bankb[o_i], cb], writes=[rlb[pp]])))
                            pend.append((("dve", lambda e, pp=pp: e.reciprocal(out=rl[pp], in_=rl[pp])), dict(writes=[rlb[pp]])))
                            pend.append((("dve", lambda e, obank=obank, pp=pp, hp=hp, blk=blk: e.tensor_tensor(
                                out=ot[hp][:, blk * 128:(blk + 1) * 128], in0=obank[:, 0:128], in1=rl[pp], op=ALU.mult)),
                                dict(reads=[bankb[o_i], rlb[pp]], writes=[ob[hp]])))
                        flush()
                        B.dma("pool", oT_a[och * 128:(och + 1) * 128, :], ot[hp], reads=[ob[hp]])
            B.barrier()

        def attn_b(l):
            A.reset()
            kT = A.get(BF16, S)
            V = A.get(BF16, NB128, 128)
            qT = [A.get(BF16, S) for _ in range(2)]
            ot = [A.get(BF16, S) for _ in range(2)]
            pbf = [A.get(BF16, 512) for _ in range(3)]
            rl = [A.get(F32, 512) for _ in range(2)]
            kb, vb = Buf(), Buf()
            qb, ob, rlb = [Buf(), Buf()], [Buf(), Buf()], [Buf(), Buf()]
            pbfb = [Buf(), Buf(), Buf()]
            hh = 0
            nq = 0
            QT = 512
            jobs = [(qk[0], vt[0], oT[0], [(18 + kvh, 2 + kvh, [(10 + kvh * 4 + g, 8 + kvh * 4 + g) for g in range(4)]) for kvh in range(2)])]
            if c.BAL:
                jobs.append((qks, vts, oTs, [(5, 1, [(3, 2), (4, 3)])]))
            for qk_a, vt_a, oT_a, groups in jobs:
                for kch, vcol, heads in groups:
                    B.dma("sp", kT, qk_a[kch * 128:(kch + 1) * 128, :], writes=[kb])
                    load_v(V, vt_a, vcol, vb)
                    for qch, och in heads:
                        hp = hh % 2
                        hh += 1
                        B.dma("sp", qT[hp], qk_a[qch * 128:(qch + 1) * 128, :], writes=[qb[hp]])
                        for qt in range(S // QT):
                            o_i, l_i = 4 + nq % 2, 6 + nq % 2
                            rp = nq % 2
                            nq += 1
                            obank, lbank = banks[o_i], banks[l_i]
                            qsl = qT[hp][:, qt * QT:(qt + 1) * QT]

                            def emit_s(kc, qsl=qsl):
                                sb_i = kc % 3
                                B.op("pe", lambda e, kc=kc, sb_i=sb_i, qsl=qsl: e.matmul(
                                    banks[sb_i][:, 0:QT], kT[:, kc * 128:(kc + 1) * 128], qsl, start=True, stop=True),
                                    reads=[kb, qb[hp]], writes=[bankb[sb_i]])
                            emit_s(0)
                            for kc in range(NB128):
                                if kc + 1 < NB128:
                                    emit_s(kc + 1)
                                sb_i = kc % 3
                                B.op("act", lambda e, sb_i=sb_i: e.activation(out=pbf[sb_i], in_=banks[sb_i][:, 0:QT], func=AF.Exp, scale=SCALE),
                                     reads=[bankb[sb_i]], writes=[pbfb[sb_i]])

                                def pv(e, kc=kc, sb_i=sb_i, obank=obank, lbank=lbank):
                                    e.matmul(obank[:, 0:QT], V[:, kc, :], pbf[sb_i], start=(kc == 0), stop=(kc == NB128 - 1))
                                    return e.matmul(lbank[:, 0:QT], ones_bf, pbf[sb_i], start=(kc == 0), stop=(kc == NB128 - 1))
                                edge = kc in (0, NB128 - 1)
                                B.op("pe", pv, reads=[vb, pbfb[sb_i], cb], writes=([bankb[o_i], bankb[l_i]] if edge else []))
                            B.op("dve", lambda e, lbank=lbank, rp=rp: e.reciprocal(out=rl[rp], in_=lbank[:, 0:QT]),
                                 reads=[bankb[l_i]], writes=[rlb[rp]])
                            B.op("dve", lambda e, obank=obank, rp=rp, hp=hp, qt=qt: e.tensor_tensor(
                                out=ot[hp][:, qt * QT:(qt + 1) * QT], in0=obank[:, 0:QT], in1=rl[rp], op=ALU.mult),
                                reads=[bankb[o_i], rlb[rp]], writes=[ob[hp]])
                        B.dma("pool", oT_a[och * 128:(och + 1) * 128, :], ot[hp], reads=[ob[hp]])
            B.barrier()

        def attn_c(l):
            li = l // 2
            A.reset()
            PT = [A.get(F32, 14, 64) for _ in range(2)]
            kT = A.get(BF16, S)
            V0 = A.get(BF16, NB128, 128)
            V1 = A.get(BF16, NB128, 128)
            qT = [A.get(BF16, S) for _ in range(2)]
            ot = [A.get(BF16, S) for _ in range(2)]
            pex = [A.get(F32, 256) for _ in range(2)]
            pbf = [A.get(BF16, 256) for _ in range(2)]
            rl = [A.get(F32, 64) for _ in range(2)]
            kb, vb = Buf(), Buf()
            ptb, qb, ob, pexb, pbfb, rlb = ([Buf(), Buf()] for _ in range(6))
            pend = []

            def flush():
                for a_, k_ in pend:
                    B.op(*a_, **k_)
                pend.clear()
            hh = 0
            n = 0
            jobs = [(qk[0], vt[0], oT[0], [(8 + kvh, kvh, [(kvh * 4 + g, EC[li, kvh * 4 + g], kvh * 4 + g) for g in range(4)]) for kvh in range(2)])]
            if c.BAL:
                jobs.append((qks, vts, oTs, [(2, 0, [(0, ECs[li, 0], 0), (1, ECs[li, 1], 1)])]))
            for qk_a, vt_a, oT_a, groups in jobs:
                for kch, vcol, heads in groups:
                    B.dma("sp", kT, qk_a[kch * 128:(kch + 1) * 128, :], writes=[kb])
                    load_v(V0, vt_a, vcol, vb)
                    B.dma("sp", V1[:, 0:NB128 - 1, :],
                          vt_a[64:64 + (NB128 - 1) * 128, vcol * 128:(vcol + 1) * 128].rearrange("(c p) d -> p c d", p=128), writes=[vb])
                    for qch, ec_a, och in heads:
                        hp = hh % 2
                        hh += 1
                        B.dma("sp", qT[hp], qk_a[qch * 128:(qch + 1) * 128, :], writes=[qb[hp]])
                        B.dma("sp", PT[hp][0:64], ec_a[0:14].rearrange("d k q -> k d q"), writes=[ptb[hp]])
                        B.dma("sp", PT[hp][64:128], ec_a[1:15].rearrange("d k q -> k d q"), writes=[ptb[hp]])
                        for r in range(ROWS):
                            rs_ = min(max(r - 4, 0), ROWS - 8)
                            d0 = rs_ - r + 7
                            sb_i = n % 3
                            o_i = 3 + n % 2
                            pp = n % 2
                            n += 1
                            sbank, obank = banks[sb_i], banks[o_i]

                            def mm(e, rs_=rs_, r=r, sbank=sbank, hp=hp):
                                for m in range(4):
                                    k0 = rs_ * 64 + m * 128
                                    ins = e.matmul(sbank[:, m * 64:(m + 1) * 64], kT[:, k0:k0 + 128], qT[hp][:, r * 64:(r + 1) * 64],
                                                   start=True, stop=True)
                                return ins
                            B.op("pe", mm, reads=[kb, qb[hp]], writes=[bankb[sb_i]])
                            B.op("act", lambda e, sbank=sbank, pp=pp: e.activation(out=pex[pp], in_=sbank[:, 0:256], func=AF.Exp, scale=SCALE),
                                 reads=[bankb[sb_i]], writes=[pexb[pp]])
                            B.op("dve", lambda e, pp=pp, hp=hp, d0=d0: e.tensor_tensor(
                                out=pbf[pp].rearrange("p (a b) -> p a b", b=64), in0=pex[pp].rearrange("p (a b) -> p a b", b=64),
                                in1=PT[hp][:, d0:d0 + 7:2, :], op=ALU.mult), reads=[pexb[pp], ptb[hp]], writes=[pbfb[pp]])

                            def pv(e, rs_=rs_, obank=obank, pp=pp):
                                for m in range(4):
                                    vv = V0[:, rs_ // 2 + m, :] if rs_ % 2 == 0 else V1[:, (rs_ - 1) // 2 + m, :]
                                    e.matmul(obank[:, 0:64], vv, pbf[pp][:, m * 64:(m + 1) * 64], start=(m == 0), stop=(m == 3))
                                for m in range(4):
                                    ins = e.matmul(obank[:, 64:128], ones_bf, pbf[pp][:, m * 64:(m + 1) * 64], start=(m == 0), stop=(m == 3))
                                return ins
                            flush()
                            pend.append((("pe", pv), dict(reads=[vb, pbfb[pp], cb], writes=[bankb[o_i]])))
                            pend.append((("dve", lambda e, obank=obank, pp=pp: e.reciprocal(out=rl[pp], in_=obank[:, 64:128])),
                                         dict(reads=[bankb[o_i]], writes=[rlb[pp]])))
                            pend.append((("dve", lambda e, obank=obank, pp=pp, hp=hp, r=r: e.tensor_tensor(
                                out=ot[hp][:, r * 64:(r + 1) * 64], in0=obank[:, 0:64], in1=rl[pp], op=ALU.mult)),
                                dict(reads=[bankb[o_i], rlb[pp]], writes=[ob[hp]])))
                        flush()
                        B.dma("pool", oT_a[och * 128:(och + 1) * 128, :], ot[hp], reads=[ob[hp]])
            B.barrier()

        def attn_d(l):
            A.reset()
            E = A.get(F32, 15, 192)
            kT = A.get(BF16, S)
            qT = [A.get(BF16, S) for _ in range(3)]
            Vg = [A.get(BF16, S // 64, 128) for _ in range(3)]
            ol = A.get(F32, 2, S)
            ot = A.get(BF16, S)
            pex = [A.get(F32, 192) for _ in range(2)]
            pbf = [A.get(BF16, 192) for _ in range(2)]
            Eb, kb, qb, vb, olb, ob = Buf(), Buf(), Buf(), Buf(), Buf(), Buf()
            pexb, pbfb = [Buf(), Buf()], [Buf(), Buf()]
            B.dma("sp", E[0:64, 0:12, :], ED.rearrange("h p f -> p h f"), writes=[Eb])
            pend = []

            def flush():
                for a_, k_ in pend:
                    B.op(*a_, **k_)
                pend.clear()
            n = 0
            jobs = [(qk[0], vt[0], oT[0], [(22 + slot, [10 + g * 4 + slot for g in range(3)], 2 + slot, [g * 4 + slot for g in range(3)], 8 + slot)
                                           for slot in range(4)])]
            if c.BAL:
                B.dma("sp", E[0:64, 12:15, :], EDs.rearrange("h p f -> p h f"), writes=[Eb])
                jobs.append((qks, vts, oTs, [(6, [3, 4, 5], 1, [12, 13, 14], 2)]))
            for qk_a, vt_a, oT_a, slots_ in jobs:
                for kch, qchs, vcol, eidx, och in slots_:
                    B.dma("sp", kT, qk_a[kch * 128:(kch + 1) * 128, :], writes=[kb])
                    for g, dil in enumerate((1, 4, 16)):
                        B.dma("sp", qT[g], qk_a[qchs[g] * 128:(qchs[g] + 1) * 128, :], writes=[qb])
                        nb = S // dil // 64
                        for rho in range(dil):
                            B.dma("sp", Vg[g][0:64, rho * nb:(rho + 1) * nb, :],
                                  vt_a[rho:S:dil, vcol * 128:(vcol + 1) * 128].rearrange("(c k) d -> k c d", k=64), writes=[vb])
                    for g, dil in enumerate((1, 4, 16)):
                        nb = S // dil // 64
                        hh = eidx[g]
                        for rho in range(dil):
                            for i in range(nb):
                                cs_ = [cc for cc in (i - 1, i, i + 1) if 0 <= cc < nb]
                                c0 = cs_[0] - (i - 1)
                                lo, hi = c0 * 64, (c0 + len(cs_)) * 64
                                sb_i = n % 3
                                o_i = 3 + n % 2
                                pp = n % 2
                                n += 1
                                sbank, obank = banks[sb_i], banks[o_i]

                                def tsl(blk, dil=dil, rho=rho):
                                    a = blk * 64 * dil + rho
                                    return slice(a, a + 63 * dil + 1, dil) if dil > 1 else slice(a, a + 64)
                                qs = tsl(i)

                                def mm(e, cs_=cs_, c0=c0, sbank=sbank, g=g, qs=qs, tsl=tsl):
                                    for ci, cch in enumerate(cs_):
                                        ins = e.matmul(sbank[0:64, (c0 + ci) * 64:(c0 + ci + 1) * 64], kT[:, tsl(cch)], qT[g][:, qs],
                                                       start=True, stop=True)
                                    return ins
                                B.op("pe", mm, reads=[kb, qb], writes=[bankb[sb_i]])
                                B.op("act", lambda e, sbank=sbank, pp=pp, lo=lo, hi=hi: e.activation(
                                    out=pex[pp][0:64, lo:hi], in_=sbank[0:64, lo:hi], func=AF.Exp, scale=SCALE),
                                    reads=[bankb[sb_i]], writes=[pexb[pp]])
                                B.op("dve", lambda e, pp=pp, lo=lo, hi=hi, hh=hh: e.tensor_tensor(
                                    out=pbf[pp][0:64, lo:hi], in0=pex[pp][0:64, lo:hi], in1=E[0:64, hh, lo:hi], op=ALU.mult),
                                    reads=[pexb[pp], Eb], writes=[pbfb[pp]])

                                def pv(e, cs_=cs_, c0=c0, obank=obank, pp=pp, g=g, rho=rho, nb=nb):
                                    nl = len(cs_)
                                    for ci, cch in enumerate(cs_):
                                        e.matmul(obank[:, 0:64], Vg[g][0:64, rho * nb + cch, :], pbf[pp][0:64, (c0 + ci) * 64:(c0 + ci + 1) * 64],
                                                 start=(ci == 0), stop=(ci == nl - 1))
                                    for ci, cch in enumerate(cs_):
                                        ins = e.matmul(obank[:, 64:128], ones_bf[0:64, :], pbf[pp][0:64, (c0 + ci) * 64:(c0 + ci + 1) * 64],
                                                       start=(ci == 0), stop=(ci == nl - 1))
                                    return ins
                                flush()
                                pend.append((("pe", pv), dict(reads=[vb, pbfb[pp], cb], writes=[bankb[o_i]])))
                                src = obank[:, 0:128].rearrange("p (a b) -> p a b", b=64)
                                if g == 0:
                                    pend.append((("act", lambda e, src=src, qs=qs: e.copy(out=ol[:, :, qs], in_=src)),
                                                 dict(reads=[bankb[o_i]], writes=[olb])))
                                else:
                                    pend.append((("dve", lambda e, src=src, qs=qs: e.tensor_tensor(out=ol[:, :, qs], in0=src, in1=ol[:, :, qs], op=ALU.add)),
                                                 dict(reads=[bankb[o_i]], writes=[olb])))
                    flush()
                    B.op("dve", lambda e: e.reciprocal(out=ol[:, 1, :], in_=ol[:, 1, :]), writes=[olb])
                    B.op("dve", lambda e: e.tensor_tensor(out=ot, in0=ol[:, 0, :], in1=ol[:, 1, :], op=ALU.mult), reads=[olb], writes=[ob])
                    B.dma("pool", oT_a[och * 128:(och + 1) * 128, :], ot, reads=[ob])
            B.barrier()

        def out_proj(l):
            OKC = 16 if l % 2 == 0 else 12
            A.reset()
            xs = [A.get(F32, KC, T) for _ in range(2)]
            ots = [A.get(BF16, OKC, T) for _ in range(2)]
            slots = [A.get(BF16, c.SLOT) for _ in range(3)]
            xb, otb = [Buf(), Buf()], [Buf(), Buf()]
            o_w, n_w, L = c.woff[l]["mout"]
            tiles = [tl for tl in TILES if tl[0] == 0]
            entries = []
            for _ in tiles:
                entries += [(o_w + j * 128 * L, L) for j in range(n_w)]
            ws = WStream(B, wbfs[l], slots, entries, wtok.get(l))
            e_i = 0

            def ld(i):
                s, t = tiles[i]
                B.dma("sp", xs[i % 2], xT_tile(s, t), writes=[xb[i % 2]])
                B.dma("sp", ots[i % 2], oT[s, 0:OKC * 128, t * T:(t + 1) * T].rearrange("(c p) t -> p c t", p=128), writes=[otb[i % 2]])
            ld(0)
            for i, (s, t) in enumerate(tiles):
                p = i % 2
                if i + 1 < len(tiles):
                    ld(i + 1)
                for oc in range(KC):
                    wbuf, wsl = ws.get(e_i)
                    e_i += 1
                    w3 = wsl[:, 0:L].rearrange("p (k n) -> p k n", n=128)
                    by, byb = banks[oc % 2], bankb[oc % 2]

                    def mm(e, w3=w3, by=by, p=p):
                        for kc in range(OKC):
                            ins = e.matmul(by[:, 0:T], w3[:, kc, :], ots[p][:, kc, :], start=(kc == 0), stop=(kc == OKC - 1))
                        return ins
                    B.op("pe", mm, reads=[wbuf, otb[p]], writes=[byb])
                    B.op("dve", lambda e, by=by, oc=oc, p=p: e.tensor_tensor(out=xs[p][:, oc, :], in0=by[:, 0:T], in1=xs[p][:, oc, :], op=ALU.add),
                         reads=[byb], writes=[xb[p]])
                B.dma("pool", xT_tile(s, t), xs[p], reads=[xb[p]])
            B.barrier()

        def out_proj_s(l):
            OKC = 4 if l % 2 == 0 else 3
            A.reset()
            ots = [A.get(BF16, OKC, T) for _ in range(2)]
            yt = [A.get(F32, KC, T) for _ in range(2)]
            slots = [A.get(BF16, OKC * 128) for _ in range(4)]
            otb, ytb = [Buf(), Buf()], [Buf(), Buf()]
            o_w, n_w, L = c.woff_s[l]["mout"]
            entries = []
            for _ in range(NT):
                entries += [(o_w + j * 128 * L, L) for j in range(n_w)]
            ws = WStream(B, wbf_s, slots, entries, wtok.get(c.DEPTH))
            e_i = 0

            def ld(t):
                B.dma("sp", ots[t % 2], oTs[0:OKC * 128, t * T:(t + 1) * T].rearrange("(c p) t -> p c t", p=128), writes=[otb[t % 2]])
            ld(0)
            for t in range(NT):
                p = t % 2
                if t + 1 < NT:
                    ld(t + 1)
                for oc in range(KC):
                    wbuf, wsl = ws.get(e_i)
                    e_i += 1
                    w3 = wsl[:, 0:L].rearrange("p (k n) -> p k n", n=128)
                    by, byb = banks[oc % 2], bankb[oc % 2]

                    def mm(e, w3=w3, by=by, p=p):
                        for kc in range(OKC):
                            ins = e.matmul(by[:, 0:T], w3[:, kc, :], ots[p][:, kc, :], start=(kc == 0), stop=(kc == OKC - 1))
                        return ins
                    B.op("pe", mm, reads=[wbuf, otb[p]], writes=[byb])
                    eng = "dve" if oc % 2 else "act"
                    B.op(eng, lambda e, by=by, oc=oc, p=p, eng=eng: (e.tensor_copy(out=yt[p][:, oc, :], in_=by[:, 0:T]) if eng == "dve"
                                                                     else e.copy(out=yt[p][:, oc, :], in_=by[:, 0:T])),
                         reads=[byb], writes=[ytb[p]])
                qd_, col = (t * T) // SQ, (t * T) % SQ
                B.dma("pool", part.rearrange("(r kc p) t -> r p kc t", r=4, p=128)[qd_][:, :, col:col + T], yt[p], reads=[ytb[p]])
            B.cc(ccsem, "ReduceScatter", ALU.add, RG, [part], [red[:, 0:SQ]])
            A.reset()
            xs = [A.get(F32, KC, T) for _ in range(2)]
            rt = [A.get(F32, KC, T) for _ in range(2)]
            xb, rb_ = [Buf(), Buf()], [Buf(), Buf()]
            for i, (s_, t) in enumerate([tl for tl in TILES if tl[0] == 1]):
                p = i % 2
                B.dma("sp", xs[p], xT_tile(1, t), writes=[xb[p]])
                B.dma("sp", rt[p], red.rearrange("(kc p) t -> p kc t", p=128)[:, :, t * T:(t + 1) * T], writes=[rb_[p]])
                B.op("dve", lambda e, p=p: e.tensor_tensor(out=xs[p], in0=xs[p], in1=rt[p], op=ALU.add), reads=[rb_[p]], writes=[xb[p]])
                B.dma("pool", xT_tile(1, t), xs[p], reads=[xb[p]])
            B.barrier()

        def gather_x():
            for k in range(NGP):
                B.cc(ccsem, "AllGather", ALU.bypass, RG, [xTs[k * GP:(k + 1) * GP, 0:SQ]], [xg[k][:, 0:SQ]])

        def final():
            A.reset()
            xs = [A.get(F32, KC, T) for _ in range(2)]
            xn = A.get(F32, KC, T)
            yo = [A.get(F32, TT, D) for _ in range(2)]
            sq2 = [A.get(BF16, min(2, KC), T) for _ in range(2)]
            rs = A.get(F32, T)
            xb, yb = [Buf(), Buf()], [Buf(), Buf()]
            xnb, rsb = Buf(), Buf()
            sqb = [Buf(), Buf()]
            g_ap = gcol(3 * c.DEPTH)
            G = min(4, KC)
            tiles = list(TILES)
            B.dma("sp", xs[0], xT_tile(*tiles[0]), writes=[xb[0]])
            for i, (s, t) in enumerate(tiles):
                p = i % 2
                if i + 1 < len(tiles):
                    B.dma("sp", xs[1 - p], xT_tile(*tiles[i + 1]), writes=[xb[1 - p]])
                rmsnorm(xs[p], xb[p], g_ap, xn, xnb, sq2, sqb, banks[6], bankb[6], rs, rsb)
                n = 0
                for tt in range(TT):
                    for g in range(KC // G):
                        bk = n % 4
                        n += 1

                        def tr(e, tt=tt, g=g, bk=bk):
                            for j in range(G):
                                kc = g * G + j
                                ins = e.transpose(banks[bk][:, j * 128:(j + 1) * 128], xn[:, kc, tt * 128:(tt + 1) * 128], ident)
                            return ins
                        B.op("pe", tr, reads=[xnb, cb], writes=[bankb[bk]])
                        eng = "dve" if n % 2 else "act"

                        def cp(e, tt=tt, g=g, bk=bk, p=p, eng=eng):
                            o = yo[p][:, tt, g * G * 128:(g + 1) * G * 128]
                            i_ = banks[bk][:, 0:G * 128]
                            return e.tensor_copy(out=o, in_=i_) if eng == "dve" else e.copy(out=o, in_=i_)
                        B.op(eng, cp, reads=[bankb[bk]], writes=[yb[p]])
                dsty = y_out[0] if s == 0 else ys_out
                B.dma("pool", dsty[t * T:(t + 1) * T, :].rearrange("(tt p) d -> p tt d", p=128), yo[p], reads=[yb[p]])
            B.barrier()

        ph = phases or ("precast", "etab", "in", "layers", "mix", "final")
        if "precast" in ph:
            precast()
        if "etab" in ph:
            etables()
        if "in" in ph:
            in_transpose()
        if "layers" in ph:
            for l in range(c.DEPTH):
                B.seg()
                if "noffn" not in ph:
                    ffn(l, 1)
                if "mix" in ph:
                    B.seg()
                    in_proj(l)
                    if c.BAL:
                        gather_x()
                        in_proj(l, samp=True)
                    B.seg()
                    if l % 2 == 0:
                        attn_a(l)
                        B.seg()
                        attn_b(l)
                    else:
                        attn_c(l)
                        B.seg()
                        attn_d(l)
                    B.seg()
                    out_proj(l)
                    if c.BAL:
                        out_proj_s(l)
                B.seg()
                if "noffn" not in ph:
                    ffn(l, 2)
        if "final" in ph:
            final()
        B.barrier()
        block = es.enter_context(nc.Block())
        B.emit(block)
    nc._nops = B.nops
    return nc


def _t5_bucket_np(rel):
    half, max_exact = 16, 8
    n = np.abs(rel)
    nf = np.maximum(n, 1).astype(np.float32)
    large = max_exact + (np.log(nf / np.float32(max_exact)) / np.float32(math.log(2048 / max_exact))
                         * np.float32(half - max_exact)).astype(np.int32)
    large = np.minimum(large, half - 1)
    return np.where(rel > 0, half, 0) + np.where(n < max_exact, n, large)


def make_consts(cfg):
    S = cfg.S
    out = {}
    n_pairs, n_freq = 64, 32
    pos = np.arange(S)
    row = (pos // GRID_W).astype(np.float32)
    col = (pos % GRID_W).astype(np.float32)
    omega = (np.float32(10000.0) ** (-(np.arange(n_freq, dtype=np.float32) * np.float32(2.0) / np.float32(n_pairs)))).astype(np.float32)
    ang = np.concatenate([row[:, None] * omega, col[:, None] * omega], axis=-1).astype(np.float32)
    cos = np.cos(ang).astype(np.float32)
    sin = np.sin(ang).astype(np.float32)
    out["cosT"] = np.ascontiguousarray(np.repeat(cos, 2, axis=1).T)
    out["sinT"] = np.ascontiguousarray(np.repeat(sin, 2, axis=1).T)
    pi = np.zeros((128, 128), np.float32)
    for i in range(64):
        pi[2 * i + 1, 2 * i] = -1.0
        pi[2 * i, 2 * i + 1] = 1.0
    out["piT"] = pi
    kk = np.arange(128)[:, None, None]
    cc = np.arange(3)[None, :, None]
    qq = np.arange(128)[None, None, :]
    rel = (cc - 1) * 128 + kk - qq
    bk = _t5_bucket_np(rel)
    m = (np.abs(rel) <= 128)
    ohA = np.zeros((128, 32, 3, 128), np.float32)
    for b in range(32):
        ohA[:, b] = ((bk == b) & m)
    out["ohA"] = ohA.reshape(128, 32, 384)
    out["mA"] = m.astype(np.float32).reshape(128, 384)
    kk = np.arange(64)[:, None, None]
    qq = np.arange(64)[None, None, :]
    rs = (cc - 1) * 64 + kk - qq
    m = (np.abs(rs) <= 64)
    ohD = np.zeros((64, 3, 32, 3, 64), np.float32)
    for g, dil in enumerate((1, 4, 16)):
        bk = _t5_bucket_np(rs * dil)
        for b in range(32):
            ohD[:, g, b] = ((bk == b) & m)
    out["ohD"] = ohD.reshape(64, 3, 32, 192)
    out["mD"] = m.astype(np.float32).reshape(64, 192)
    kc_ = np.arange(64)[:, None]
    qc_ = np.arange(64)[None, :]
    dci = np.clip(kc_ - qc_ + 15, 0, 30)
    cs = np.clip(qc_ - 8, 0, 48)
    m = (kc_ >= cs) & (kc_ < cs + 16)
    ohC = np.zeros((31, 64, 64), np.float32)
    for d_ in range(31):
        ohC[d_] = ((dci == d_) & m).T
    out["ohC"] = ohC
    out["mC"] = m.astype(np.float32)
    return out


def _chunks(W, kc_n, oc_n):
    return W.reshape(kc_n, 128, oc_n, 128).transpose(2, 1, 0, 3)


def pack_weights(cfg, inp):
    c = cfg
    KC, FFC, FF = c.KC, c.FFC, c.FF
    wp = np.empty((c.NW,), np.float32)

    def put(off, arr):
        a = np.ascontiguousarray(arr, dtype=np.float32).reshape(-1)
        wp[off:off + a.size] = a

    w_in = {1: inp["ffn1_w_in"], 2: inp["ffn2_w_in"]}
    w_out = {1: inp["ffn1_w_out"], 2: inp["ffn2_w_out"]}
    for l in range(c.DEPTH):
        i = l // 2
        d = {k: (v[0] + c.lbase[l], v[1], v[2]) for k, v in c.woff[l].items()}
        for which in (1, 2):
            Win = np.asarray(w_in[which][l])
            g = _chunks(Win[:, :FF], KC, FFC)
            u = _chunks(Win[:, FF:], KC, FFC)
            put(d["f%din" % which][0], np.concatenate([g, u], axis=3))
            Wout = np.asarray(w_out[which][l])
            put(d["f%dout" % which][0], _chunks(Wout, FFC, KC))
        if l % 2 == 0:
            put(d["min"][0], _chunks(np.asarray(inp["ab_w_in"][i]), KC, 24))
            put(d["mout"][0], _chunks(np.asarray(inp["ab_w_out"][i]), 16, KC))
        else:
            put(d["min"][0], _chunks(np.asarray(inp["cd_w_in"][i]), KC, 32))
            put(d["mout"][0], _chunks(np.asarray(inp["cd_w_out"][i]), 12, KC))
    return wp


def pack_weights_s(cfg, inp, qd):
    c = cfg
    KC = c.KC
    wp = np.empty((c.NWS,), np.float32)

    def put(off, arr):
        a = np.ascontiguousarray(arr, dtype=np.float32).reshape(-1)
        wp[off:off + a.size] = a

    for l in range(c.DEPTH):
        i = l // 2
        d = c.woff_s[l]
        if l % 2 == 0:
            cin = [2 * qd, 2 * qd + 1, 8 + qd // 2, 12 + 2 * qd, 13 + 2 * qd, 20 + qd // 2, 10 + qd // 2, 22 + qd // 2]
            rout = [2 * qd, 2 * qd + 1, 8 + 2 * qd, 9 + 2 * qd]
            Win, Wout = np.asarray(inp["ab_w_in"][i]), np.asarray(inp["ab_w_out"][i])
            ninc = 24
        else:
            cin = [2 * qd, 2 * qd + 1, 8 + qd // 2, 12 + qd, 16 + qd, 20 + qd, 24 + qd, 10 + qd // 2, 28 + qd]
            rout = [2 * qd, 2 * qd + 1, 8 + qd]
            Win, Wout = np.asarray(inp["cd_w_in"][i]), np.asarray(inp["cd_w_out"][i])
            ninc = 32
        put(d["min"][0], _chunks(Win, KC, ninc)[cin])
        Ws = np.concatenate([Wout[r * 128:(r + 1) * 128, :] for r in rout], axis=0)
        put(d["mout"][0], _chunks(Ws, len(rout), KC))
    return wp


def host_inputs(cfg, inp):
    c = cfg
    KC = c.KC
    shared = make_consts(c)
    shared["wpack"] = pack_weights(c, inp)

    def fm(v):
        return np.asarray(v, np.float32).reshape(KC, 128).T

    cols = [fm(inp["norm_ffn1"][l]) for l in range(c.DEPTH)] + [fm(inp["norm_mix"][l]) for l in range(c.DEPTH)] + \
           [fm(inp["norm_ffn2"][l]) for l in range(c.DEPTH)] + [fm(inp["final_norm"])]
    shared["gains"] = np.ascontiguousarray(np.concatenate(cols, axis=1))
    qkg = np.zeros((128, 2 * c.n_even), np.float32)
    for i in range(c.n_even):
        qkg[:, 2 * i] = np.asarray(inp["ab_q_gain"][i])
        qkg[:, 2 * i + 1] = np.asarray(inp["ab_k_gain"][i])
    shared["qkg"] = qkg
    shared["sink"] = np.ascontiguousarray(np.asarray(inp["ab_sink"], np.float32).reshape(-1))
    shared["t5"] = np.ascontiguousarray(np.asarray(inp["t5_table"], np.float32).reshape(-1))
    r = np.asarray(inp["cd_rpb"], np.float32)
    shared["rpb"] = (np.ascontiguousarray(r.reshape(r.shape[0], 120, 31).transpose(0, 2, 1)) if r.shape[0]
                     else np.zeros((1, 31, 120), np.float32))
    return shared


def core_inputs(cfg, inp, qd):
    c = cfg
    m = {"wpack_s": pack_weights_s(c, inp, qd)}
    t5 = np.asarray(inp["t5_table"], np.float32)
    m["t5m"] = np.ascontiguousarray(t5[:, [2 * qd, 2 * qd + 1, 8 + qd, 12 + qd, 16 + qd]].reshape(-1))
    sk = np.asarray(inp["ab_sink"], np.float32)
    m["sinkm"] = np.ascontiguousarray(sk[:, 2 * qd:2 * qd + 2].reshape(-1))
    r = np.asarray(inp["cd_rpb"], np.float32)
    m["rpbm"] = (np.ascontiguousarray(r[:, 2 * qd:2 * qd + 2].reshape(r.shape[0], 30, 31).transpose(0, 2, 1)) if r.shape[0]
                 else np.zeros((1, 31, 30), np.float32))
    return m


_NC_CACHE = {}


def kernel(**inputs):
    cfg = Cfg()
    shared = host_inputs(cfg, inputs)
    xp = np.asarray(inputs["x_prompt"], np.float32)
    xsm = np.asarray(inputs["x_sample"], np.float32)
    SQ = cfg.SQ
    percore = [core_inputs(cfg, inputs, qd) for qd in range(4)]
    in_maps = []
    for cid in range(N_CORES):
        j, qd = cid // 4, cid % 4
        m = dict(shared)
        m.update(percore[qd])
        m["x"] = xp[cid:cid + 1]
        m["xs"] = np.ascontiguousarray(xsm[j, qd * SQ:(qd + 1) * SQ])
        in_maps.append(m)
    if "nc" not in _NC_CACHE:
        _NC_CACHE["nc"] = build(cfg)
    res = run_bass_kernel_spmd(_NC_CACHE["nc"], in_maps, core_ids=list(range(N_CORES)))
    y_prompt = np.stack([res.results[cid]["y"][0] for cid in range(N_CORES)], axis=0)
    y_sample = np.stack([np.concatenate([res.results[4 * j + qd]["ys"] for qd in range(4)], axis=0)
                         for j in range(xsm.shape[0])], axis=0)
    return (y_prompt.astype(np.float32), y_sample.astype(np.float32))
```

```python
import contextlib
import math
import numpy as np
import concourse.bass as bass
import concourse.mybir as mybir
from concourse.bass_utils import run_bass_kernel_spmd

F32 = mybir.dt.float32
BF16 = mybir.dt.bfloat16
ALU = mybir.AluOpType
AF = mybir.ActivationFunctionType
ENG = ("pe", "act", "dve", "pool", "sp")

HD = 128
GRID_W = 64
EPS = 1e-6
SCALE = HD ** -0.5
N_CORES = 8


class Cfg:
    def __init__(self, D=2048, FF=5632, S=4096, NSEQ=1, DEPTH=4, T=512, BAL=True, NCORES=8):
        self.D, self.FF, self.S, self.NSEQ, self.DEPTH, self.T = D, FF, S, NSEQ, DEPTH, T
        self.KC = D // 128
        self.BAL = BAL
        self.NCORES = NCORES
        self.SQ = S // 4 if BAL else 0
        self.FFC = FF // 128
        self.NT = S // T
        self.TT = T // 128
        self.n_even = (DEPTH + 1) // 2
        self.n_odd = DEPTH // 2
        off = 0
        self.woff = []
        self.lbase = []
        for l in range(DEPTH):
            d = {}
            self.lbase.append(off)
            inc = 24 if l % 2 == 0 else 32
            okc = 16 if l % 2 == 0 else 12
            for name, nblk, L in (("f1in", self.FFC, self.KC * 256), ("f1out", self.KC, self.FFC * 128),
                                  ("min", inc, self.KC * 128), ("mout", self.KC, okc * 128),
                                  ("f2in", self.FFC, self.KC * 256), ("f2out", self.KC, self.FFC * 128)):
                d[name] = (off - self.lbase[l], nblk, L)
                off += nblk * 128 * L
            self.woff.append(d)
        self.NW = off
        self.lbase.append(off)
        off = 0
        self.woff_s = []
        for l in range(DEPTH):
            nin, okc = (8, 4) if l % 2 == 0 else (9, 3)
            d = {"min": (off, nin, self.KC * 128)}
            off += nin * 128 * self.KC * 128
            d["mout"] = (off, self.KC, okc * 128)
            off += self.KC * 128 * okc * 128
            self.woff_s.append(d)
        self.NWS = off
        self.SLOT = max(self.KC * 256, self.FFC * 128, 16 * 128)


class Buf:
    __slots__ = ("w", "r", "const")

    def __init__(self, const=False):
        self.w = None
        self.r = {}
        self.const = const


class Bld:
    def __init__(self, nc, es, nsets=4, ndma=7):
        self.nc = nc
        self.q = {e: [] for e in ENG}
        self.nsets, self.ndma, self.cur = nsets, ndma, 0
        self.psem = [{e: es.enter_context(nc.semaphore(f"p{s}{e}")) for e in ENG} for s in range(nsets)]
        self.pcnt = [{e: 0 for e in ENG} for s in range(nsets)]
        self.dq = ("sp", "pool")
        self.dsem = [{qn: [es.enter_context(nc.semaphore(f"d{s}{qn}{i}")) for i in range(ndma)] for qn in self.dq}
                     for s in range(nsets)]
        self.dcnt = [{qn: [0] * ndma for qn in self.dq} for s in range(nsets)]
        self.drr = {qn: 0 for qn in self.dq}
        self.last = {e: None for e in ENG}
        self.dlast = {}
        self.nops = 0

    def seg(self):
        self.cur = (self.cur + 1) % self.nsets

    @staticmethod
    def _add(deps, t):
        if t is None:
            return
        k = t[2]
        if k not in deps or deps[k][1] < t[1]:
            deps[k] = t

    def _deps(self, reads, writes):
        deps = {}
        for b in reads:
            self._add(deps, b.w)
        for b in writes:
            self._add(deps, b.w)
            for t in b.r.values():
                self._add(deps, t)
        return deps

    def _commit(self, tok, reads, writes):
        for b in reads:
            if not b.const:
                b.r[tok[2]] = tok
        for b in writes:
            b.w = tok
            b.r = {}

    def op(self, eng, fn, reads=(), writes=()):
        deps = self._deps(reads, writes)
        s = self.cur
        key = ("p", s, eng)
        if eng == "pe":
            for k in [k for k in deps if k[0] == "p" and k[2] == "pe"]:
                del deps[k]
        self.pcnt[s][eng] += 1
        tok = (self.psem[s][eng], self.pcnt[s][eng], key)
        self.q[eng].append((fn, list(deps.values()), tok[0], 1))
        self._commit(tok, reads, writes)
        self.last[eng] = tok
        self.nops += 1
        return tok

    def dma(self, qn, out, in_, reads=(), writes=(), extra=()):
        deps = self._deps(reads, writes)
        for t in extra:
            self._add(deps, t)
        s = self.cur
        i = self.drr[qn]
        self.drr[qn] = (i + 1) % self.ndma
        key = ("d", s, qn, i)
        self._add(deps, self.dlast.get(key))
        self.dcnt[s][qn][i] += 16
        tok = (self.dsem[s][qn][i], self.dcnt[s][qn][i], key)
        self.dlast[key] = tok
        self.q[qn].append((lambda e, o=out, i_=in_: e.dma_start(out=o, in_=i_), list(deps.values()), tok[0], 16))
        self._commit(tok, reads, writes)
        self.nops += 1
        return tok

    def cc(self, es_sem, kind, op, rg, ins, outs):
        self.barrier()
        self.cccnt = getattr(self, "cccnt", 0) + 1
        tok = (es_sem, self.cccnt, ("c",))
        self.q["pool"].append((lambda e: e.collective_compute(kind, op, replica_groups=rg, ins=ins, outs=outs), [], es_sem, 1))
        self.dlast[("c",)] = tok
        self.barrier()

    def barrier(self):
        toks = [t for t in self.last.values() if t is not None] + list(self.dlast.values())
        for e in ENG:
            self.q[e].append((None, list(toks), None, 0))

    def emit(self, block):
        def mk(name):
            def f(e):
                seen = {}
                for fn, deps, sem, inc in self.q[name]:
                    for t in deps:
                        if seen.get(t[2], 0) < t[1]:
                            e.wait_ge(t[0], t[1])
                            seen[t[2]] = t[1]
                    if fn is not None:
                        fn(e).then_inc(sem, inc)
            return f
        block.tensor(mk("pe"))
        block.scalar(mk("act"))
        block.vector(mk("dve"))
        block.gpsimd(mk("pool"))
        block.sync(mk("sp"))


class Arena:
    def __init__(self, t, nbytes):
        self.t, self.n, self.p = t, nbytes, 0

    def reset(self, p=0):
        self.p = p

    def get(self, dtype, *free):
        esz = 4 if dtype == F32 else 2
        n = int(np.prod(free)) * esz
        n_al = (n + 63) // 64 * 64
        assert self.p + n_al <= self.n, f"arena overflow {self.p}+{n_al}>{self.n}"
        a = self.t[:, self.p // 2:(self.p + n) // 2]
        self.p += n_al
        if dtype == F32:
            a = a.bitcast(F32)
        if len(free) == 2:
            a = a.rearrange("p (a b) -> p a b", b=free[1])
        elif len(free) == 3:
            a = a.rearrange("p (a b c) -> p a b c", b=free[1], c=free[2])
        return a


class WStream:
    def __init__(self, B, wbf, slots, entries, tok=None):
        self.B, self.wbf, self.slots, self.entries = B, wbf, slots, entries
        self.extra = list(tok) if tok else []
        self.bufs = [Buf() for _ in slots]
        self.issued = 0

    def _load(self, e):
        off, L = self.entries[e]
        k = e % len(self.slots)
        src = self.wbf[off:off + 128 * L].rearrange("(p l) -> p l", p=128)
        self.B.dma("sp", self.slots[k][:, 0:L], src, writes=[self.bufs[k]], extra=self.extra)

    def get(self, e):
        n = len(self.slots)
        while self.issued < min(len(self.entries), e + n):
            self._load(self.issued)
            self.issued += 1
        k = e % n
        return self.bufs[k], self.slots[k]


def build(cfg, phases=None):
    c = cfg
    D, FF, S, NSEQ, KC, FFC, T, TT, NT = c.D, c.FF, c.S, c.NSEQ, c.KC, c.FFC, c.T, c.TT, c.NT
    NB128 = S // 128
    ROWS = S // GRID_W
    nc = bass.Bass("TRN2", target_bir_lowering=False)
    dt = nc.dram_tensor
    x_in = dt("x", [NSEQ, S, D], F32, kind="ExternalInput").ap()
    wpack = dt("wpack", [c.NW], F32, kind="ExternalInput").ap()
    gains = dt("gains", [128, 3 * c.DEPTH * KC + KC], F32, kind="ExternalInput").ap()
    qkg = dt("qkg", [128, 2 * c.n_even], F32, kind="ExternalInput").ap()
    sink = dt("sink", [c.n_even * 8], F32, kind="ExternalInput").ap()
    t5 = dt("t5", [32 * 20], F32, kind="ExternalInput").ap()
    rpb = dt("rpb", [max(c.n_odd, 1), 31, 120], F32, kind="ExternalInput").ap()
    cosT = dt("cosT", [128, S], F32, kind="ExternalInput").ap()
    sinT = dt("sinT", [128, S], F32, kind="ExternalInput").ap()
    piT = dt("piT", [128, 128], F32, kind="ExternalInput").ap()
    ohA = dt("ohA", [128, 32, 384], F32, kind="ExternalInput").ap()
    mA = dt("mA", [128, 384], F32, kind="ExternalInput").ap()
    ohD = dt("ohD", [64, 3, 32, 192], F32, kind="ExternalInput").ap()
    mD = dt("mD", [64, 192], F32, kind="ExternalInput").ap()
    ohC = dt("ohC", [31, 64, 64], F32, kind="ExternalInput").ap()
    mC = dt("mC", [64, 64], F32, kind="ExternalInput").ap()
    y_out = dt("y", [NSEQ, S, D], F32, kind="ExternalOutput").ap()
    SQ = c.SQ
    SQn = max(SQ, T)
    xs_in = dt("xs", [SQn, D], F32, kind="ExternalInput").ap()
    ys_out = dt("ys", [SQn, D], F32, kind="ExternalOutput").ap()
    wpack_s = dt("wpack_s", [c.NWS], F32, kind="ExternalInput").ap()
    t5m = dt("t5m", [32 * 5], F32, kind="ExternalInput").ap()
    sinkm = dt("sinkm", [c.n_even * 2], F32, kind="ExternalInput").ap()
    rpbm = dt("rpbm", [max(c.n_odd, 1), 31, 30], F32, kind="ExternalInput").ap()
    wbf_s = dt("wbf_s", [c.NWS], BF16, kind="Internal").ap()
    xTs = dt("xTs", [D, SQn], F32, kind="Internal").ap()
    GP = min(D, 256)
    NGP = D // GP
    xg = dt("xg", [NGP, 4 * GP, SQn], F32, kind="Internal").ap()
    part = dt("part", [4 * D, SQn], F32, kind="Internal").ap()
    red = dt("red", [D, SQn], F32, kind="Internal").ap()
    qks = dt("qks", [7 * 128, S], BF16, kind="Internal").ap()
    vts = dt("vts", [S, 2 * 128], BF16, kind="Internal").ap()
    oTs = dt("oTs", [4 * 128, S], BF16, kind="Internal").ap()
    EAs = dt("EAs", [2, 128, 384], F32, kind="Internal").ap()
    EDs = dt("EDs", [3, 64, 192], F32, kind="Internal").ap()
    ECs = dt("ECs", [max(c.n_odd, 1), 2, 15, 64, 64], F32, kind="Internal").ap()
    TILES = [(0, t) for t in range(NT)] + [(1, t) for t in range(SQ // T)]
    RG = [list(range(g * 4, g * 4 + 4)) for g in range(c.NCORES // 4)]

    wbfs = [dt(f"wbf{l}", [c.lbase[l + 1] - c.lbase[l]], BF16, kind="Internal").ap() for l in range(c.DEPTH)]
    xT = dt("xT", [NSEQ, D, S], F32, kind="Internal").ap()
    qk = dt("qk", [NSEQ, 26 * 128, S], BF16, kind="Internal").ap()
    vt = dt("vt", [NSEQ, S, 6 * 128], BF16, kind="Internal").ap()
    oT = dt("oT", [NSEQ, 2048, S], BF16, kind="Internal").ap()
    EA = dt("EA", [8, 128, 384], F32, kind="Internal").ap()
    ED = dt("ED", [12, 64, 192], F32, kind="Internal").ap()
    EC = dt("EC", [max(c.n_odd, 1), 8, 15, 64, 64], F32, kind="Internal").ap()

    ARENA_BYTES = 178 * 1024
    CONST_BYTES = 10 * 1024
    with contextlib.ExitStack() as es:
        arena_t = es.enter_context(nc.sbuf_tensor("arena", [128, ARENA_BYTES // 2], BF16))
        const_t = es.enter_context(nc.sbuf_tensor("consts", [128, CONST_BYTES // 2], BF16))
        banks = [es.enter_context(nc.psum_tensor(f"bank{i}", [128, 512], F32)) for i in range(8)]
        B = Bld(nc, es)
        ccsem = es.enter_context(nc.semaphore("ccsem"))
        A = Arena(arena_t, ARENA_BYTES)
        CA = Arena(const_t, CONST_BYTES)
        bankb = [Buf() for _ in range(8)]

        ident = CA.get(F32, 128)
        onesD = CA.get(BF16, 128)
        ones128 = CA.get(BF16, 128)
        ones_bf = CA.get(BF16, 128)
        pi_sb = CA.get(F32, 128)
        gains_sb = CA.get(F32, 3 * c.DEPTH * KC + KC)
        qkg_sb = CA.get(F32, 2 * c.n_even)
        esink = CA.get(F32, c.n_even * 10)
        cb = Buf(const=True)
        B.op("pool", lambda e: e.memset(ident, 0.0), writes=[cb])
        B.op("pool", lambda e: e.affine_select(out=ident, in_=ident, pattern=[[-1, 128]], compare_op=ALU.not_equal,
                                               fill=1.0, base=0, channel_multiplier=1), writes=[cb])
        B.op("pool", lambda e: e.memset(onesD, 1.0 / D), writes=[cb])
        B.op("pool", lambda e: e.memset(ones128, 1.0 / 128), writes=[cb])
        B.op("pool", lambda e: e.memset(ones_bf, 1.0), writes=[cb])
        B.dma("sp", pi_sb, piT, writes=[cb])
        B.dma("sp", gains_sb, gains, writes=[cb])
        B.dma("sp", qkg_sb, qkg, writes=[cb])
        B.dma("sp", esink[:, 0:c.n_even * 8], sink.partition_broadcast(128), writes=[cb])
        B.dma("sp", esink[:, c.n_even * 8:c.n_even * 10], sinkm.partition_broadcast(128), writes=[cb])
        B.barrier()
        B.op("act", lambda e: e.activation(out=esink, in_=esink, func=AF.Exp), writes=[cb])
        B.barrier()

        def gcol(idx):
            return gains_sb[:, idx * KC:(idx + 1) * KC]

        pcsem = {(l, h): es.enter_context(nc.semaphore(f"pc{l}{h}")) for l in range(c.DEPTH) for h in "ab"}
        pcsem["s"] = es.enter_context(nc.semaphore("pcs"))
        wtok = {}
        pc_pending = {}

        def pc_plan():
            blk = 128 * 16384
            jobs = []
            if c.BAL:
                jobs.append(("s", wpack_s, wbf_s, c.NWS, c.NWS))
            for l in range(c.DEPTH):
                jobs.append((l, wpack[c.lbase[l]:c.lbase[l + 1]], wbfs[l], c.lbase[l + 1] - c.lbase[l], c.woff[l]["min"][0]))
            for key, src, dst, n, split in jobs:
                lst = []
                ca = cb_ = 0
                for e0 in range(0, n, blk):
                    e1 = min(n, e0 + blk)
                    first = e0 < split
                    sem = pcsem["s"] if key == "s" else pcsem[(key, "a" if first else "b")]
                    lst.append((dst[e0:e1].rearrange("(p n) -> p n", p=128), src[e0:e1].rearrange("(p n) -> p n", p=128), sem))
                    if first:
                        ca += 16
                    else:
                        cb_ += 16
                pc_pending[key] = lst
                if key == "s":
                    wtok["s"] = [(pcsem["s"], ca, ("pc", "s"))]
                else:
                    ta = (pcsem[(key, "a")], ca, ("pc", key, "a"))
                    wtok[(key, "a")] = [ta]
                    wtok[(key, "b")] = [ta] + ([(pcsem[(key, "b")], cb_, ("pc", key, "b"))] if cb_ else [])

        def pc_drip(key, k):
            lst = pc_pending.get(key)
            while lst and k > 0:
                o, i_, sem = lst.pop(0)
                B.q["pool"].append((lambda e, o=o, i_=i_: e.dma_start(out=o, in_=i_), [], sem, 16))
                k -= 1

        def precast():
            pc_plan()
            if c.BAL:
                pc_drip("s", 10 ** 9)
            pc_drip(0, 10 ** 9)

        def etables():
            etab_a(t5, 20, list(range(8)), EA)
            etab_d(t5, 20, [(g, 8 + g * 4 + s_) for g in range(3) for s_ in range(4)], ED)
            for li in range(c.n_odd):
                etab_c(li)
            if c.BAL:
                etab_a(t5m, 5, [0, 1], EAs)
                etab_d(t5m, 5, [(g, 2 + g) for g in range(3)], EDs)

        def accum(acc_h, oh_b, sc, first, bufs_r, buf_w):
            if first:
                B.op("dve", lambda e: e.tensor_scalar(out=acc_h, in0=oh_b, scalar1=sc, scalar2=None, op0=ALU.mult),
                     reads=bufs_r, writes=[buf_w])
            else:
                B.op("dve", lambda e: e.scalar_tensor_tensor(out=acc_h, in0=oh_b, scalar=sc, in1=acc_h, op0=ALU.mult, op1=ALU.add),
                     reads=bufs_r, writes=[buf_w])

        def etab_a(tsrc, nct, cols, out_d):
            A.reset()
            nh = len(cols)
            tb = A.get(F32, 32 * nct)
            oh = A.get(F32, 32, 384)
            acc = A.get(F32, nh, 384)
            msk = A.get(F32, 384)
            bT, bO, bA, bM = Buf(), Buf(), Buf(), Buf()
            B.dma("sp", tb, tsrc.partition_broadcast(128), writes=[bT])
            B.dma("sp", oh, ohA, writes=[bO])
            B.dma("sp", msk, mA, writes=[bM])
            for h, col in enumerate(cols):
                for b_ in range(32):
                    accum(acc[:, h, :], oh[:, b_, :], tb[:, b_ * nct + col:b_ * nct + col + 1], b_ == 0, [bT, bO], bA)
            B.op("act", lambda e: e.activation(out=acc, in_=acc, func=AF.Exp), writes=[bA])
            for h in range(nh):
                B.op("dve", lambda e, h=h: e.tensor_tensor(out=acc[:, h, :], in0=acc[:, h, :], in1=msk, op=ALU.mult),
                     reads=[bM], writes=[bA])
            B.dma("sp", out_d.rearrange("h p f -> p h f"), acc, reads=[bA])
            B.barrier()

        def etab_d(tsrc, nct, gcols, out_d):
            A.reset()
            nh = len(gcols)
            tb = A.get(F32, 32 * nct)
            oh = A.get(F32, 3, 32, 192)
            acc = A.get(F32, nh, 192)
            msk = A.get(F32, 192)
            bT, bO, bA, bM = Buf(), Buf(), Buf(), Buf()
            B.dma("sp", tb, tsrc.partition_broadcast(128), writes=[bT])
            B.dma("sp", oh[0:64], ohD, writes=[bO])
            B.dma("sp", msk[0:64], mD, writes=[bM])
            for hh, (g, col) in enumerate(gcols):
                for b_ in range(32):
                    accum(acc[0:64, hh, :], oh[0:64, g, b_, :], tb[0:64, b_ * nct + col:b_ * nct + col + 1], b_ == 0, [bT, bO], bA)
            B.op("act", lambda e: e.activation(out=acc[0:64], in_=acc[0:64], func=AF.Exp), writes=[bA])
            for hh in range(nh):
                B.op("dve", lambda e, hh=hh: e.tensor_tensor(out=acc[0:64, hh, :], in0=acc[0:64, hh, :], in1=msk[0:64], op=ALU.mult),
                     reads=[bM], writes=[bA])
            B.dma("sp", out_d.rearrange("h p f -> p h f"), acc[0:64], reads=[bA])
            B.barrier()

        def etab_c(li):
            A.reset()
            NH = 150 if c.BAL else 120
            R = A.get(F32, NH)
            M = A.get(F32, 64, 64)
            acc = A.get(F32, NH, 64)
            msk = A.get(F32, 64)
            bR, bM, bA, bK = Buf(), Buf(), Buf(), Buf()
            B.dma("sp", R[0:31, 0:120], rpb[li], writes=[bR])
            if c.BAL:
                B.dma("sp", R[0:31, 120:150], rpbm[li], writes=[bR])
            B.dma("sp", M[0:31], ohC, writes=[bM])
            B.dma("sp", msk[0:64], mC, writes=[bK])
            for qc in range(64):
                bk = qc % 4
                B.op("pe", lambda e, qc=qc, bk=bk: e.matmul(banks[bk][0:64, 0:NH], M[0:31, qc, :], R[0:31, 0:NH], start=True, stop=True),
                     reads=[bR, bM], writes=[bankb[bk]])
                eng = "dve" if qc % 2 else "act"
                B.op(eng, lambda e, qc=qc, bk=bk, eng=eng: (e.tensor_copy(out=acc[0:64, :, qc], in_=banks[bk][0:64, 0:NH]) if eng == "dve"
                                                         else e.copy(out=acc[0:64, :, qc], in_=banks[bk][0:64, 0:NH])),
                     reads=[bankb[bk]], writes=[bA])
            B.op("act", lambda e: e.activation(out=acc[0:64], in_=acc[0:64], func=AF.Exp), writes=[bA])
            m64 = msk[0:64]
            mb = bass.AP(m64.tensor, m64.offset, [list(m64.ap[0]), [0, NH], [1, 64]])
            B.op("dve", lambda e: e.tensor_tensor(out=acc[0:64], in0=acc[0:64], in1=mb, op=ALU.mult), reads=[bK], writes=[bA])
            B.dma("sp", EC[li].rearrange("h d k q -> k (h d) q"), acc[0:64, 0:120, :], reads=[bA])
            if c.BAL:
                B.dma("sp", ECs[li].rearrange("h d k q -> k (h d) q"), acc[0:64, 120:150, :], reads=[bA])
            B.barrier()

        def rmsnorm(x_ap, xb, g_ap, xn_ap, xnb, sq2, sqb, pss, pssb, rs, rsb, width=T):
            G = min(2, KC)
            ng = KC // G
            for g in range(ng):
                sq = sq2[g % 2]
                B.op("act", lambda e, g=g, sq=sq: e.activation(out=sq[:, 0:G, 0:width], in_=x_ap[:, g * G:(g + 1) * G, :],
                                                               func=AF.Square), reads=[xb], writes=[sqb[g % 2]])

                def mm(e, g=g, sq=sq):
                    for j in range(G):
                        ins = e.matmul(pss[:, 0:width], onesD, sq[:, j, 0:width], start=(g == 0 and j == 0),
                                       stop=(g == ng - 1 and j == G - 1))
                    return ins
                B.op("pe", mm, reads=[sqb[g % 2]], writes=([pssb] if g in (0, ng - 1) else []))
            B.op("act", lambda e: e.activation(out=rs[:, 0:width], in_=pss[:, 0:width], func=AF.Sqrt, bias=EPS, scale=1.0),
                 reads=[pssb], writes=[rsb])
            B.op("dve", lambda e: e.reciprocal(out=rs[:, 0:width], in_=rs[:, 0:width]), writes=[rsb])
            for kc in range(KC):
                B.op("dve", lambda e, kc=kc: e.scalar_tensor_tensor(
                    out=xn_ap[:, kc, :], in0=x_ap[:, kc, :], scalar=g_ap[:, kc:kc + 1], in1=rs[:, 0:width],
                    op0=ALU.mult, op1=ALU.mult), reads=[xb, rsb], writes=[xnb])

        def xT_tile(s, t):
            src = xT[0] if s == 0 else xTs
            return src.rearrange("(kc p) t -> p kc t", p=128)[:, :, t * T:(t + 1) * T]

        def in_transpose():
            A.reset()
            xin = [A.get(F32, TT, D) for _ in range(2)]
            xo = [A.get(F32, KC, T) for _ in range(2)]
            xinb = [Buf(), Buf()]
            xob = [Buf(), Buf()]
            G = min(4, KC)
            it = 0
            for s, t in TILES:
                if True:
                    p = it % 2
                    srcx = x_in[0] if s == 0 else xs_in
                    B.dma("sp", xin[p], srcx[t * T:(t + 1) * T, :].rearrange("(tt p) d -> p tt d", p=128),
                          writes=[xinb[p]])
                    n = 0
                    for tt in range(TT):
                        for g in range(KC // G):
                            bk = n % 4
                            n += 1

                            def tr(e, tt=tt, g=g, bk=bk, p=p):
                                for j in range(G):
                                    kc = g * G + j
                                    ins = e.transpose(banks[bk][:, j * 128:(j + 1) * 128],
                                                      xin[p][:, tt, kc * 128:(kc + 1) * 128], ident)
                                return ins
                            B.op("pe", tr, reads=[xinb[p], cb], writes=[bankb[bk]])
                            eng = "dve" if n % 2 else "act"

                            def cp(e, tt=tt, g=g, bk=bk, p=p, eng=eng):
                                o = xo[p][:, g * G:(g + 1) * G, tt * 128:(tt + 1) * 128]
                                i_ = banks[bk][:, 0:G * 128].rearrange("p (a b) -> p a b", b=128)
                                return e.tensor_copy(out=o, in_=i_) if eng == "dve" else e.copy(out=o, in_=i_)
                            B.op(eng, cp, reads=[bankb[bk]], writes=[xob[p]])
                    B.dma("sp", xT_tile(s, t), xo[p], reads=[xob[p]])
                    it += 1
            B.barrier()

        def ffn(l, which):
            A.reset()
            xs = [A.get(F32, KC, T) for _ in range(2)]
            xn = A.get(BF16, KC, T)
            hT = A.get(BF16, FFC, T)
            sq2 = [A.get(BF16, min(2, KC), T) for _ in range(2)]
            rs = A.get(F32, T)
            sg = [A.get(F32, T) for _ in range(2)]
            slots = [A.get(BF16, c.SLOT) for _ in range(3)]
            xb = [Buf(), Buf()]
            xnb, hb, rsb = Buf(), Buf(), Buf()
            sqb = [Buf(), Buf()]
            sgb = [Buf(), Buf()]
            o_in, n_in, L_in = c.woff[l]["f%din" % which]
            o_out, n_out, L_out = c.woff[l]["f%dout" % which]
            entries = []
            tiles = list(TILES)
            for _ in tiles:
                entries += [(o_in + j * 128 * L_in, L_in) for j in range(n_in)]
                entries += [(o_out + j * 128 * L_out, L_out) for j in range(n_out)]
            ws = WStream(B, wbfs[l], slots, entries, wtok.get((l, "a" if which == 1 else "b")))
            g_ap = gcol((0 if which == 1 else 2) * c.DEPTH + l)
            pss, pssb = banks[6], bankb[6]
            e_i = 0

            def do_norm(i):
                rmsnorm(xs[i % 2], xb[i % 2], g_ap, xn, xnb, sq2, sqb, pss, pssb, rs, rsb)

            B.dma("sp", xs[0], xT_tile(*tiles[0]), writes=[xb[0]])
            do_norm(0)
            for i, (s, t) in enumerate(tiles):
                p = i % 2
                if i + 1 < len(tiles):
                    B.dma("sp", xs[1 - p], xT_tile(*tiles[i + 1]), writes=[xb[1 - p]])
                for jj in range(FFC):
                    wbuf, wsl = ws.get(e_i)
                    e_i += 1
                    w3 = wsl[:, 0:L_in].rearrange("p (k n) -> p k n", n=256)
                    par = jj % 2
                    bg, bu = banks[par * 2], banks[par * 2 + 1]

                    def mm(e, w3=w3, bg=bg, bu=bu):
                        for kc in range(KC):
                            e.matmul(bg[:, 0:T], w3[:, kc, 0:128], xn[:, kc, :], start=(kc == 0), stop=(kc == KC - 1))
                        for kc in range(KC):
                            ins = e.matmul(bu[:, 0:T], w3[:, kc, 128:256], xn[:, kc, :], start=(kc == 0), stop=(kc == KC - 1))
                        return ins
                    B.op("pe", mm, reads=[wbuf, xnb], writes=[bankb[par * 2], bankb[par * 2 + 1]])
                    B.op("act", lambda e, bg=bg, par=par: e.activation(out=sg[par], in_=bg[:, 0:T], func=AF.Silu),
                         reads=[bankb[par * 2]], writes=[sgb[par]])
                    B.op("dve", lambda e, bu=bu, par=par, jj=jj: e.tensor_tensor(out=hT[:, jj, :], in0=sg[par], in1=bu[:, 0:T],
                                                                                 op=ALU.mult),
                         reads=[sgb[par], bankb[par * 2 + 1]], writes=[hb])
                if i + 1 < len(tiles):
                    do_norm(i + 1)
                for oc in range(KC):
                    wbuf, wsl = ws.get(e_i)
                    e_i += 1
                    w3 = wsl[:, 0:L_out].rearrange("p (k n) -> p k n", n=128)
                    by, byb = banks[4 + oc % 2], bankb[4 + oc % 2]

                    def mm2(e, w3=w3, by=by):
                        for kc in range(FFC):
                            ins = e.matmul(by[:, 0:T], w3[:, kc, :], hT[:, kc, :], start=(kc == 0), stop=(kc == FFC - 1))
                        return ins
                    B.op("pe", mm2, reads=[wbuf, hb], writes=[byb])
                    B.op("dve", lambda e, by=by, oc=oc, p=p: e.scalar_tensor_tensor(
                        out=xs[p][:, oc, :], in0=by[:, 0:T], scalar=0.5, in1=xs[p][:, oc, :], op0=ALU.mult, op1=ALU.add),
                        reads=[byb], writes=[xb[p]])
                B.dma("pool", xT_tile(s, t), xs[p], reads=[xb[p]])
                pc_drip(l + 1, 3)
            if which == 2:
                pc_drip(l + 1, 10 ** 9)
            B.barrier()

        def in_proj(l, samp=False):
            even = (l % 2 == 0)
            i2 = l // 2
            A.reset()
            nqk, nv = (20, 4) if even else (26, 6)
            if samp:
                nqk, nv = (6, 2) if even else (7, 2)
            if even:
                plan = [("f", j, j, None) for j in range(10)] + [("v", 10 + j, j, None) for j in range(2)] + \
                       [("r", 12 + j, 10 + j, 2 * i2) for j in range(8)] + [("r", 20 + j, 18 + j, 2 * i2 + 1) for j in range(2)] + \
                       [("v", 22 + j, 2 + j, None) for j in range(2)]
            else:
                plan = [("f", j, j, None) for j in range(10)] + [("v", 10 + j, j, None) for j in range(2)] + \
                       [("f", 12 + j, 10 + j, None) for j in range(16)] + [("v", 28 + j, 2 + j, None) for j in range(4)]
            if samp:
                if even:
                    plan = [("f", 0, 0, None), ("f", 1, 1, None), ("f", 2, 2, None), ("r", 3, 3, 2 * i2), ("r", 4, 4, 2 * i2),
                            ("r", 5, 5, 2 * i2 + 1), ("v", 6, 0, None), ("v", 7, 1, None)]
                else:
                    plan = [("f", j, j, None) for j in range(7)] + [("v", 7, 0, None), ("v", 8, 1, None)]
            xs1 = A.get(F32, KC, T)
            xs = [xs1, xs1]
            xn = A.get(BF16, KC, T)
            sq2 = [A.get(BF16, min(2, KC), T) for _ in range(2)]
            rs = A.get(F32, T)
            qkt = [A.get(BF16, nqk, T) for _ in range(2)]
            vtt = [A.get(BF16, TT, nv * 128) for _ in range(2)]
            slots = [A.get(BF16, KC * 128) for _ in range(4)]
            if even:
                cs = [[A.get(F32, T) for _ in range(2)] for _ in range(2)]
                tmp = [[A.get(BF16, T)] + [A.get(F32, T) for _ in range(4)] for _ in range(2)]
            xb1 = Buf()
            xb, qkb, vtb, csb = [xb1, xb1], [Buf(), Buf()], [Buf(), Buf()], [Buf(), Buf()]
            tmpb = [[Buf() for _ in range(5)] for _ in range(2)]
            xnb, rsb = Buf(), Buf()
            sqb = [Buf(), Buf()]
            o_w, n_w, L = (c.woff_s if samp else c.woff)[l]["min"]
            tiles = [(2, t) for t in range(NT)] if samp else [tl for tl in TILES if tl[0] == 0]

            def xsrc(s, t):
                if s != 2:
                    return xT_tile(s, t)
                qd_, col = (t * T) // SQ, (t * T) % SQ
                return xg.rearrange("k (r h p) t -> r p k h t", r=4, p=128)[qd_][:, :, :, col:col + T]
            entries = []
            for _ in tiles:
                entries += [(o_w + pl[1] * 128 * L, L) for pl in plan]
            ws = WStream(B, wbf_s if samp else wbfs[l], slots, entries, wtok.get("s" if samp else (l, "b")))
            g_ap = gcol(c.DEPTH + l)
            pss, pssb = banks[7], bankb[7]
            e_i = 0
            dq = []

            def tick(flush_all=False):
                keep = []
                for ent in dq:
                    ent[0] -= 1
                    if ent[0] <= 0 or flush_all:
                        for a_, k_ in ent[1]:
                            B.op(*a_, **k_)
                    else:
                        keep.append(ent)
                dq[:] = keep

            def xload(j, s_, t_):
                if s_ != 2:
                    B.dma("sp", xs[j], xT_tile(s_, t_), writes=[xb[j]])
                    return
                qd_, col = (t_ * T) // SQ, (t_ * T) % SQ
                hh_ = GP // 128
                for k in range(NGP):
                    B.dma("sp", xs[j][:, k * hh_:(k + 1) * hh_, :],
                          xg[k, qd_ * GP:(qd_ + 1) * GP, col:col + T].rearrange("(h p) t -> p h t", p=128), writes=[xb[j]])
            xload(0, *tiles[0])
            for i, (s, t) in enumerate(tiles):
                p = i % 2
                if even:
                    B.dma("sp", cs[p][0], cosT[:, t * T:(t + 1) * T], writes=[csb[p]])
                    B.dma("sp", cs[p][1], sinT[:, t * T:(t + 1) * T], writes=[csb[p]])
                rmsnorm(xs[p], xb[p], g_ap, xn, xnb, sq2, sqb, pss, pssb, rs, rsb)
                if i + 1 < len(tiles):
                    xload(1 - p, *tiles[i + 1])
                nf = 0
                nvv = 0
                nr = 0
                for kind, wc, oi, gi in plan:
                    tick()
                    wbuf, wsl = ws.get(e_i)
                    e_i += 1
                    w3 = wsl[:, 0:L].rearrange("p (k n) -> p k n", n=128)
                    if kind in ("f", "r"):
                        bk = nf % 3
                        nf += 1
                        bank = banks[bk]

                        def mm(e, w3=w3, bank=bank):
                            for kc in range(KC):
                                ins = e.matmul(bank[:, 0:T], w3[:, kc, :], xn[:, kc, :], start=(kc == 0), stop=(kc == KC - 1))
                            return ins
                        B.op("pe", mm, reads=[wbuf, xnb], writes=[bankb[bk]])
                        if kind == "f":
                            B.op("act", lambda e, bank=bank, oi=oi, p=p: e.copy(out=qkt[p][:, oi, :], in_=bank[:, 0:T]),
                                 reads=[bankb[bk]], writes=[qkb[p]])
                        else:
                            q2 = nr % 2
                            nr += 1
                            sqh, rr, qn, t1, t2 = tmp[q2]
                            bsq, brr, bqn, bt1, bt2 = tmpb[q2]
                            B.op("act", lambda e, bank=bank, sqh=sqh: e.activation(out=sqh, in_=bank[:, 0:T], func=AF.Square),
                                 reads=[bankb[bk]], writes=[bsq])
                            la = []
                            lb = []
                            la.append((("pe", lambda e, sqh=sqh: e.matmul(banks[4][:, 0:T], ones128, sqh, start=True, stop=True)),
                                       dict(reads=[bsq, cb], writes=[bankb[4]])))
                            la.append((("act", lambda e, rr=rr: e.activation(out=rr, in_=banks[4][:, 0:T], func=AF.Sqrt, bias=EPS, scale=1.0)),
                                       dict(reads=[bankb[4]], writes=[brr])))
                            la.append((("dve", lambda e, rr=rr: e.reciprocal(out=rr, in_=rr)), dict(writes=[brr])))
                            la.append((("dve", lambda e, bank=bank, qn=qn, rr=rr, gi=gi: e.scalar_tensor_tensor(
                                out=qn, in0=bank[:, 0:T], scalar=qkg_sb[:, gi:gi + 1], in1=rr, op0=ALU.mult, op1=ALU.mult)),
                                dict(reads=[bankb[bk], brr, cb], writes=[bqn])))
                            lb.append((("pe", lambda e, qn=qn: e.matmul(banks[5][:, 0:T], pi_sb, qn, start=True, stop=True)),
                                       dict(reads=[bqn, cb], writes=[bankb[5]])))
                            lb.append((("dve", lambda e, qn=qn, t1=t1, p=p: e.tensor_tensor(out=t1, in0=qn, in1=cs[p][0], op=ALU.mult)),
                                       dict(reads=[bqn, csb[p]], writes=[bt1])))
                            lb.append((("dve", lambda e, t2=t2, p=p: e.tensor_tensor(out=t2, in0=banks[5][:, 0:T], in1=cs[p][1], op=ALU.mult)),
                                       dict(reads=[bankb[5], csb[p]], writes=[bt2])))
                            lb.append((("pool", lambda e, t1=t1, t2=t2, oi=oi, p=p: e.tensor_tensor(out=qkt[p][:, oi, :], in0=t1, in1=t2, op=ALU.add)),
                                       dict(reads=[bt1, bt2], writes=[qkb[p]])))
                            dq.append([2, la])
                            dq.append([3, lb])
                    else:
                        bk = (3, 6)[nvv % 2]
                        nvv += 1
                        bank = banks[bk]

                        def mmv(e, w3=w3, bank=bank):
                            for tt in range(TT):
                                for kc in range(KC):
                                    ins = e.matmul(bank[:, tt * 128:(tt + 1) * 128], xn[:, kc, tt * 128:(tt + 1) * 128], w3[:, kc, :],
                                                   start=(kc == 0), stop=(kc == KC - 1))
                            return ins
                        B.op("pe", mmv, reads=[wbuf, xnb], writes=[bankb[bk]])
                        B.op("dve", lambda e, bank=bank, oi=oi, p=p: e.tensor_copy(
                            out=vtt[p][:, :, oi * 128:(oi + 1) * 128], in_=bank[:, 0:T].rearrange("p (a b) -> p a b", b=128)),
                            reads=[bankb[bk]], writes=[vtb[p]])
                tick(True)
                if samp:
                    qk_d, vt_d = qks, vts
                    t0 = t * T
                else:
                    assert s == 0 or not c.BAL or True
                    qk_d, vt_d = qk[0], vt[0]
                    t0 = t * T
                if (not samp) and s == 1:
                    continue
                B.dma("pool", qk_d[0:nqk * 128, t0:t0 + T].rearrange("(c p) t -> p c t", p=128), qkt[p], reads=[qkb[p]])
                B.dma("pool", vt_d[t0:t0 + T, 0:nv * 128].rearrange("(tt p) f -> p tt f", p=128), vtt[p], reads=[vtb[p]])
            B.barrier()

        def load_v(dst, vt_a, col, buf):
            step = max(1, NB128 // 4)
            for a in range(0, NB128, step):
                b = min(NB128, a + step)
                B.dma("sp", dst[:, a:b, :], vt_a[a * 128:b * 128, col * 128:(col + 1) * 128].rearrange("(c p) d -> p c d", p=128),
                      writes=[buf])

        def attn_a(l):
            i2 = l // 2
            A.reset()
            E = A.get(F32, 10, 384)
            kT = A.get(BF16, S)
            V = A.get(BF16, NB128, 128)
            qT = [A.get(BF16, S) for _ in range(2)]
            ot = [A.get(BF16, S) for _ in range(2)]
            pex = [A.get(F32, 384) for _ in range(2)]
            pbf = [A.get(BF16, 384) for _ in range(2)]
            rl = [A.get(F32, 128) for _ in range(2)]
            Eb, kb, vb = Buf(), Buf(), Buf()
            qb, ob, pexb, pbfb, rlb = [Buf(), Buf()], [Buf(), Buf()], [Buf(), Buf()], [Buf(), Buf()], [Buf(), Buf()]
            B.dma("sp", E[:, 0:8, :], EA.rearrange("h p f -> p h f"), writes=[Eb])
            SK = c.n_even * 8
            jobs = [(qk[0], vt[0], oT[0], [(8 + kvh, kvh, [(kvh * 4 + g, kvh * 4 + g, i2 * 8 + kvh * 4 + g, kvh * 4 + g) for g in range(4)])
                                           for kvh in range(2)])]
            if c.BAL:
                B.dma("sp", E[:, 8:10, :], EAs.rearrange("h p f -> p h f"), writes=[Eb])
                jobs.append((qks, vts, oTs, [(2, 0, [(0, 8, SK + i2 * 2, 0), (1, 9, SK + i2 * 2 + 1, 1)])]))
            pend = []

            def flush():
                for a_, k_ in pend:
                    B.op(*a_, **k_)
                pend.clear()
            hh = 0
            n = 0
            for qk_a, vt_a, oT_a, groups in jobs:
                for kch, vcol, heads in groups:
                    B.dma("sp", kT, qk_a[kch * 128:(kch + 1) * 128, :], writes=[kb])
                    load_v(V, vt_a, vcol, vb)
                    for qch, h, skc, och in heads:
                        hp = hh % 2
                        hh += 1
                        B.dma("sp", qT[hp], qk_a[qch * 128:(qch + 1) * 128, :], writes=[qb[hp]])
                        for blk in range(NB128):
                            cs_ = [cc for cc in (blk - 1, blk, blk + 1) if 0 <= cc < NB128]
                            c0 = cs_[0] - (blk - 1)
                            lo, hi = c0 * 128, (c0 + len(cs_)) * 128
                            sb_i = n % 3
                            o_i = 3 + n % 2
                            pp = n % 2
                            n += 1
                            sbank, obank = banks[sb_i], banks[o_i]

                            def mm(e, cs_=cs_, c0=c0, sbank=sbank, hp=hp, blk=blk):
                                for ci, cch in enumerate(cs_):
                                    ins = e.matmul(sbank[:, (c0 + ci) * 128:(c0 + ci + 1) * 128], kT[:, cch * 128:(cch + 1) * 128],
                                                   qT[hp][:, blk * 128:(blk + 1) * 128], start=True, stop=True)
                                return ins
                            B.op("pe", mm, reads=[kb, qb[hp]], writes=[bankb[sb_i]])
                            B.op("act", lambda e, sbank=sbank, pp=pp, lo=lo, hi=hi: e.activation(
                                out=pex[pp][:, lo:hi], in_=sbank[:, lo:hi], func=AF.Exp, scale=SCALE),
                                reads=[bankb[sb_i]], writes=[pexb[pp]])
                            B.op("dve", lambda e, pp=pp, lo=lo, hi=hi, h=h: e.tensor_tensor(
                                out=pbf[pp][:, lo:hi], in0=pex[pp][:, lo:hi], in1=E[:, h, lo:hi], op=ALU.mult),
                                reads=[pexb[pp], Eb], writes=[pbfb[pp]])

                            def pv(e, cs_=cs_, c0=c0, obank=obank, pp=pp):
                                nl = len(cs_)
                                for ci, cch in enumerate(cs_):
                                    e.matmul(obank[:, 0:128], V[:, cch, :], pbf[pp][:, (c0 + ci) * 128:(c0 + ci + 1) * 128],
                                             start=(ci == 0), stop=(ci == nl - 1))
                                for ci, cch in enumerate(cs_):
                                    ins = e.matmul(obank[:, 128:256], ones_bf, pbf[pp][:, (c0 + ci) * 128:(c0 + ci + 1) * 128],
                                                   start=(ci == 0), stop=(ci == nl - 1))
                                return ins
                            flush()
                            pend.append((("pe", pv), dict(reads=[vb, pbfb[pp], cb], writes=[bankb[o_i]])))
                            sk = esink[:, skc:skc + 1]
                            pend.append((("dve", lambda e, obank=obank, pp=pp, sk=sk: e.tensor_scalar(
                                out=rl[pp], in0=obank[:, 128:256], scalar1=sk, scalar2=None, op0=ALU.add)),
                                dict(reads=[bankb[o_i], cb], writes=[rlb[pp]])))
                            pend.append((("dve", lambda e, pp=pp: e.reciprocal(out=rl[pp], in_=rl[pp])), dict(writes=[rlb[pp]])))
                            pend.append((("dve", lambda e, obank=obank, pp=pp, hp=hp, blk=blk: e.tensor_tensor(
                                out=ot[hp][:, blk * 128:(blk + 1) * 128], in0=obank[:, 0:128], in1=rl[pp], op=ALU.mult)),
                                dict(reads=[bankb[o_i], rlb[pp]], writes=[ob[hp]])))
                        flush()
                        B.dma("pool", oT_a[och * 128:(och + 1) * 128, :], ot[hp], reads=[ob[hp]])
            B.barrier()

        def attn_b(l):
            A.reset()
            kT = A.get(BF16, S)
            V = A.get(BF16, NB128, 128)
            qT = [A.get(BF16, S) for _ in range(2)]
            ot = [A.get(BF16, S) for _ in range(2)]
            pbf = [A.get(BF16, 512) for _ in range(3)]
            rl = [A.get(F32, 512) for _ in range(2)]
            kb, vb = Buf(), Buf()
            qb, ob, rlb = [Buf(), Buf()], [Buf(), Buf()], [Buf(), Buf()]
            pbfb = [Buf(), Buf(), Buf()]
            hh = 0
            nq = 0
            QT = 512
            jobs = [(qk[0], vt[0], oT[0], [(18 + kvh, 2 + kvh, [(10 + kvh * 4 + g, 8 + kvh * 4 + g) for g in range(4)]) for kvh in range(2)])]
            if c.BAL:
                jobs.append((qks, vts, oTs, [(5, 1, [(3, 2), (4, 3)])]))
            for qk_a, vt_a, oT_a, groups in jobs:
                for kch, vcol, heads in groups:
                    B.dma("sp", kT, qk_a[kch * 128:(kch + 1) * 128, :], writes=[kb])
                    load_v(V, vt_a, vcol, vb)
                    for qch, och in heads:
                        hp = hh % 2
                        hh += 1
                        B.dma("sp", qT[hp], qk_a[qch * 128:(qch + 1) * 128, :], writes=[qb[hp]])
                        for qt in range(S // QT):
                            o_i, l_i = 4 + nq % 2, 6 + nq % 2
                            rp = nq % 2
                            nq += 1
                            obank, lbank = banks[o_i], banks[l_i]
                            qsl = qT[hp][:, qt * QT:(qt + 1) * QT]

                            def emit_s(kc, qsl=qsl):
                                sb_i = kc % 3
                                B.op("pe", lambda e, kc=kc, sb_i=sb_i, qsl=qsl: e.matmul(
                                    banks[sb_i][:, 0:QT], kT[:, kc * 128:(kc + 1) * 128], qsl, start=True, stop=True),
                                    reads=[kb, qb[hp]], writes=[bankb[sb_i]])
                            emit_s(0)
                            for kc in range(NB128):
                                if kc + 1 < NB128:
                                    emit_s(kc + 1)
                                sb_i = kc % 3
                                B.op("act", lambda e, sb_i=sb_i: e.activation(out=pbf[sb_i], in_=banks[sb_i][:, 0:QT], func=AF.Exp, scale=SCALE),
                                     reads=[bankb[sb_i]], writes=[pbfb[sb_i]])

                                def pv(e, kc=kc, sb_i=sb_i, obank=obank, lbank=lbank):
                                    e.matmul(obank[:, 0:QT], V[:, kc, :], pbf[sb_i], start=(kc == 0), stop=(kc == NB128 - 1))
                                    return e.matmul(lbank[:, 0:QT], ones_bf, pbf[sb_i], start=(kc == 0), stop=(kc == NB128 - 1))
                                edge = kc in (0, NB128 - 1)
                                B.op("pe", pv, reads=[vb, pbfb[sb_i], cb], writes=([bankb[o_i], bankb[l_i]] if edge else []))
                            B.op("dve", lambda e, lbank=lbank, rp=rp: e.reciprocal(out=rl[rp], in_=lbank[:, 0:QT]),
                                 reads=[bankb[l_i]], writes=[rlb[rp]])
                            B.op("dve", lambda e, obank=obank, rp=rp, hp=hp, qt=qt: e.tensor_tensor(
                                out=ot[hp][:, qt * QT:(qt + 1) * QT], in0=obank[:, 0:QT], in1=rl[rp], op=ALU.mult),
                                reads=[bankb[o_i], rlb[rp]], writes=[ob[hp]])
                        B.dma("pool", oT_a[och * 128:(och + 1) * 128, :], ot[hp], reads=[ob[hp]])
            B.barrier()

        def attn_c(l):
            li = l // 2
            A.reset()
            PT = [A.get(F32, 14, 64) for _ in range(2)]
            kT = A.get(BF16, S)
            V0 = A.get(BF16, NB128, 128)
            V1 = A.get(BF16, NB128, 128)
            qT = [A.get(BF16, S) for _ in range(2)]
            ot = [A.get(BF16, S) for _ in range(2)]
            pex = [A.get(F32, 256) for _ in range(2)]
            pbf = [A.get(BF16, 256) for _ in range(2)]
            rl = [A.get(F32, 64) for _ in range(2)]
            kb, vb = Buf(), Buf()
            ptb, qb, ob, pexb, pbfb, rlb = ([Buf(), Buf()] for _ in range(6))
            pend = []

            def flush():
                for a_, k_ in pend:
                    B.op(*a_, **k_)
                pend.clear()
            hh = 0
            n = 0
            jobs = [(qk[0], vt[0], oT[0], [(8 + kvh, kvh, [(kvh * 4 + g, EC[li, kvh * 4 + g], kvh * 4 + g) for g in range(4)]) for kvh in range(2)])]
            if c.BAL:
                jobs.append((qks, vts, oTs, [(2, 0, [(0, ECs[li, 0], 0), (1, ECs[li, 1], 1)])]))
            for qk_a, vt_a, oT_a, groups in jobs:
                for kch, vcol, heads in groups:
                    B.dma("sp", kT, qk_a[kch * 128:(kch + 1) * 128, :], writes=[kb])
                    load_v(V0, vt_a, vcol, vb)
                    B.dma("sp", V1[:, 0:NB128 - 1, :],
                          vt_a[64:64 + (NB128 - 1) * 128, vcol * 128:(vcol + 1) * 128].rearrange("(c p) d -> p c d", p=128), writes=[vb])
                    for qch, ec_a, och in heads:
                        hp = hh % 2
                        hh += 1
                        B.dma("sp", qT[hp], qk_a[qch * 128:(qch + 1) * 128, :], writes=[qb[hp]])
                        B.dma("sp", PT[hp][0:64], ec_a[0:14].rearrange("d k q -> k d q"), writes=[ptb[hp]])
                        B.dma("sp", PT[hp][64:128], ec_a[1:15].rearrange("d k q -> k d q"), writes=[ptb[hp]])
                        for r in range(ROWS):
                            rs_ = min(max(r - 4, 0), ROWS - 8)
                            d0 = rs_ - r + 7
                            sb_i = n % 3
                            o_i = 3 + n % 2
                            pp = n % 2
                            n += 1
                            sbank, obank = banks[sb_i], banks[o_i]

                            def mm(e, rs_=rs_, r=r, sbank=sbank, hp=hp):
                                for m in range(4):
                                    k0 = rs_ * 64 + m * 128
                                    ins = e.matmul(sbank[:, m * 64:(m + 1) * 64], kT[:, k0:k0 + 128], qT[hp][:, r * 64:(r + 1) * 64],
                                                   start=True, stop=True)
                                return ins
                            B.op("pe", mm, reads=[kb, qb[hp]], writes=[bankb[sb_i]])
                            B.op("act", lambda e, sbank=sbank, pp=pp: e.activation(out=pex[pp], in_=sbank[:, 0:256], func=AF.Exp, scale=SCALE),
                                 reads=[bankb[sb_i]], writes=[pexb[pp]])
                            B.op("dve", lambda e, pp=pp, hp=hp, d0=d0: e.tensor_tensor(
                                out=pbf[pp].rearrange("p (a b) -> p a b", b=64), in0=pex[pp].rearrange("p (a b) -> p a b", b=64),
                                in1=PT[hp][:, d0:d0 + 7:2, :], op=ALU.mult), reads=[pexb[pp], ptb[hp]], writes=[pbfb[pp]])

                            def pv(e, rs_=rs_, obank=obank, pp=pp):
                                for m in range(4):
                                    vv = V0[:, rs_ // 2 + m, :] if rs_ % 2 == 0 else V1[:, (rs_ - 1) // 2 + m, :]
                                    e.matmul(obank[:, 0:64], vv, pbf[pp][:, m * 64:(m + 1) * 64], start=(m == 0), stop=(m == 3))
                                for m in range(4):
                                    ins = e.matmul(obank[:, 64:128], ones_bf, pbf[pp][:, m * 64:(m + 1) * 64], start=(m == 0), stop=(m == 3))
                                return ins
                            flush()
                            pend.append((("pe", pv), dict(reads=[vb, pbfb[pp], cb], writes=[bankb[o_i]])))
                            pend.append((("dve", lambda e, obank=obank, pp=pp: e.reciprocal(out=rl[pp], in_=obank[:, 64:128])),
                                         dict(reads=[bankb[o_i]], writes=[rlb[pp]])))
                            pend.append((("dve", lambda e, obank=obank, pp=pp, hp=hp, r=r: e.tensor_tensor(
                                out=ot[hp][:, r * 64:(r + 1) * 64], in0=obank[:, 0:64], in1=rl[pp], op=ALU.mult)),
                                dict(reads=[bankb[o_i], rlb[pp]], writes=[ob[hp]])))
                        flush()
                        B.dma("pool", oT_a[och * 128:(och + 1) * 128, :], ot[hp], reads=[ob[hp]])
            B.barrier()

        def attn_d(l):
            A.reset()
            E = A.get(F32, 15, 192)
            kT = A.get(BF16, S)
            qT = [A.get(BF16, S) for _ in range(3)]
            Vg = [A.get(BF16, S // 64, 128) for _ in range(3)]
            ol = A.get(F32, 2, S)
            ot = A.get(BF16, S)
            pex = [A.get(F32, 192) for _ in range(2)]
            pbf = [A.get(BF16, 192) for _ in range(2)]
            Eb, kb, qb, vb, olb, ob = Buf(), Buf(), Buf(), Buf(), Buf(), Buf()
            pexb, pbfb = [Buf(), Buf()], [Buf(), Buf()]
            B.dma("sp", E[0:64, 0:12, :], ED.rearrange("h p f -> p h f"), writes=[Eb])
            pend = []

            def flush():
                for a_, k_ in pend:
                    B.op(*a_, **k_)
                pend.clear()
            n = 0
            jobs = [(qk[0], vt[0], oT[0], [(22 + slot, [10 + g * 4 + slot for g in range(3)], 2 + slot, [g * 4 + slot for g in range(3)], 8 + slot)
                                           for slot in range(4)])]
            if c.BAL:
                B.dma("sp", E[0:64, 12:15, :], EDs.rearrange("h p f -> p h f"), writes=[Eb])
                jobs.append((qks, vts, oTs, [(6, [3, 4, 5], 1, [12, 13, 14], 2)]))
            for qk_a, vt_a, oT_a, slots_ in jobs:
                for kch, qchs, vcol, eidx, och in slots_:
                    B.dma("sp", kT, qk_a[kch * 128:(kch + 1) * 128, :], writes=[kb])
                    for g, dil in enumerate((1, 4, 16)):
                        B.dma("sp", qT[g], qk_a[qchs[g] * 128:(qchs[g] + 1) * 128, :], writes=[qb])
                        nb = S // dil // 64
                        for rho in range(dil):
                            B.dma("sp", Vg[g][0:64, rho * nb:(rho + 1) * nb, :],
                                  vt_a[rho:S:dil, vcol * 128:(vcol + 1) * 128].rearrange("(c k) d -> k c d", k=64), writes=[vb])
                    for g, dil in enumerate((1, 4, 16)):
                        nb = S // dil // 64
                        hh = eidx[g]
                        for rho in range(dil):
                            for i in range(nb):
                                cs_ = [cc for cc in (i - 1, i, i + 1) if 0 <= cc < nb]
                                c0 = cs_[0] - (i - 1)
                                lo, hi = c0 * 64, (c0 + len(cs_)) * 64
                                sb_i = n % 3
                                o_i = 3 + n % 2
                                pp = n % 2
                                n += 1
                                sbank, obank = banks[sb_i], banks[o_i]

                                def tsl(blk, dil=dil, rho=rho):
                                    a = blk * 64 * dil + rho
                                    return slice(a, a + 63 * dil + 1, dil) if dil > 1 else slice(a, a + 64)
                                qs = tsl(i)

                                def mm(e, cs_=cs_, c0=c0, sbank=sbank, g=g, qs=qs, tsl=tsl):
                                    for ci, cch in enumerate(cs_):
                                        ins = e.matmul(sbank[0:64, (c0 + ci) * 64:(c0 + ci + 1) * 64], kT[:, tsl(cch)], qT[g][:, qs],
                                                       start=True, stop=True)
                                    return ins
                                B.op("pe", mm, reads=[kb, qb], writes=[bankb[sb_i]])
                                B.op("act", lambda e, sbank=sbank, pp=pp, lo=lo, hi=hi: e.activation(
                                    out=pex[pp][0:64, lo:hi], in_=sbank[0:64, lo:hi], func=AF.Exp, scale=SCALE),
                                    reads=[bankb[sb_i]], writes=[pexb[pp]])
                                B.op("dve", lambda e, pp=pp, lo=lo, hi=hi, hh=hh: e.tensor_tensor(
                                    out=pbf[pp][0:64, lo:hi], in0=pex[pp][0:64, lo:hi], in1=E[0:64, hh, lo:hi], op=ALU.mult),
                                    reads=[pexb[pp], Eb], writes=[pbfb[pp]])

                                def pv(e, cs_=cs_, c0=c0, obank=obank, pp=pp, g=g, rho=rho, nb=nb):
                                    nl = len(cs_)
                                    for ci, cch in enumerate(cs_):
                                        e.matmul(obank[:, 0:64], Vg[g][0:64, rho * nb + cch, :], pbf[pp][0:64, (c0 + ci) * 64:(c0 + ci + 1) * 64],
                                                 start=(ci == 0), stop=(ci == nl - 1))
                                    for ci, cch in enumerate(cs_):
                                        ins = e.matmul(obank[:, 64:128], ones_bf[0:64, :], pbf[pp][0:64, (c0 + ci) * 64:(c0 + ci + 1) * 64],
                                                       start=(ci == 0), stop=(ci == nl - 1))
                                    return ins
                                flush()
                                pend.append((("pe", pv), dict(reads=[vb, pbfb[pp], cb], writes=[bankb[o_i]])))
                                src = obank[:, 0:128].rearrange("p (a b) -> p a b", b=64)
                                if g == 0:
                                    pend.append((("act", lambda e, src=src, qs=qs: e.copy(out=ol[:, :, qs], in_=src)),
                                                 dict(reads=[bankb[o_i]], writes=[olb])))
                                else:
                                    pend.append((("dve", lambda e, src=src, qs=qs: e.tensor_tensor(out=ol[:, :, qs], in0=src, in1=ol[:, :, qs], op=ALU.add)),
                                                 dict(reads=[bankb[o_i]], writes=[olb])))
                    flush()
                    B.op("dve", lambda e: e.reciprocal(out=ol[:, 1, :], in_=ol[:, 1, :]), writes=[olb])
                    B.op("dve", lambda e: e.tensor_tensor(out=ot, in0=ol[:, 0, :], in1=ol[:, 1, :], op=ALU.mult), reads=[olb], writes=[ob])
                    B.dma("pool", oT_a[och * 128:(och + 1) * 128, :], ot, reads=[ob])
            B.barrier()

        def out_proj(l):
            OKC = 16 if l % 2 == 0 else 12
            A.reset()
            xs = [A.get(F32, KC, T) for _ in range(2)]
            ots = [A.get(BF16, OKC, T) for _ in range(2)]
            slots = [A.get(BF16, c.SLOT) for _ in range(3)]
            xb, otb = [Buf(), Buf()], [Buf(), Buf()]
            o_w, n_w, L = c.woff[l]["mout"]
            tiles = [tl for tl in TILES if tl[0] == 0]
            entries = []
            for _ in tiles:
                entries += [(o_w + j * 128 * L, L) for j in range(n_w)]
            ws = WStream(B, wbfs[l], slots, entries, wtok.get((l, "b")))
            e_i = 0

            def ld(i):
                s, t = tiles[i]
                B.dma("sp", xs[i % 2], xT_tile(s, t), writes=[xb[i % 2]])
                B.dma("sp", ots[i % 2], oT[s, 0:OKC * 128, t * T:(t + 1) * T].rearrange("(c p) t -> p c t", p=128), writes=[otb[i % 2]])
            ld(0)
            for i, (s, t) in enumerate(tiles):
                p = i % 2
                if i + 1 < len(tiles):
                    ld(i + 1)
                for oc in range(KC):
                    wbuf, wsl = ws.get(e_i)
                    e_i += 1
                    w3 = wsl[:, 0:L].rearrange("p (k n) -> p k n", n=128)
                    by, byb = banks[oc % 2], bankb[oc % 2]

                    def mm(e, w3=w3, by=by, p=p):
                        for kc in range(OKC):
                            ins = e.matmul(by[:, 0:T], w3[:, kc, :], ots[p][:, kc, :], start=(kc == 0), stop=(kc == OKC - 1))
                        return ins
                    B.op("pe", mm, reads=[wbuf, otb[p]], writes=[byb])
                    B.op("dve", lambda e, by=by, oc=oc, p=p: e.tensor_tensor(out=xs[p][:, oc, :], in0=by[:, 0:T], in1=xs[p][:, oc, :], op=ALU.add),
                         reads=[byb], writes=[xb[p]])
                B.dma("pool", xT_tile(s, t), xs[p], reads=[xb[p]])
            B.barrier()

        def out_proj_s(l):
            OKC = 4 if l % 2 == 0 else 3
            A.reset()
            ots = [A.get(BF16, OKC, T) for _ in range(2)]
            yt = [A.get(F32, KC, T) for _ in range(2)]
            slots = [A.get(BF16, OKC * 128) for _ in range(4)]
            otb, ytb = [Buf(), Buf()], [Buf(), Buf()]
            o_w, n_w, L = c.woff_s[l]["mout"]
            entries = []
            for _ in range(NT):
                entries += [(o_w + j * 128 * L, L) for j in range(n_w)]
            ws = WStream(B, wbf_s, slots, entries, wtok.get("s"))
            e_i = 0

            def ld(t):
                B.dma("sp", ots[t % 2], oTs[0:OKC * 128, t * T:(t + 1) * T].rearrange("(c p) t -> p c t", p=128), writes=[otb[t % 2]])
            ld(0)
            for t in range(NT):
                p = t % 2
                if t + 1 < NT:
                    ld(t + 1)
                for oc in range(KC):
                    wbuf, wsl = ws.get(e_i)
                    e_i += 1
                    w3 = wsl[:, 0:L].rearrange("p (k n) -> p k n", n=128)
                    by, byb = banks[oc % 2], bankb[oc % 2]

                    def mm(e, w3=w3, by=by, p=p):
                        for kc in range(OKC):
                            ins = e.matmul(by[:, 0:T], w3[:, kc, :], ots[p][:, kc, :], start=(kc == 0), stop=(kc == OKC - 1))
                        return ins
                    B.op("pe", mm, reads=[wbuf, otb[p]], writes=[byb])
                    eng = "dve" if oc % 2 else "act"
                    B.op(eng, lambda e, by=by, oc=oc, p=p, eng=eng: (e.tensor_copy(out=yt[p][:, oc, :], in_=by[:, 0:T]) if eng == "dve"
                                                                     else e.copy(out=yt[p][:, oc, :], in_=by[:, 0:T])),
                         reads=[byb], writes=[ytb[p]])
                qd_, col = (t * T) // SQ, (t * T) % SQ
                B.dma("pool", part.rearrange("(r kc p) t -> r p kc t", r=4, p=128)[qd_][:, :, col:col + T], yt[p], reads=[ytb[p]])
            B.cc(ccsem, "ReduceScatter", ALU.add, RG, [part], [red[:, 0:SQ]])
            A.reset()
            xs = [A.get(F32, KC, T) for _ in range(2)]
            rt = [A.get(F32, KC, T) for _ in range(2)]
            xb, rb_ = [Buf(), Buf()], [Buf(), Buf()]
            for i, (s_, t) in enumerate([tl for tl in TILES if tl[0] == 1]):
                p = i % 2
                B.dma("sp", xs[p], xT_tile(1, t), writes=[xb[p]])
                B.dma("sp", rt[p], red.rearrange("(kc p) t -> p kc t", p=128)[:, :, t * T:(t + 1) * T], writes=[rb_[p]])
                B.op("dve", lambda e, p=p: e.tensor_tensor(out=xs[p], in0=xs[p], in1=rt[p], op=ALU.add), reads=[rb_[p]], writes=[xb[p]])
                B.dma("pool", xT_tile(1, t), xs[p], reads=[xb[p]])
            B.barrier()

        def gather_x():
            for k in range(NGP):
                B.cc(ccsem, "AllGather", ALU.bypass, RG, [xTs[k * GP:(k + 1) * GP, 0:SQ]], [xg[k][:, 0:SQ]])

        def final():
            A.reset()
            xs = [A.get(F32, KC, T) for _ in range(2)]
            xn = A.get(F32, KC, T)
            yo = [A.get(F32, TT, D) for _ in range(2)]
            sq2 = [A.get(BF16, min(2, KC), T) for _ in range(2)]
            rs = A.get(F32, T)
            xb, yb = [Buf(), Buf()], [Buf(), Buf()]
            xnb, rsb = Buf(), Buf()
            sqb = [Buf(), Buf()]
            g_ap = gcol(3 * c.DEPTH)
            G = min(4, KC)
            tiles = list(TILES)
            B.dma("sp", xs[0], xT_tile(*tiles[0]), writes=[xb[0]])
            for i, (s, t) in enumerate(tiles):
                p = i % 2
                if i + 1 < len(tiles):
                    B.dma("sp", xs[1 - p], xT_tile(*tiles[i + 1]), writes=[xb[1 - p]])
                rmsnorm(xs[p], xb[p], g_ap, xn, xnb, sq2, sqb, banks[6], bankb[6], rs, rsb)
                n = 0
                for tt in range(TT):
                    for g in range(KC // G):
                        bk = n % 4
                        n += 1

                        def tr(e, tt=tt, g=g, bk=bk):
                            for j in range(G):
                                kc = g * G + j
                                ins = e.transpose(banks[bk][:, j * 128:(j + 1) * 128], xn[:, kc, tt * 128:(tt + 1) * 128], ident)
                            return ins
                        B.op("pe", tr, reads=[xnb, cb], writes=[bankb[bk]])
                        eng = "dve" if n % 2 else "act"

                        def cp(e, tt=tt, g=g, bk=bk, p=p, eng=eng):
                            o = yo[p][:, tt, g * G * 128:(g + 1) * G * 128]
                            i_ = banks[bk][:, 0:G * 128]
                            return e.tensor_copy(out=o, in_=i_) if eng == "dve" else e.copy(out=o, in_=i_)
                        B.op(eng, cp, reads=[bankb[bk]], writes=[yb[p]])
                dsty = y_out[0] if s == 0 else ys_out
                B.dma("pool", dsty[t * T:(t + 1) * T, :].rearrange("(tt p) d -> p tt d", p=128), yo[p], reads=[yb[p]])
            B.barrier()

        ph = phases or ("precast", "etab", "in", "layers", "mix", "final")
        if "precast" in ph:
            precast()
        if "etab" in ph:
            etables()
        if "in" in ph:
            in_transpose()
        if "layers" in ph:
            for l in range(c.DEPTH):
                B.seg()
                if "noffn" not in ph:
                    ffn(l, 1)
                if "mix" in ph:
                    B.seg()
                    in_proj(l)
                    if c.BAL:
                        gather_x()
                        in_proj(l, samp=True)
                    B.seg()
                    if l % 2 == 0:
                        attn_a(l)
                        B.seg()
                        attn_b(l)
                    else:
                        attn_c(l)
                        B.seg()
                        attn_d(l)
                    B.seg()
                    out_proj(l)
                    if c.BAL:
                        out_proj_s(l)
                B.seg()
                if "noffn" not in ph:
                    ffn(l, 2)
        if "final" in ph:
            final()
        B.barrier()
        block = es.enter_context(nc.Block())
        B.emit(block)
    nc._nops = B.nops
    return nc


def _t5_bucket_np(rel):
    half, max_exact = 16, 8
    n = np.abs(rel)
    nf = np.maximum(n, 1).astype(np.float32)
    large = max_exact + (np.log(nf / np.float32(max_exact)) / np.float32(math.log(2048 / max_exact))
                         * np.float32(half - max_exact)).astype(np.int32)
    large = np.minimum(large, half - 1)
    return np.where(rel > 0, half, 0) + np.where(n < max_exact, n, large)


def make_consts(cfg):
    S = cfg.S
    out = {}
    n_pairs, n_freq = 64, 32
    pos = np.arange(S)
    row = (pos // GRID_W).astype(np.float32)
    col = (pos % GRID_W).astype(np.float32)
    omega = (np.float32(10000.0) ** (-(np.arange(n_freq, dtype=np.float32) * np.float32(2.0) / np.float32(n_pairs)))).astype(np.float32)
    ang = np.concatenate([row[:, None] * omega, col[:, None] * omega], axis=-1).astype(np.float32)
    cos = np.cos(ang).astype(np.float32)
    sin = np.sin(ang).astype(np.float32)
    out["cosT"] = np.ascontiguousarray(np.repeat(cos, 2, axis=1).T)
    out["sinT"] = np.ascontiguousarray(np.repeat(sin, 2, axis=1).T)
    pi = np.zeros((128, 128), np.float32)
    for i in range(64):
        pi[2 * i + 1, 2 * i] = -1.0
        pi[2 * i, 2 * i + 1] = 1.0
    out["piT"] = pi
    kk = np.arange(128)[:, None, None]
    cc = np.arange(3)[None, :, None]
    qq = np.arange(128)[None, None, :]
    rel = (cc - 1) * 128 + kk - qq
    bk = _t5_bucket_np(rel)
    m = (np.abs(rel) <= 128)
    ohA = np.zeros((128, 32, 3, 128), np.float32)
    for b in range(32):
        ohA[:, b] = ((bk == b) & m)
    out["ohA"] = ohA.reshape(128, 32, 384)
    out["mA"] = m.astype(np.float32).reshape(128, 384)
    kk = np.arange(64)[:, None, None]
    qq = np.arange(64)[None, None, :]
    rs = (cc - 1) * 64 + kk - qq
    m = (np.abs(rs) <= 64)
    ohD = np.zeros((64, 3, 32, 3, 64), np.float32)
    for g, dil in enumerate((1, 4, 16)):
        bk = _t5_bucket_np(rs * dil)
        for b in range(32):
            ohD[:, g, b] = ((bk == b) & m)
    out["ohD"] = ohD.reshape(64, 3, 32, 192)
    out["mD"] = m.astype(np.float32).reshape(64, 192)
    kc_ = np.arange(64)[:, None]
    qc_ = np.arange(64)[None, :]
    dci = np.clip(kc_ - qc_ + 15, 0, 30)
    cs = np.clip(qc_ - 8, 0, 48)
    m = (kc_ >= cs) & (kc_ < cs + 16)
    ohC = np.zeros((31, 64, 64), np.float32)
    for d_ in range(31):
        ohC[d_] = ((dci == d_) & m).T
    out["ohC"] = ohC
    out["mC"] = m.astype(np.float32)
    return out


def _chunks(W, kc_n, oc_n):
    return W.reshape(kc_n, 128, oc_n, 128).transpose(2, 1, 0, 3)


def pack_weights(cfg, inp):
    c = cfg
    KC, FFC, FF = c.KC, c.FFC, c.FF
    wp = np.empty((c.NW,), np.float32)

    def put(off, arr):
        a = np.ascontiguousarray(arr, dtype=np.float32).reshape(-1)
        wp[off:off + a.size] = a

    w_in = {1: inp["ffn1_w_in"], 2: inp["ffn2_w_in"]}
    w_out = {1: inp["ffn1_w_out"], 2: inp["ffn2_w_out"]}
    for l in range(c.DEPTH):
        i = l // 2
        d = {k: (v[0] + c.lbase[l], v[1], v[2]) for k, v in c.woff[l].items()}
        for which in (1, 2):
            Win = np.asarray(w_in[which][l])
            g = _chunks(Win[:, :FF], KC, FFC)
            u = _chunks(Win[:, FF:], KC, FFC)
            put(d["f%din" % which][0], np.concatenate([g, u], axis=3))
            Wout = np.asarray(w_out[which][l])
            put(d["f%dout" % which][0], _chunks(Wout, FFC, KC))
        if l % 2 == 0:
            put(d["min"][0], _chunks(np.asarray(inp["ab_w_in"][i]), KC, 24))
            put(d["mout"][0], _chunks(np.asarray(inp["ab_w_out"][i]), 16, KC))
        else:
            put(d["min"][0], _chunks(np.asarray(inp["cd_w_in"][i]), KC, 32))
            put(d["mout"][0], _chunks(np.asarray(inp["cd_w_out"][i]), 12, KC))
    return wp


def pack_weights_s(cfg, inp, qd):
    c = cfg
    KC = c.KC
    wp = np.empty((c.NWS,), np.float32)

    def put(off, arr):
        a = np.ascontiguousarray(arr, dtype=np.float32).reshape(-1)
        wp[off:off + a.size] = a

    for l in range(c.DEPTH):
        i = l // 2
        d = c.woff_s[l]
        if l % 2 == 0:
            cin = [2 * qd, 2 * qd + 1, 8 + qd // 2, 12 + 2 * qd, 13 + 2 * qd, 20 + qd // 2, 10 + qd // 2, 22 + qd // 2]
            rout = [2 * qd, 2 * qd + 1, 8 + 2 * qd, 9 + 2 * qd]
            Win, Wout = np.asarray(inp["ab_w_in"][i]), np.asarray(inp["ab_w_out"][i])
            ninc = 24
        else:
            cin = [2 * qd, 2 * qd + 1, 8 + qd // 2, 12 + qd, 16 + qd, 20 + qd, 24 + qd, 10 + qd // 2, 28 + qd]
            rout = [2 * qd, 2 * qd + 1, 8 + qd]
            Win, Wout = np.asarray(inp["cd_w_in"][i]), np.asarray(inp["cd_w_out"][i])
            ninc = 32
        put(d["min"][0], _chunks(Win, KC, ninc)[cin])
        Ws = np.concatenate([Wout[r * 128:(r + 1) * 128, :] for r in rout], axis=0)
        put(d["mout"][0], _chunks(Ws, len(rout), KC))
    return wp


def host_inputs(cfg, inp):
    c = cfg
    KC = c.KC
    shared = make_consts(c)
    shared["wpack"] = pack_weights(c, inp)

    def fm(v):
        return np.asarray(v, np.float32).reshape(KC, 128).T

    cols = [fm(inp["norm_ffn1"][l]) for l in range(c.DEPTH)] + [fm(inp["norm_mix"][l]) for l in range(c.DEPTH)] + \
           [fm(inp["norm_ffn2"][l]) for l in range(c.DEPTH)] + [fm(inp["final_norm"])]
    shared["gains"] = np.ascontiguousarray(np.concatenate(cols, axis=1))
    qkg = np.zeros((128, 2 * c.n_even), np.float32)
    for i in range(c.n_even):
        qkg[:, 2 * i] = np.asarray(inp["ab_q_gain"][i])
        qkg[:, 2 * i + 1] = np.asarray(inp["ab_k_gain"][i])
    shared["qkg"] = qkg
    shared["sink"] = np.ascontiguousarray(np.asarray(inp["ab_sink"], np.float32).reshape(-1))
    shared["t5"] = np.ascontiguousarray(np.asarray(inp["t5_table"], np.float32).reshape(-1))
    r = np.asarray(inp["cd_rpb"], np.float32)
    shared["rpb"] = (np.ascontiguousarray(r.reshape(r.shape[0], 120, 31).transpose(0, 2, 1)) if r.shape[0]
                     else np.zeros((1, 31, 120), np.float32))
    return shared


def core_inputs(cfg, inp, qd):
    c = cfg
    m = {"wpack_s": pack_weights_s(c, inp, qd)}
    t5 = np.asarray(inp["t5_table"], np.float32)
    m["t5m"] = np.ascontiguousarray(t5[:, [2 * qd, 2 * qd + 1, 8 + qd, 12 + qd, 16 + qd]].reshape(-1))
    sk = np.asarray(inp["ab_sink"], np.float32)
    m["sinkm"] = np.ascontiguousarray(sk[:, 2 * qd:2 * qd + 2].reshape(-1))
    r = np.asarray(inp["cd_rpb"], np.float32)
    m["rpbm"] = (np.ascontiguousarray(r[:, 2 * qd:2 * qd + 2].reshape(r.shape[0], 30, 31).transpose(0, 2, 1)) if r.shape[0]
                 else np.zeros((1, 31, 30), np.float32))
    return m


_NC_CACHE = {}


def kernel(**inputs):
    cfg = Cfg()
    shared = host_inputs(cfg, inputs)
    xp = np.asarray(inputs["x_prompt"], np.float32)
    xsm = np.asarray(inputs["x_sample"], np.float32)
    SQ = cfg.SQ
    percore = [core_inputs(cfg, inputs, qd) for qd in range(4)]
    in_maps = []
    for cid in range(N_CORES):
        j, qd = cid // 4, cid % 4
        m = dict(shared)
        m.update(percore[qd])
        m["x"] = xp[cid:cid + 1]
        m["xs"] = np.ascontiguousarray(xsm[j, qd * SQ:(qd + 1) * SQ])
        in_maps.append(m)
    if "nc" not in _NC_CACHE:
        _NC_CACHE["nc"] = build(cfg)
    res = run_bass_kernel_spmd(_NC_CACHE["nc"], in_maps, core_ids=list(range(N_CORES)))
    y_prompt = np.stack([res.results[cid]["y"][0] for cid in range(N_CORES)], axis=0)
    y_sample = np.stack([np.concatenate([res.results[4 * j + qd]["ys"] for qd in range(4)], axis=0)
                         for j in range(xsm.shape[0])], axis=0)
    return (y_prompt.astype(np.float32), y_sample.astype(np.float32))
```

```python
import contextlib
import math
import numpy as np
import concourse.bass as bass
import concourse.mybir as mybir
from concourse.bass_utils import run_bass_kernel_spmd

F32 = mybir.dt.float32
BF16 = mybir.dt.bfloat16
ALU = mybir.AluOpType
AF = mybir.ActivationFunctionType
ENG = ("pe", "act", "dve", "pool", "sp")

HD = 128
GRID_W = 64
EPS = 1e-6
SCALE = HD ** -0.5
N_CORES = 8


class Cfg:
    def __init__(self, D=2048, FF=5632, S=4096, NSEQ=1, DEPTH=4, T=512, BAL=True, NCORES=8):
        self.D, self.FF, self.S, self.NSEQ, self.DEPTH, self.T = D, FF, S, NSEQ, DEPTH, T
        self.KC = D // 128
        self.BAL = BAL
        self.NCORES = NCORES
        self.SQ = S // 4 if BAL else 0
        self.FFC = FF // 128
        self.NT = S // T
        self.TT = T // 128
        self.n_even = (DEPTH + 1) // 2
        self.n_odd = DEPTH // 2
        off = 0
        self.woff = []
        self.lbase = []
        for l in range(DEPTH):
            d = {}
            self.lbase.append(off)
            inc = 24 if l % 2 == 0 else 32
            okc = 16 if l % 2 == 0 else 12
            for name, nblk, L in (("f1in", self.FFC, self.KC * 256), ("f1out", self.KC, self.FFC * 128),
                                  ("min", inc, self.KC * 128), ("mout", self.KC, okc * 128),
                                  ("f2in", self.FFC, self.KC * 256), ("f2out", self.KC, self.FFC * 128)):
                d[name] = (off - self.lbase[l], nblk, L)
                off += nblk * 128 * L
            self.woff.append(d)
        self.NW = off
        self.lbase.append(off)
        off = 0
        self.woff_s = []
        for l in range(DEPTH):
            nin, okc = (8, 4) if l % 2 == 0 else (9, 3)
            d = {"min": (off, nin, self.KC * 128)}
            off += nin * 128 * self.KC * 128
            d["mout"] = (off, self.KC, okc * 128)
            off += self.KC * 128 * okc * 128
            self.woff_s.append(d)
        self.NWS = off
        self.SLOT = max(self.KC * 256, self.FFC * 128, 16 * 128)


class Buf:
    __slots__ = ("w", "r", "const")

    def __init__(self, const=False):
        self.w = None
        self.r = {}
        self.const = const


class Bld:
    def __init__(self, nc, es, nsets=4, ndma=7):
        self.nc = nc
        self.q = {e: [] for e in ENG}
        self.nsets, self.ndma, self.cur = nsets, ndma, 0
        self.psem = [{e: es.enter_context(nc.semaphore(f"p{s}{e}")) for e in ENG} for s in range(nsets)]
        self.pcnt = [{e: 0 for e in ENG} for s in range(nsets)]
        self.dq = ("sp", "pool")
        self.dsem = [{qn: [es.enter_context(nc.semaphore(f"d{s}{qn}{i}")) for i in range(ndma)] for qn in self.dq}
                     for s in range(nsets)]
        self.dcnt = [{qn: [0] * ndma for qn in self.dq} for s in range(nsets)]
        self.drr = {qn: 0 for qn in self.dq}
        self.last = {e: None for e in ENG}
        self.dlast = {}
        self.nops = 0

    def seg(self):
        self.cur = (self.cur + 1) % self.nsets

    @staticmethod
    def _add(deps, t):
        if t is None:
            return
        k = t[2]
        if k not in deps or deps[k][1] < t[1]:
            deps[k] = t

    def _deps(self, reads, writes):
        deps = {}
        for b in reads:
            self._add(deps, b.w)
        for b in writes:
            self._add(deps, b.w)
            for t in b.r.values():
                self._add(deps, t)
        return deps

    def _commit(self, tok, reads, writes):
        for b in reads:
            if not b.const:
                b.r[tok[2]] = tok
        for b in writes:
            b.w = tok
            b.r = {}

    def op(self, eng, fn, reads=(), writes=()):
        deps = self._deps(reads, writes)
        s = self.cur
        key = ("p", s, eng)
        if eng == "pe":
            for k in [k for k in deps if k[0] == "p" and k[2] == "pe"]:
                del deps[k]
        self.pcnt[s][eng] += 1
        tok = (self.psem[s][eng], self.pcnt[s][eng], key)
        self.q[eng].append((fn, list(deps.values()), tok[0], 1))
        self._commit(tok, reads, writes)
        self.last[eng] = tok
        self.nops += 1
        return tok

    def dma(self, qn, out, in_, reads=(), writes=(), extra=()):
        deps = self._deps(reads, writes)
        for t in extra:
            self._add(deps, t)
        s = self.cur
        i = self.drr[qn]
        self.drr[qn] = (i + 1) % self.ndma
        key = ("d", s, qn, i)
        self._add(deps, self.dlast.get(key))
        self.dcnt[s][qn][i] += 16
        tok = (self.dsem[s][qn][i], self.dcnt[s][qn][i], key)
        self.dlast[key] = tok
        self.q[qn].append((lambda e, o=out, i_=in_: e.dma_start(out=o, in_=i_), list(deps.values()), tok[0], 16))
        self._commit(tok, reads, writes)
        self.nops += 1
        return tok

    def cc(self, es_sem, kind, op, rg, ins, outs):
        self.barrier()
        self.cccnt = getattr(self, "cccnt", 0) + 1
        tok = (es_sem, self.cccnt, ("c",))
        self.q["pool"].append((lambda e: e.collective_compute(kind, op, replica_groups=rg, ins=ins, outs=outs), [], es_sem, 1))
        self.dlast[("c",)] = tok
        self.barrier()

    def barrier(self):
        toks = [t for t in self.last.values() if t is not None] + list(self.dlast.values())
        for e in ENG:
            self.q[e].append((None, list(toks), None, 0))

    def emit(self, block):
        def mk(name):
            def f(e):
                seen = {}
                for fn, deps, sem, inc in self.q[name]:
                    for t in deps:
                        if seen.get(t[2], 0) < t[1]:
                            e.wait_ge(t[0], t[1])
                            seen[t[2]] = t[1]
                    if fn is not None:
                        fn(e).then_inc(sem, inc)
            return f
        block.tensor(mk("pe"))
        block.scalar(mk("act"))
        block.vector(mk("dve"))
        block.gpsimd(mk("pool"))
        block.sync(mk("sp"))


class Arena:
    def __init__(self, t, nbytes):
        self.t, self.n, self.p = t, nbytes, 0

    def reset(self, p=0):
        self.p = p

    def get(self, dtype, *free):
        esz = 4 if dtype == F32 else 2
        n = int(np.prod(free)) * esz
        n_al = (n + 63) // 64 * 64
        assert self.p + n_al <= self.n, f"arena overflow {self.p}+{n_al}>{self.n}"
        a = self.t[:, self.p // 2:(self.p + n) // 2]
        self.p += n_al
        if dtype == F32:
            a = a.bitcast(F32)
        if len(free) == 2:
            a = a.rearrange("p (a b) -> p a b", b=free[1])
        elif len(free) == 3:
            a = a.rearrange("p (a b c) -> p a b c", b=free[1], c=free[2])
        return a


class WStream:
    def __init__(self, B, wbf, slots, entries, tok=None):
        self.B, self.wbf, self.slots, self.entries = B, wbf, slots, entries
        self.extra = list(tok) if tok else []
        self.bufs = [Buf() for _ in slots]
        self.issued = 0

    def _load(self, e):
        off, L = self.entries[e]
        k = e % len(self.slots)
        src = self.wbf[off:off + 128 * L].rearrange("(p l) -> p l", p=128)
        self.B.dma("sp", self.slots[k][:, 0:L], src, writes=[self.bufs[k]], extra=self.extra)

    def get(self, e):
        n = len(self.slots)
        while self.issued < min(len(self.entries), e + n):
            self._load(self.issued)
            self.issued += 1
        k = e % n
        return self.bufs[k], self.slots[k]


def build(cfg, phases=None):
    c = cfg
    D, FF, S, NSEQ, KC, FFC, T, TT, NT = c.D, c.FF, c.S, c.NSEQ, c.KC, c.FFC, c.T, c.TT, c.NT
    NB128 = S // 128
    ROWS = S // GRID_W
    nc = bass.Bass("TRN2", target_bir_lowering=False)
    dt = nc.dram_tensor
    x_in = dt("x", [NSEQ, S, D], F32, kind="ExternalInput").ap()
    wpack = dt("wpack", [c.NW], F32, kind="ExternalInput").ap()
    gains = dt("gains", [128, 3 * c.DEPTH * KC + KC], F32, kind="ExternalInput").ap()
    qkg = dt("qkg", [128, 2 * c.n_even], F32, kind="ExternalInput").ap()
    sink = dt("sink", [c.n_even * 8], F32, kind="ExternalInput").ap()
    t5 = dt("t5", [32 * 20], F32, kind="ExternalInput").ap()
    rpb = dt("rpb", [max(c.n_odd, 1), 31, 120], F32, kind="ExternalInput").ap()
    cosT = dt("cosT", [128, S], F32, kind="ExternalInput").ap()
    sinT = dt("sinT", [128, S], F32, kind="ExternalInput").ap()
    piT = dt("piT", [128, 128], F32, kind="ExternalInput").ap()
    ohA = dt("ohA", [128, 32, 384], F32, kind="ExternalInput").ap()
    mA = dt("mA", [128, 384], F32, kind="ExternalInput").ap()
    ohD = dt("ohD", [64, 3, 32, 192], F32, kind="ExternalInput").ap()
    mD = dt("mD", [64, 192], F32, kind="ExternalInput").ap()
    ohC = dt("ohC", [31, 64, 64], F32, kind="ExternalInput").ap()
    mC = dt("mC", [64, 64], F32, kind="ExternalInput").ap()
    y_out = dt("y", [NSEQ, S, D], F32, kind="ExternalOutput").ap()
    SQ = c.SQ
    SQn = max(SQ, T)
    xs_in = dt("xs", [SQn, D], F32, kind="ExternalInput").ap()
    ys_out = dt("ys", [SQn, D], F32, kind="ExternalOutput").ap()
    wpack_s = dt("wpack_s", [c.NWS], F32, kind="ExternalInput").ap()
    t5m = dt("t5m", [32 * 5], F32, kind="ExternalInput").ap()
    sinkm = dt("sinkm", [c.n_even * 2], F32, kind="ExternalInput").ap()
    rpbm = dt("rpbm", [max(c.n_odd, 1), 31, 30], F32, kind="ExternalInput").ap()
    wbf_s = dt("wbf_s", [c.NWS], BF16, kind="Internal").ap()
    xTs = dt("xTs", [D, SQn], F32, kind="Internal").ap()
    GP = min(D, 256)
    NGP = D // GP
    xg = dt("xg", [NGP, 4 * GP, SQn], F32, kind="Internal").ap()
    part = dt("part", [4 * D, SQn], F32, kind="Internal").ap()
    red = dt("red", [D, SQn], F32, kind="Internal").ap()
    qks = dt("qks", [7 * 128, S], BF16, kind="Internal").ap()
    vts = dt("vts", [S, 2 * 128], BF16, kind="Internal").ap()
    oTs = dt("oTs", [4 * 128, S], BF16, kind="Internal").ap()
    EAs = dt("EAs", [2, 128, 384], F32, kind="Internal").ap()
    EDs = dt("EDs", [3, 64, 192], F32, kind="Internal").ap()
    ECs = dt("ECs", [max(c.n_odd, 1), 2, 15, 64, 64], F32, kind="Internal").ap()
    TILES = [(0, t) for t in range(NT)] + [(1, t) for t in range(SQ // T)]
    RG = [list(range(g * 4, g * 4 + 4)) for g in range(c.NCORES // 4)]

    wbfs = [dt(f"wbf{l}", [c.lbase[l + 1] - c.lbase[l]], BF16, kind="Internal").ap() for l in range(c.DEPTH)]
    xT = dt("xT", [NSEQ, D, S], F32, kind="Internal").ap()
    qk = dt("qk", [NSEQ, 26 * 128, S], BF16, kind="Internal").ap()
    vt = dt("vt", [NSEQ, S, 6 * 128], BF16, kind="Internal").ap()
    oT = dt("oT", [NSEQ, 2048, S], BF16, kind="Internal").ap()
    EA = dt("EA", [8, 128, 384], F32, kind="Internal").ap()
    ED = dt("ED", [12, 64, 192], F32, kind="Internal").ap()
    EC = dt("EC", [max(c.n_odd, 1), 8, 15, 64, 64], F32, kind="Internal").ap()

    ARENA_BYTES = 178 * 1024
    CONST_BYTES = 10 * 1024
    with contextlib.ExitStack() as es:
        arena_t = es.enter_context(nc.sbuf_tensor("arena", [128, ARENA_BYTES // 2], BF16))
        const_t = es.enter_context(nc.sbuf_tensor("consts", [128, CONST_BYTES // 2], BF16))
        banks = [es.enter_context(nc.psum_tensor(f"bank{i}", [128, 512], F32)) for i in range(8)]
        B = Bld(nc, es)
        ccsem = es.enter_context(nc.semaphore("ccsem"))
        A = Arena(arena_t, ARENA_BYTES)
        CA = Arena(const_t, CONST_BYTES)
        bankb = [Buf() for _ in range(8)]

        ident = CA.get(F32, 128)
        onesD = CA.get(BF16, 128)
        ones128 = CA.get(BF16, 128)
        ones_bf = CA.get(BF16, 128)
        pi_sb = CA.get(F32, 128)
        gains_sb = CA.get(F32, 3 * c.DEPTH * KC + KC)
        qkg_sb = CA.get(F32, 2 * c.n_even)
        esink = CA.get(F32, c.n_even * 10)
        cb = Buf(const=True)
        B.op("pool", lambda e: e.memset(ident, 0.0), writes=[cb])
        B.op("pool", lambda e: e.affine_select(out=ident, in_=ident, pattern=[[-1, 128]], compare_op=ALU.not_equal,
                                               fill=1.0, base=0, channel_multiplier=1), writes=[cb])
        B.op("pool", lambda e: e.memset(onesD, 1.0 / D), writes=[cb])
        B.op("pool", lambda e: e.memset(ones128, 1.0 / 128), writes=[cb])
        B.op("pool", lambda e: e.memset(ones_bf, 1.0), writes=[cb])
        B.dma("sp", pi_sb, piT, writes=[cb])
        B.dma("sp", gains_sb, gains, writes=[cb])
        B.dma("sp", qkg_sb, qkg, writes=[cb])
        B.dma("sp", esink[:, 0:c.n_even * 8], sink.partition_broadcast(128), writes=[cb])
        B.dma("sp", esink[:, c.n_even * 8:c.n_even * 10], sinkm.partition_broadcast(128), writes=[cb])
        B.barrier()
        B.op("act", lambda e: e.activation(out=esink, in_=esink, func=AF.Exp), writes=[cb])
        B.barrier()

        def gcol(idx):
            return gains_sb[:, idx * KC:(idx + 1) * KC]

        pcsem = {(l, h): es.enter_context(nc.semaphore(f"pc{l}{h}")) for l in range(c.DEPTH) for h in "ab"}
        pcsem["s"] = es.enter_context(nc.semaphore("pcs"))
        wtok = {}
        pc_pending = {}

        def pc_plan():
            blk = 128 * 16384
            jobs = []
            if c.BAL:
                jobs.append(("s", wpack_s, wbf_s, c.NWS, c.NWS))
            for l in range(c.DEPTH):
                jobs.append((l, wpack[c.lbase[l]:c.lbase[l + 1]], wbfs[l], c.lbase[l + 1] - c.lbase[l], c.woff[l]["min"][0]))
            for key, src, dst, n, split in jobs:
                lst = []
                ca = cb_ = 0
                for e0 in range(0, n, blk):
                    e1 = min(n, e0 + blk)
                    first = e0 < split
                    sem = pcsem["s"] if key == "s" else pcsem[(key, "a" if first else "b")]
                    lst.append((dst[e0:e1].rearrange("(p n) -> p n", p=128), src[e0:e1].rearrange("(p n) -> p n", p=128), sem))
                    if first:
                        ca += 16
                    else:
                        cb_ += 16
                pc_pending[key] = lst
                if key == "s":
                    wtok["s"] = [(pcsem["s"], ca, ("pc", "s"))]
                else:
                    ta = (pcsem[(key, "a")], ca, ("pc", key, "a"))
                    wtok[(key, "a")] = [ta]
                    wtok[(key, "b")] = [ta] + ([(pcsem[(key, "b")], cb_, ("pc", key, "b"))] if cb_ else [])

        def pc_drip(key, k):
            lst = pc_pending.get(key)
            while lst and k > 0:
                o, i_, sem = lst.pop(0)
                B.q["pool"].append((lambda e, o=o, i_=i_: e.dma_start(out=o, in_=i_), [], sem, 16))
                k -= 1

        def precast():
            pc_plan()
            if c.BAL:
                pc_drip("s", 10 ** 9)
            pc_drip(0, 10 ** 9)

        def etables():
            etab_a(t5, 20, list(range(8)), EA)
            etab_d(t5, 20, [(g, 8 + g * 4 + s_) for g in range(3) for s_ in range(4)], ED)
            for li in range(c.n_odd):
                etab_c(li)
            if c.BAL:
                etab_a(t5m, 5, [0, 1], EAs)
                etab_d(t5m, 5, [(g, 2 + g) for g in range(3)], EDs)

        def accum(acc_h, oh_b, sc, first, bufs_r, buf_w):
            if first:
                B.op("dve", lambda e: e.tensor_scalar(out=acc_h, in0=oh_b, scalar1=sc, scalar2=None, op0=ALU.mult),
                     reads=bufs_r, writes=[buf_w])
            else:
                B.op("dve", lambda e: e.scalar_tensor_tensor(out=acc_h, in0=oh_b, scalar=sc, in1=acc_h, op0=ALU.mult, op1=ALU.add),
                     reads=bufs_r, writes=[buf_w])

        def etab_a(tsrc, nct, cols, out_d):
            A.reset()
            nh = len(cols)
            tb = A.get(F32, 32 * nct)
            oh = A.get(F32, 32, 384)
            acc = A.get(F32, nh, 384)
            msk = A.get(F32, 384)
            bT, bO, bA, bM = Buf(), Buf(), Buf(), Buf()
            B.dma("sp", tb, tsrc.partition_broadcast(128), writes=[bT])
            B.dma("sp", oh, ohA, writes=[bO])
            B.dma("sp", msk, mA, writes=[bM])
            for h, col in enumerate(cols):
                for b_ in range(32):
                    accum(acc[:, h, :], oh[:, b_, :], tb[:, b_ * nct + col:b_ * nct + col + 1], b_ == 0, [bT, bO], bA)
            B.op("act", lambda e: e.activation(out=acc, in_=acc, func=AF.Exp), writes=[bA])
            for h in range(nh):
                B.op("dve", lambda e, h=h: e.tensor_tensor(out=acc[:, h, :], in0=acc[:, h, :], in1=msk, op=ALU.mult),
                     reads=[bM], writes=[bA])
            B.dma("sp", out_d.rearrange("h p f -> p h f"), acc, reads=[bA])
            B.barrier()

        def etab_d(tsrc, nct, gcols, out_d):
            A.reset()
            nh = len(gcols)
            tb = A.get(F32, 32 * nct)
            oh = A.get(F32, 3, 32, 192)
            acc = A.get(F32, nh, 192)
            msk = A.get(F32, 192)
            bT, bO, bA, bM = Buf(), Buf(), Buf(), Buf()
            B.dma("sp", tb, tsrc.partition_broadcast(128), writes=[bT])
            B.dma("sp", oh[0:64], ohD, writes=[bO])
            B.dma("sp", msk[0:64], mD, writes=[bM])
            for hh, (g, col) in enumerate(gcols):
                for b_ in range(32):
                    accum(acc[0:64, hh, :], oh[0:64, g, b_, :], tb[0:64, b_ * nct + col:b_ * nct + col + 1], b_ == 0, [bT, bO], bA)
            B.op("act", lambda e: e.activation(out=acc[0:64], in_=acc[0:64], func=AF.Exp), writes=[bA])
            for hh in range(nh):
                B.op("dve", lambda e, hh=hh: e.tensor_tensor(out=acc[0:64, hh, :], in0=acc[0:64, hh, :], in1=msk[0:64], op=ALU.mult),
                     reads=[bM], writes=[bA])
            B.dma("sp", out_d.rearrange("h p f -> p h f"), acc[0:64], reads=[bA])
            B.barrier()

        def etab_c(li):
            A.reset()
            NH = 150 if c.BAL else 120
            R = A.get(F32, NH)
            M = A.get(F32, 64, 64)
            acc = A.get(F32, NH, 64)
            msk = A.get(F32, 64)
            bR, bM, bA, bK = Buf(), Buf(), Buf(), Buf()
            B.dma("sp", R[0:31, 0:120], rpb[li], writes=[bR])
            if c.BAL:
                B.dma("sp", R[0:31, 120:150], rpbm[li], writes=[bR])
            B.dma("sp", M[0:31], ohC, writes=[bM])
            B.dma("sp", msk[0:64], mC, writes=[bK])
            for qc in range(64):
                bk = qc % 4
                B.op("pe", lambda e, qc=qc, bk=bk: e.matmul(banks[bk][0:64, 0:NH], M[0:31, qc, :], R[0:31, 0:NH], start=True, stop=True),
                     reads=[bR, bM], writes=[bankb[bk]])
                eng = "dve" if qc % 2 else "act"
                B.op(eng, lambda e, qc=qc, bk=bk, eng=eng: (e.tensor_copy(out=acc[0:64, :, qc], in_=banks[bk][0:64, 0:NH]) if eng == "dve"
                                                         else e.copy(out=acc[0:64, :, qc], in_=banks[bk][0:64, 0:NH])),
                     reads=[bankb[bk]], writes=[bA])
            B.op("act", lambda e: e.activation(out=acc[0:64], in_=acc[0:64], func=AF.Exp), writes=[bA])
            m64 = msk[0:64]
            mb = bass.AP(m64.tensor, m64.offset, [list(m64.ap[0]), [0, NH], [1, 64]])
            B.op("dve", lambda e: e.tensor_tensor(out=acc[0:64], in0=acc[0:64], in1=mb, op=ALU.mult), reads=[bK], writes=[bA])
            B.dma("sp", EC[li].rearrange("h d k q -> k (h d) q"), acc[0:64, 0:120, :], reads=[bA])
            if c.BAL:
                B.dma("sp", ECs[li].rearrange("h d k q -> k (h d) q"), acc[0:64, 120:150, :], reads=[bA])
            B.barrier()

        def rmsnorm(x_ap, xb, g_ap, xn_ap, xnb, sq2, sqb, pss, pssb, rs, rsb, width=T):
            G = min(2, KC)
            ng = KC // G
            for g in range(ng):
                sq = sq2[g % 2]
                B.op("act", lambda e, g=g, sq=sq: e.activation(out=sq[:, 0:G, 0:width], in_=x_ap[:, g * G:(g + 1) * G, :],
                                                               func=AF.Square), reads=[xb], writes=[sqb[g % 2]])

                def mm(e, g=g, sq=sq):
                    for j in range(G):
                        ins = e.matmul(pss[:, 0:width], onesD, sq[:, j, 0:width], start=(g == 0 and j == 0),
                                       stop=(g == ng - 1 and j == G - 1))
                    return ins
                B.op("pe", mm, reads=[sqb[g % 2]], writes=([pssb] if g in (0, ng - 1) else []))
            B.op("act", lambda e: e.activation(out=rs[:, 0:width], in_=pss[:, 0:width], func=AF.Sqrt, bias=EPS, scale=1.0),
                 reads=[pssb], writes=[rsb])
            B.op("dve", lambda e: e.reciprocal(out=rs[:, 0:width], in_=rs[:, 0:width]), writes=[rsb])
            for kc in range(KC):
                B.op("dve", lambda e, kc=kc: e.scalar_tensor_tensor(
                    out=xn_ap[:, kc, :], in0=x_ap[:, kc, :], scalar=g_ap[:, kc:kc + 1], in1=rs[:, 0:width],
                    op0=ALU.mult, op1=ALU.mult), reads=[xb, rsb], writes=[xnb])

        def xT_tile(s, t):
            src = xT[0] if s == 0 else xTs
            return src.rearrange("(kc p) t -> p kc t", p=128)[:, :, t * T:(t + 1) * T]

        def in_transpose():
            A.reset()
            xin = [A.get(F32, TT, D) for _ in range(2)]
            xo = [A.get(F32, KC, T) for _ in range(2)]
            xinb = [Buf(), Buf()]
            xob = [Buf(), Buf()]
            G = min(4, KC)
            it = 0
            for s, t in TILES:
                if True:
                    p = it % 2
                    srcx = x_in[0] if s == 0 else xs_in
                    B.dma("sp", xin[p], srcx[t * T:(t + 1) * T, :].rearrange("(tt p) d -> p tt d", p=128),
                          writes=[xinb[p]])
                    n = 0
                    for tt in range(TT):
                        for g in range(KC // G):
                            bk = n % 4
                            n += 1

                            def tr(e, tt=tt, g=g, bk=bk, p=p):
                                for j in range(G):
                                    kc = g * G + j
                                    ins = e.transpose(banks[bk][:, j * 128:(j + 1) * 128],
                                                      xin[p][:, tt, kc * 128:(kc + 1) * 128], ident)
                                return ins
                            B.op("pe", tr, reads=[xinb[p], cb], writes=[bankb[bk]])
                            eng = "dve" if n % 2 else "act"

                            def cp(e, tt=tt, g=g, bk=bk, p=p, eng=eng):
                                o = xo[p][:, g * G:(g + 1) * G, tt * 128:(tt + 1) * 128]
                                i_ = banks[bk][:, 0:G * 128].rearrange("p (a b) -> p a b", b=128)
                                return e.tensor_copy(out=o, in_=i_) if eng == "dve" else e.copy(out=o, in_=i_)
                            B.op(eng, cp, reads=[bankb[bk]], writes=[xob[p]])
                    B.dma("sp", xT_tile(s, t), xo[p], reads=[xob[p]])
                    it += 1
            B.barrier()

        def ffn(l, which):
            A.reset()
            xs = [A.get(F32, KC, T) for _ in range(2)]
            xn = A.get(BF16, KC, T)
            hT = A.get(BF16, FFC, T)
            sq2 = [A.get(BF16, min(2, KC), T) for _ in range(2)]
            rs = A.get(F32, T)
            sg = [A.get(F32, T) for _ in range(2)]
            slots = [A.get(BF16, c.SLOT) for _ in range(3)]
            xb = [Buf(), Buf()]
            xnb, hb, rsb = Buf(), Buf(), Buf()
            sqb = [Buf(), Buf()]
            sgb = [Buf(), Buf()]
            o_in, n_in, L_in = c.woff[l]["f%din" % which]
            o_out, n_out, L_out = c.woff[l]["f%dout" % which]
            entries = []
            tiles = list(TILES)
            for _ in tiles:
                entries += [(o_in + j * 128 * L_in, L_in) for j in range(n_in)]
                entries += [(o_out + j * 128 * L_out, L_out) for j in range(n_out)]
            ws = WStream(B, wbfs[l], slots, entries, wtok.get((l, "a" if which == 1 else "b")))
            g_ap = gcol((0 if which == 1 else 2) * c.DEPTH + l)
            pss, pssb = banks[6], bankb[6]
            e_i = 0

            def do_norm(i):
                rmsnorm(xs[i % 2], xb[i % 2], g_ap, xn, xnb, sq2, sqb, pss, pssb, rs, rsb)

            B.dma("sp", xs[0], xT_tile(*tiles[0]), writes=[xb[0]])
            do_norm(0)
            for i, (s, t) in enumerate(tiles):
                p = i % 2
                if i + 1 < len(tiles):
                    B.dma("sp", xs[1 - p], xT_tile(*tiles[i + 1]), writes=[xb[1 - p]])
                for jj in range(FFC):
                    wbuf, wsl = ws.get(e_i)
                    e_i += 1
                    w3 = wsl[:, 0:L_in].rearrange("p (k n) -> p k n", n=256)
                    par = jj % 2
                    bg, bu = banks[par * 2], banks[par * 2 + 1]

                    def mm(e, w3=w3, bg=bg, bu=bu):
                        for kc in range(KC):
                            e.matmul(bg[:, 0:T], w3[:, kc, 0:128], xn[:, kc, :], start=(kc == 0), stop=(kc == KC - 1))
                        for kc in range(KC):
                            ins = e.matmul(bu[:, 0:T], w3[:, kc, 128:256], xn[:, kc, :], start=(kc == 0), stop=(kc == KC - 1))
                        return ins
                    B.op("pe", mm, reads=[wbuf, xnb], writes=[bankb[par * 2], bankb[par * 2 + 1]])
                    B.op("act", lambda e, bg=bg, par=par: e.activation(out=sg[par], in_=bg[:, 0:T], func=AF.Silu),
                         reads=[bankb[par * 2]], writes=[sgb[par]])
                    B.op("dve", lambda e, bu=bu, par=par, jj=jj: e.tensor_tensor(out=hT[:, jj, :], in0=sg[par], in1=bu[:, 0:T],
                                                                                 op=ALU.mult),
                         reads=[sgb[par], bankb[par * 2 + 1]], writes=[hb])
                if i + 1 < len(tiles):
                    do_norm(i + 1)
                for oc in range(KC):
                    wbuf, wsl = ws.get(e_i)
                    e_i += 1
                    w3 = wsl[:, 0:L_out].rearrange("p (k n) -> p k n", n=128)
                    by, byb = banks[4 + oc % 2], bankb[4 + oc % 2]

                    def mm2(e, w3=w3, by=by):
                        for kc in range(FFC):
                            ins = e.matmul(by[:, 0:T], w3[:, kc, :], hT[:, kc, :], start=(kc == 0), stop=(kc == FFC - 1))
                        return ins
                    B.op("pe", mm2, reads=[wbuf, hb], writes=[byb])
                    B.op("dve", lambda e, by=by, oc=oc, p=p: e.scalar_tensor_tensor(
                        out=xs[p][:, oc, :], in0=by[:, 0:T], scalar=0.5, in1=xs[p][:, oc, :], op0=ALU.mult, op1=ALU.add),
                        reads=[byb], writes=[xb[p]])
                B.dma("pool", xT_tile(s, t), xs[p], reads=[xb[p]])
            B.barrier()

        def in_proj(l, samp=False):
            even = (l % 2 == 0)
            i2 = l // 2
            A.reset()
            nqk, nv = (20, 4) if even else (26, 6)
            if samp:
                nqk, nv = (6, 2) if even else (7, 2)
            if even:
                plan = [("f", j, j, None) for j in range(10)] + [("v", 10 + j, j, None) for j in range(2)] + \
                       [("r", 12 + j, 10 + j, 2 * i2) for j in range(8)] + [("r", 20 + j, 18 + j, 2 * i2 + 1) for j in range(2)] + \
                       [("v", 22 + j, 2 + j, None) for j in range(2)]
            else:
                plan = [("f", j, j, None) for j in range(10)] + [("v", 10 + j, j, None) for j in range(2)] + \
                       [("f", 12 + j, 10 + j, None) for j in range(16)] + [("v", 28 + j, 2 + j, None) for j in range(4)]
            if samp:
                if even:
                    plan = [("f", 0, 0, None), ("f", 1, 1, None), ("f", 2, 2, None), ("r", 3, 3, 2 * i2), ("r", 4, 4, 2 * i2),
                            ("r", 5, 5, 2 * i2 + 1), ("v", 6, 0, None), ("v", 7, 1, None)]
                else:
                    plan = [("f", j, j, None) for j in range(7)] + [("v", 7, 0, None), ("v", 8, 1, None)]
            xs1 = A.get(F32, KC, T)
            xs = [xs1, xs1]
            xn = A.get(BF16, KC, T)
            sq2 = [A.get(BF16, min(2, KC), T) for _ in range(2)]
            rs = A.get(F32, T)
            qkt = [A.get(BF16, nqk, T) for _ in range(2)]
            vtt = [A.get(BF16, TT, nv * 128) for _ in range(2)]
            slots = [A.get(BF16, KC * 128) for _ in range(4)]
            if even:
                cs = [[A.get(F32, T) for _ in range(2)] for _ in range(2)]
                tmp = [[A.get(BF16, T)] + [A.get(F32, T) for _ in range(4)] for _ in range(2)]
            xb1 = Buf()
            xb, qkb, vtb, csb = [xb1, xb1], [Buf(), Buf()], [Buf(), Buf()], [Buf(), Buf()]
            tmpb = [[Buf() for _ in range(5)] for _ in range(2)]
            xnb, rsb = Buf(), Buf()
            sqb = [Buf(), Buf()]
            o_w, n_w, L = (c.woff_s if samp else c.woff)[l]["min"]
            tiles = [(2, t) for t in range(NT)] if samp else [tl for tl in TILES if tl[0] == 0]

            def xsrc(s, t):
                if s != 2:
                    return xT_tile(s, t)
                qd_, col = (t * T) // SQ, (t * T) % SQ
                return xg.rearrange("k (r h p) t -> r p k h t", r=4, p=128)[qd_][:, :, :, col:col + T]
            entries = []
            for _ in tiles:
                entries += [(o_w + pl[1] * 128 * L, L) for pl in plan]
            ws = WStream(B, wbf_s if samp else wbfs[l], slots, entries, wtok.get("s" if samp else (l, "b")))
            g_ap = gcol(c.DEPTH + l)
            pss, pssb = banks[7], bankb[7]
            e_i = 0
            dq = []

            def tick(flush_all=False):
                keep = []
                for ent in dq:
                    ent[0] -= 1
                    if ent[0] <= 0 or flush_all:
                        for a_, k_ in ent[1]:
                            B.op(*a_, **k_)
                    else:
                        keep.append(ent)
                dq[:] = keep

            def xload(j, s_, t_):
                if s_ != 2:
                    B.dma("sp", xs[j], xT_tile(s_, t_), writes=[xb[j]])
                    return
                qd_, col = (t_ * T) // SQ, (t_ * T) % SQ
                hh_ = GP // 128
                for k in range(NGP):
                    B.dma("sp", xs[j][:, k * hh_:(k + 1) * hh_, :],
                          xg[k, qd_ * GP:(qd_ + 1) * GP, col:col + T].rearrange("(h p) t -> p h t", p=128), writes=[xb[j]])
            xload(0, *tiles[0])
            for i, (s, t) in enumerate(tiles):
                p = i % 2
                if even:
                    B.dma("sp", cs[p][0], cosT[:, t * T:(t + 1) * T], writes=[csb[p]])
                    B.dma("sp", cs[p][1], sinT[:, t * T:(t + 1) * T], writes=[csb[p]])
                rmsnorm(xs[p], xb[p], g_ap, xn, xnb, sq2, sqb, pss, pssb, rs, rsb)
                if i + 1 < len(tiles):
                    xload(1 - p, *tiles[i + 1])
                nf = 0
                nvv = 0
                nr = 0
                for kind, wc, oi, gi in plan:
                    tick()
                    wbuf, wsl = ws.get(e_i)
                    e_i += 1
                    w3 = wsl[:, 0:L].rearrange("p (k n) -> p k n", n=128)
                    if kind in ("f", "r"):
                        bk = nf % 3
                        nf += 1
                        bank = banks[bk]

                        def mm(e, w3=w3, bank=bank):
                            for kc in range(KC):
                                ins = e.matmul(bank[:, 0:T], w3[:, kc, :], xn[:, kc, :], start=(kc == 0), stop=(kc == KC - 1))
                            return ins
                        B.op("pe", mm, reads=[wbuf, xnb], writes=[bankb[bk]])
                        if kind == "f":
                            B.op("act", lambda e, bank=bank, oi=oi, p=p: e.copy(out=qkt[p][:, oi, :], in_=bank[:, 0:T]),
                                 reads=[bankb[bk]], writes=[qkb[p]])
                        else:
                            q2 = nr % 2
                            nr += 1
                            sqh, rr, qn, t1, t2 = tmp[q2]
                            bsq, brr, bqn, bt1, bt2 = tmpb[q2]
                            B.op("act", lambda e, bank=bank, sqh=sqh: e.activation(out=sqh, in_=bank[:, 0:T], func=AF.Square),
                                 reads=[bankb[bk]], writes=[bsq])
                            la = []
                            lb = []
                            la.append((("pe", lambda e, sqh=sqh: e.matmul(banks[4][:, 0:T], ones128, sqh, start=True, stop=True)),
                                       dict(reads=[bsq, cb], writes=[bankb[4]])))
                            la.append((("act", lambda e, rr=rr: e.activation(out=rr, in_=banks[4][:, 0:T], func=AF.Sqrt, bias=EPS, scale=1.0)),
                                       dict(reads=[bankb[4]], writes=[brr])))
                            la.append((("dve", lambda e, rr=rr: e.reciprocal(out=rr, in_=rr)), dict(writes=[brr])))
                            la.append((("dve", lambda e, bank=bank, qn=qn, rr=rr, gi=gi: e.scalar_tensor_tensor(
                                out=qn, in0=bank[:, 0:T], scalar=qkg_sb[:, gi:gi + 1], in1=rr, op0=ALU.mult, op1=ALU.mult)),
                                dict(reads=[bankb[bk], brr, cb], writes=[bqn])))
                            lb.append((("pe", lambda e, qn=qn: e.matmul(banks[5][:, 0:T], pi_sb, qn, start=True, stop=True)),
                                       dict(reads=[bqn, cb], writes=[bankb[5]])))
                            lb.append((("dve", lambda e, qn=qn, t1=t1, p=p: e.tensor_tensor(out=t1, in0=qn, in1=cs[p][0], op=ALU.mult)),
                                       dict(reads=[bqn, csb[p]], writes=[bt1])))
                            lb.append((("dve", lambda e, t2=t2, p=p: e.tensor_tensor(out=t2, in0=banks[5][:, 0:T], in1=cs[p][1], op=ALU.mult)),
                                       dict(reads=[bankb[5], csb[p]], writes=[bt2])))
                            lb.append((("pool", lambda e, t1=t1, t2=t2, oi=oi, p=p: e.tensor_tensor(out=qkt[p][:, oi, :], in0=t1, in1=t2, op=ALU.add)),
                                       dict(reads=[bt1, bt2], writes=[qkb[p]])))
                            dq.append([2, la])
                            dq.append([3, lb])
                    else:
                        bk = (3, 6)[nvv % 2]
                        nvv += 1
                        bank = banks[bk]

                        def mmv(e, w3=w3, bank=bank):
                            for tt in range(TT):
                                for kc in range(KC):
                                    ins = e.matmul(bank[:, tt * 128:(tt + 1) * 128], xn[:, kc, tt * 128:(tt + 1) * 128], w3[:, kc, :],
                                                   start=(kc == 0), stop=(kc == KC - 1))
                            return ins
                        B.op("pe", mmv, reads=[wbuf, xnb], writes=[bankb[bk]])
                        B.op("dve", lambda e, bank=bank, oi=oi, p=p: e.tensor_copy(
                            out=vtt[p][:, :, oi * 128:(oi + 1) * 128], in_=bank[:, 0:T].rearrange("p (a b) -> p a b", b=128)),
                            reads=[bankb[bk]], writes=[vtb[p]])
                tick(True)
                if samp:
                    qk_d, vt_d = qks, vts
                    t0 = t * T
                else:
                    assert s == 0 or not c.BAL or True
                    qk_d, vt_d = qk[0], vt[0]
                    t0 = t * T
                if (not samp) and s == 1:
                    continue
                B.dma("pool", qk_d[0:nqk * 128, t0:t0 + T].rearrange("(c p) t -> p c t", p=128), qkt[p], reads=[qkb[p]])
                B.dma("pool", vt_d[t0:t0 + T, 0:nv * 128].rearrange("(tt p) f -> p tt f", p=128), vtt[p], reads=[vtb[p]])
            B.barrier()

        def load_v(dst, vt_a, col, buf):
            step = max(1, NB128 // 4)
            for a in range(0, NB128, step):
                b = min(NB128, a + step)
                B.dma("sp", dst[:, a:b, :], vt_a[a * 128:b * 128, col * 128:(col + 1) * 128].rearrange("(c p) d -> p c d", p=128),
                      writes=[buf])

        def attn_a(l):
            i2 = l // 2
            A.reset()
            E = A.get(F32, 10, 384)
            kT = A.get(BF16, S)
            V = A.get(BF16, NB128, 128)
            qT = [A.get(BF16, S) for _ in range(2)]
            ot = [A.get(BF16, S) for _ in range(2)]
            pex = [A.get(F32, 384) for _ in range(2)]
            pbf = [A.get(BF16, 384) for _ in range(2)]
            rl = [A.get(F32, 128) for _ in range(2)]
            Eb, kb, vb = Buf(), Buf(), Buf()
            qb, ob, pexb, pbfb, rlb = [Buf(), Buf()], [Buf(), Buf()], [Buf(), Buf()], [Buf(), Buf()], [Buf(), Buf()]
            B.dma("sp", E[:, 0:8, :], EA.rearrange("h p f -> p h f"), writes=[Eb])
            SK = c.n_even * 8
            jobs = [(qk[0], vt[0], oT[0], [(8 + kvh, kvh, [(kvh * 4 + g, kvh * 4 + g, i2 * 8 + kvh * 4 + g, kvh * 4 + g) for g in range(4)])
                                           for kvh in range(2)])]
            if c.BAL:
                B.dma("sp", E[:, 8:10, :], EAs.rearrange("h p f -> p h f"), writes=[Eb])
                jobs.append((qks, vts, oTs, [(2, 0, [(0, 8, SK + i2 * 2, 0), (1, 9, SK + i2 * 2 + 1, 1)])]))
            pend = []

            def flush():
                for a_, k_ in pend:
                    B.op(*a_, **k_)
                pend.clear()
            hh = 0
            n = 0
            for qk_a, vt_a, oT_a, groups in jobs:
                for kch, vcol, heads in groups:
                    B.dma("sp", kT, qk_a[kch * 128:(kch + 1) * 128, :], writes=[kb])
                    load_v(V, vt_a, vcol, vb)
                    for qch, h, skc, och in heads:
                        hp = hh % 2
                        hh += 1
                        B.dma("sp", qT[hp], qk_a[qch * 128:(qch + 1) * 128, :], writes=[qb[hp]])
                        for blk in range(NB128):
                            cs_ = [cc for cc in (blk - 1, blk, blk + 1) if 0 <= cc < NB128]
                            c0 = cs_[0] - (blk - 1)
                            lo, hi = c0 * 128, (c0 + len(cs_)) * 128
                            sb_i = n % 3
                            o_i = 3 + n % 2
                            pp = n % 2
                            n += 1
                            sbank, obank = banks[sb_i], banks[o_i]

                            def mm(e, cs_=cs_, c0=c0, sbank=sbank, hp=hp, blk=blk):
                                for ci, cch in enumerate(cs_):
                                    ins = e.matmul(sbank[:, (c0 + ci) * 128:(c0 + ci + 1) * 128], kT[:, cch * 128:(cch + 1) * 128],
                                                   qT[hp][:, blk * 128:(blk + 1) * 128], start=True, stop=True)
                                return ins
                            B.op("pe", mm, reads=[kb, qb[hp]], writes=[bankb[sb_i]])
                            B.op("act", lambda e, sbank=sbank, pp=pp, lo=lo, hi=hi: e.activation(
                                out=pex[pp][:, lo:hi], in_=sbank[:, lo:hi], func=AF.Exp, scale=SCALE),
                                reads=[bankb[sb_i]], writes=[pexb[pp]])
                            B.op("dve", lambda e, pp=pp, lo=lo, hi=hi, h=h: e.tensor_tensor(
                                out=pbf[pp][:, lo:hi], in0=pex[pp][:, lo:hi], in1=E[:, h, lo:hi], op=ALU.mult),
                                reads=[pexb[pp], Eb], writes=[pbfb[pp]])

                            def pv(e, cs_=cs_, c0=c0, obank=obank, pp=pp):
                                nl = len(cs_)
                                for ci, cch in enumerate(cs_):
                                    e.matmul(obank[:, 0:128], V[:, cch, :], pbf[pp][:, (c0 + ci) * 128:(c0 + ci + 1) * 128],
                                             start=(ci == 0), stop=(ci == nl - 1))
                                for ci, cch in enumerate(cs_):
                                    ins = e.matmul(obank[:, 128:256], ones_bf, pbf[pp][:, (c0 + ci) * 128:(c0 + ci + 1) * 128],
                                                   start=(ci == 0), stop=(ci == nl - 1))
                                return ins
                            flush()
                            pend.append((("pe", pv), dict(reads=[vb, pbfb[pp], cb], writes=[bankb[o_i]])))
                            sk = esink[:, skc:skc + 1]
                            pend.append((("dve", lambda e, obank=obank, pp=pp, sk=sk: e.tensor_scalar(
                                out=rl[pp], in0=obank[:, 128:256], scalar1=sk, scalar2=None, op0=ALU.add)),
                                dict(reads=[bankb[o_i], cb], writes=[rlb[pp]])))
                            pend.append((("dve", lambda e, pp=pp: e.reciprocal(out=rl[pp], in_=rl[pp])), dict(writes=[rlb[pp]])))
                            pend.append((("dve", lambda e, obank=obank, pp=pp, hp=hp, blk=blk: e.tensor_tensor(
                                out=ot[hp][:, blk * 128:(blk + 1) * 128], in0=obank[:, 0:128], in1=rl[pp], op=ALU.mult)),
                                dict(reads=[bankb[o_i], rlb[pp]], writes=[ob[hp]])))
                        flush()
                        B.dma("pool", oT_a[och * 128:(och + 1) * 128, :], ot[hp], reads=[ob[hp]])
                        pc_drip(l + 1, 2)
            B.barrier()

        def attn_b(l):
            A.reset()
            kT = A.get(BF16, S)
            V = A.get(BF16, NB128, 128)
            qT = [A.get(BF16, S) for _ in range(2)]
            ot = [A.get(BF16, S) for _ in range(2)]
            pbf = [A.get(BF16, 512) for _ in range(3)]
            rl = [A.get(F32, 512) for _ in range(2)]
            kb, vb = Buf(), Buf()
            qb, ob, rlb = [Buf(), Buf()], [Buf(), Buf()], [Buf(), Buf()]
            pbfb = [Buf(), Buf(), Buf()]
            hh = 0
            nq = 0
            QT = 512
            jobs = [(qk[0], vt[0], oT[0], [(18 + kvh, 2 + kvh, [(10 + kvh * 4 + g, 8 + kvh * 4 + g) for g in range(4)]) for kvh in range(2)])]
            if c.BAL:
                jobs.append((qks, vts, oTs, [(5, 1, [(3, 2), (4, 3)])]))
            for qk_a, vt_a, oT_a, groups in jobs:
                for kch, vcol, heads in groups:
                    B.dma("sp", kT, qk_a[kch * 128:(kch + 1) * 128, :], writes=[kb])
                    load_v(V, vt_a, vcol, vb)
                    for qch, och in heads:
                        hp = hh % 2
                        hh += 1
                        B.dma("sp", qT[hp], qk_a[qch * 128:(qch + 1) * 128, :], writes=[qb[hp]])
                        for qt in range(S // QT):
                            o_i, l_i = 4 + nq % 2, 6 + nq % 2
                            rp = nq % 2
                            nq += 1
                            obank, lbank = banks[o_i], banks[l_i]
                            qsl = qT[hp][:, qt * QT:(qt + 1) * QT]

                            def emit_s(kc, qsl=qsl):
                                sb_i = kc % 3
                                B.op("pe", lambda e, kc=kc, sb_i=sb_i, qsl=qsl: e.matmul(
                                    banks[sb_i][:, 0:QT], kT[:, kc * 128:(kc + 1) * 128], qsl, start=True, stop=True),
                                    reads=[kb, qb[hp]], writes=[bankb[sb_i]])
                            emit_s(0)
                            for kc in range(NB128):
                                if kc + 1 < NB128:
                                    emit_s(kc + 1)
                                sb_i = kc % 3
                                B.op("act", lambda e, sb_i=sb_i: e.activation(out=pbf[sb_i], in_=banks[sb_i][:, 0:QT], func=AF.Exp, scale=SCALE),
                                     reads=[bankb[sb_i]], writes=[pbfb[sb_i]])

                                def pv(e, kc=kc, sb_i=sb_i, obank=obank, lbank=lbank):
                                    e.matmul(obank[:, 0:QT], V[:, kc, :], pbf[sb_i], start=(kc == 0), stop=(kc == NB128 - 1))
                                    return e.matmul(lbank[:, 0:QT], ones_bf, pbf[sb_i], start=(kc == 0), stop=(kc == NB128 - 1))
                                edge = kc in (0, NB128 - 1)
                                B.op("pe", pv, reads=[vb, pbfb[sb_i], cb], writes=([bankb[o_i], bankb[l_i]] if edge else []))
                            B.op("dve", lambda e, lbank=lbank, rp=rp: e.reciprocal(out=rl[rp], in_=lbank[:, 0:QT]),
                                 reads=[bankb[l_i]], writes=[rlb[rp]])
                            B.op("dve", lambda e, obank=obank, rp=rp, hp=hp, qt=qt: e.tensor_tensor(
                                out=ot[hp][:, qt * QT:(qt + 1) * QT], in0=obank[:, 0:QT], in1=rl[rp], op=ALU.mult),
                                reads=[bankb[o_i], rlb[rp]], writes=[ob[hp]])
                        B.dma("pool", oT_a[och * 128:(och + 1) * 128, :], ot[hp], reads=[ob[hp]])
                        pc_drip(l + 1, 2)
            pc_drip(l + 1, 10 ** 9)
            B.barrier()

        def attn_c(l):
            li = l // 2
            A.reset()
            PT = [A.get(F32, 14, 64) for _ in range(2)]
            kT = A.get(BF16, S)
            V0 = A.get(BF16, NB128, 128)
            V1 = A.get(BF16, NB128, 128)
            qT = [A.get(BF16, S) for _ in range(2)]
            ot = [A.get(BF16, S) for _ in range(2)]
            pex = [A.get(F32, 256) for _ in range(2)]
            pbf = [A.get(BF16, 256) for _ in range(2)]
            rl = [A.get(F32, 64) for _ in range(2)]
            kb, vb = Buf(), Buf()
            ptb, qb, ob, pexb, pbfb, rlb = ([Buf(), Buf()] for _ in range(6))
            pend = []

            def flush():
                for a_, k_ in pend:
                    B.op(*a_, **k_)
                pend.clear()
            hh = 0
            n = 0
            jobs = [(qk[0], vt[0], oT[0], [(8 + kvh, kvh, [(kvh * 4 + g, EC[li, kvh * 4 + g], kvh * 4 + g) for g in range(4)]) for kvh in range(2)])]
            if c.BAL:
                jobs.append((qks, vts, oTs, [(2, 0, [(0, ECs[li, 0], 0), (1, ECs[li, 1], 1)])]))
            for qk_a, vt_a, oT_a, groups in jobs:
                for kch, vcol, heads in groups:
                    B.dma("sp", kT, qk_a[kch * 128:(kch + 1) * 128, :], writes=[kb])
                    load_v(V0, vt_a, vcol, vb)
                    B.dma("sp", V1[:, 0:NB128 - 1, :],
                          vt_a[64:64 + (NB128 - 1) * 128, vcol * 128:(vcol + 1) * 128].rearrange("(c p) d -> p c d", p=128), writes=[vb])
                    for qch, ec_a, och in heads:
                        hp = hh % 2
                        hh += 1
                        B.dma("sp", qT[hp], qk_a[qch * 128:(qch + 1) * 128, :], writes=[qb[hp]])
                        B.dma("sp", PT[hp][0:64], ec_a[0:14].rearrange("d k q -> k d q"), writes=[ptb[hp]])
                        B.dma("sp", PT[hp][64:128], ec_a[1:15].rearrange("d k q -> k d q"), writes=[ptb[hp]])
                        for r in range(ROWS):
                            rs_ = min(max(r - 4, 0), ROWS - 8)
                            d0 = rs_ - r + 7
                            sb_i = n % 3
                            o_i = 3 + n % 2
                            pp = n % 2
                            n += 1
                            sbank, obank = banks[sb_i], banks[o_i]

                            def mm(e, rs_=rs_, r=r, sbank=sbank, hp=hp):
                                for m in range(4):
                                    k0 = rs_ * 64 + m * 128
                                    ins = e.matmul(sbank[:, m * 64:(m + 1) * 64], kT[:, k0:k0 + 128], qT[hp][:, r * 64:(r + 1) * 64],
                                                   start=True, stop=True)
                                return ins
                            B.op("pe", mm, reads=[kb, qb[hp]], writes=[bankb[sb_i]])
                            B.op("act", lambda e, sbank=sbank, pp=pp: e.activation(out=pex[pp], in_=sbank[:, 0:256], func=AF.Exp, scale=SCALE),
                                 reads=[bankb[sb_i]], writes=[pexb[pp]])
                            B.op("dve", lambda e, pp=pp, hp=hp, d0=d0: e.tensor_tensor(
                                out=pbf[pp].rearrange("p (a b) -> p a b", b=64), in0=pex[pp].rearrange("p (a b) -> p a b", b=64),
                                in1=PT[hp][:, d0:d0 + 7:2, :], op=ALU.mult), reads=[pexb[pp], ptb[hp]], writes=[pbfb[pp]])

                            def pv(e, rs_=rs_, obank=obank, pp=pp):
                                for m in range(4):
                                    vv = V0[:, rs_ // 2 + m, :] if rs_ % 2 == 0 else V1[:, (rs_ - 1) // 2 + m, :]
                                    e.matmul(obank[:, 0:64], vv, pbf[pp][:, m * 64:(m + 1) * 64], start=(m == 0), stop=(m == 3))
                                for m in range(4):
                                    ins = e.matmul(obank[:, 64:128], ones_bf, pbf[pp][:, m * 64:(m + 1) * 64], start=(m == 0), stop=(m == 3))
                                return ins
                            flush()
                            pend.append((("pe", pv), dict(reads=[vb, pbfb[pp], cb], writes=[bankb[o_i]])))
                            pend.append((("dve", lambda e, obank=obank, pp=pp: e.reciprocal(out=rl[pp], in_=obank[:, 64:128])),
                                         dict(reads=[bankb[o_i]], writes=[rlb[pp]])))
                            pend.append((("dve", lambda e, obank=obank, pp=pp, hp=hp, r=r: e.tensor_tensor(
                                out=ot[hp][:, r * 64:(r + 1) * 64], in0=obank[:, 0:64], in1=rl[pp], op=ALU.mult)),
                                dict(reads=[bankb[o_i], rlb[pp]], writes=[ob[hp]])))
                        flush()
                        B.dma("pool", oT_a[och * 128:(och + 1) * 128, :], ot[hp], reads=[ob[hp]])
                        pc_drip(l + 1, 3)
            B.barrier()

        def attn_d(l):
            A.reset()
            E = A.get(F32, 15, 192)
            kT = A.get(BF16, S)
            qT = [A.get(BF16, S) for _ in range(3)]
            Vg = [A.get(BF16, S // 64, 128) for _ in range(3)]
            ol = A.get(F32, 2, S)
            ot = A.get(BF16, S)
            pex = [A.get(F32, 192) for _ in range(2)]
            pbf = [A.get(BF16, 192) for _ in range(2)]
            Eb, kb, qb, vb, olb, ob = Buf(), Buf(), Buf(), Buf(), Buf(), Buf()
            pexb, pbfb = [Buf(), Buf()], [Buf(), Buf()]
            B.dma("sp", E[0:64, 0:12, :], ED.rearrange("h p f -> p h f"), writes=[Eb])
            pend = []

            def flush():
                for a_, k_ in pend:
                    B.op(*a_, **k_)
                pend.clear()
            n = 0
            jobs = [(qk[0], vt[0], oT[0], [(22 + slot, [10 + g * 4 + slot for g in range(3)], 2 + slot, [g * 4 + slot for g in range(3)], 8 + slot)
                                           for slot in range(4)])]
            if c.BAL:
                B.dma("sp", E[0:64, 12:15, :], EDs.rearrange("h p f -> p h f"), writes=[Eb])
                jobs.append((qks, vts, oTs, [(6, [3, 4, 5], 1, [12, 13, 14], 2)]))
            for qk_a, vt_a, oT_a, slots_ in jobs:
                for kch, qchs, vcol, eidx, och in slots_:
                    B.dma("sp", kT, qk_a[kch * 128:(kch + 1) * 128, :], writes=[kb])
                    for g, dil in enumerate((1, 4, 16)):
                        B.dma("sp", qT[g], qk_a[qchs[g] * 128:(qchs[g] + 1) * 128, :], writes=[qb])
                        nb = S // dil // 64
                        for rho in range(dil):
                            B.dma("sp", Vg[g][0:64, rho * nb:(rho + 1) * nb, :],
                                  vt_a[rho:S:dil, vcol * 128:(vcol + 1) * 128].rearrange("(c k) d -> k c d", k=64), writes=[vb])
                    for g, dil in enumerate((1, 4, 16)):
                        nb = S // dil // 64
                        hh = eidx[g]
                        for rho in range(dil):
                            for i in range(nb):
                                cs_ = [cc for cc in (i - 1, i, i + 1) if 0 <= cc < nb]
                                c0 = cs_[0] - (i - 1)
                                lo, hi = c0 * 64, (c0 + len(cs_)) * 64
                                sb_i = n % 3
                                o_i = 3 + n % 2
                                pp = n % 2
                                n += 1
                                sbank, obank = banks[sb_i], banks[o_i]

                                def tsl(blk, dil=dil, rho=rho):
                                    a = blk * 64 * dil + rho
                                    return slice(a, a + 63 * dil + 1, dil) if dil > 1 else slice(a, a + 64)
                                qs = tsl(i)

                                def mm(e, cs_=cs_, c0=c0, sbank=sbank, g=g, qs=qs, tsl=tsl):
                                    for ci, cch in enumerate(cs_):
                                        ins = e.matmul(sbank[0:64, (c0 + ci) * 64:(c0 + ci + 1) * 64], kT[:, tsl(cch)], qT[g][:, qs],
                                                       start=True, stop=True)
                                    return ins
                                B.op("pe", mm, reads=[kb, qb], writes=[bankb[sb_i]])
                                B.op("act", lambda e, sbank=sbank, pp=pp, lo=lo, hi=hi: e.activation(
                                    out=pex[pp][0:64, lo:hi], in_=sbank[0:64, lo:hi], func=AF.Exp, scale=SCALE),
                                    reads=[bankb[sb_i]], writes=[pexb[pp]])
                                B.op("dve", lambda e, pp=pp, lo=lo, hi=hi, hh=hh: e.tensor_tensor(
                                    out=pbf[pp][0:64, lo:hi], in0=pex[pp][0:64, lo:hi], in1=E[0:64, hh, lo:hi], op=ALU.mult),
                                    reads=[pexb[pp], Eb], writes=[pbfb[pp]])

                                def pv(e, cs_=cs_, c0=c0, obank=obank, pp=pp, g=g, rho=rho, nb=nb):
                                    nl = len(cs_)
                                    for ci, cch in enumerate(cs_):
                                        e.matmul(obank[:, 0:64], Vg[g][0:64, rho * nb + cch, :], pbf[pp][0:64, (c0 + ci) * 64:(c0 + ci + 1) * 64],
                                                 start=(ci == 0), stop=(ci == nl - 1))
                                    for ci, cch in enumerate(cs_):
                                        ins = e.matmul(obank[:, 64:128], ones_bf[0:64, :], pbf[pp][0:64, (c0 + ci) * 64:(c0 + ci + 1) * 64],
                                                       start=(ci == 0), stop=(ci == nl - 1))
                                    return ins
                                flush()
                                pend.append((("pe", pv), dict(reads=[vb, pbfb[pp], cb], writes=[bankb[o_i]])))
                                src = obank[:, 0:128].rearrange("p (a b) -> p a b", b=64)
                                if g == 0:
                                    pend.append((("act", lambda e, src=src, qs=qs: e.copy(out=ol[:, :, qs], in_=src)),
                                                 dict(reads=[bankb[o_i]], writes=[olb])))
                                else:
                                    pend.append((("dve", lambda e, src=src, qs=qs: e.tensor_tensor(out=ol[:, :, qs], in0=src, in1=ol[:, :, qs], op=ALU.add)),
                                                 dict(reads=[bankb[o_i]], writes=[olb])))
                    flush()
                    B.op("dve", lambda e: e.reciprocal(out=ol[:, 1, :], in_=ol[:, 1, :]), writes=[olb])
                    B.op("dve", lambda e: e.tensor_tensor(out=ot, in0=ol[:, 0, :], in1=ol[:, 1, :], op=ALU.mult), reads=[olb], writes=[ob])
                    B.dma("pool", oT_a[och * 128:(och + 1) * 128, :], ot, reads=[ob])
                    pc_drip(l + 1, 3)
            pc_drip(l + 1, 10 ** 9)
            B.barrier()

        def out_proj(l):
            OKC = 16 if l % 2 == 0 else 12
            A.reset()
            xs = [A.get(F32, KC, T) for _ in range(2)]
            ots = [A.get(BF16, OKC, T) for _ in range(2)]
            slots = [A.get(BF16, c.SLOT) for _ in range(3)]
            xb, otb = [Buf(), Buf()], [Buf(), Buf()]
            o_w, n_w, L = c.woff[l]["mout"]
            tiles = [tl for tl in TILES if tl[0] == 0]
            entries = []
            for _ in tiles:
                entries += [(o_w + j * 128 * L, L) for j in range(n_w)]
            ws = WStream(B, wbfs[l], slots, entries, wtok.get((l, "b")))
            e_i = 0

            def ld(i):
                s, t = tiles[i]
                B.dma("sp", xs[i % 2], xT_tile(s, t), writes=[xb[i % 2]])
                B.dma("sp", ots[i % 2], oT[s, 0:OKC * 128, t * T:(t + 1) * T].rearrange("(c p) t -> p c t", p=128), writes=[otb[i % 2]])
            ld(0)
            for i, (s, t) in enumerate(tiles):
                p = i % 2
                if i + 1 < len(tiles):
                    ld(i + 1)
                for oc in range(KC):
                    wbuf, wsl = ws.get(e_i)
                    e_i += 1
                    w3 = wsl[:, 0:L].rearrange("p (k n) -> p k n", n=128)
                    by, byb = banks[oc % 2], bankb[oc % 2]

                    def mm(e, w3=w3, by=by, p=p):
                        for kc in range(OKC):
                            ins = e.matmul(by[:, 0:T], w3[:, kc, :], ots[p][:, kc, :], start=(kc == 0), stop=(kc == OKC - 1))
                        return ins
                    B.op("pe", mm, reads=[wbuf, otb[p]], writes=[byb])
                    B.op("dve", lambda e, by=by, oc=oc, p=p: e.tensor_tensor(out=xs[p][:, oc, :], in0=by[:, 0:T], in1=xs[p][:, oc, :], op=ALU.add),
                         reads=[byb], writes=[xb[p]])
                B.dma("pool", xT_tile(s, t), xs[p], reads=[xb[p]])
            B.barrier()

        def out_proj_s(l):
            OKC = 4 if l % 2 == 0 else 3
            A.reset()
            ots = [A.get(BF16, OKC, T) for _ in range(2)]
            yt = [A.get(F32, KC, T) for _ in range(2)]
            slots = [A.get(BF16, OKC * 128) for _ in range(4)]
            otb, ytb = [Buf(), Buf()], [Buf(), Buf()]
            o_w, n_w, L = c.woff_s[l]["mout"]
            entries = []
            for _ in range(NT):
                entries += [(o_w + j * 128 * L, L) for j in range(n_w)]
            ws = WStream(B, wbf_s, slots, entries, wtok.get("s"))
            e_i = 0

            def ld(t):
                B.dma("sp", ots[t % 2], oTs[0:OKC * 128, t * T:(t + 1) * T].rearrange("(c p) t -> p c t", p=128), writes=[otb[t % 2]])
            ld(0)
            for t in range(NT):
                p = t % 2
                if t + 1 < NT:
                    ld(t + 1)
                for oc in range(KC):
                    wbuf, wsl = ws.get(e_i)
                    e_i += 1
                    w3 = wsl[:, 0:L].rearrange("p (k n) -> p k n", n=128)
                    by, byb = banks[oc % 2], bankb[oc % 2]

                    def mm(e, w3=w3, by=by, p=p):
                        for kc in range(OKC):
                            ins = e.matmul(by[:, 0:T], w3[:, kc, :], ots[p][:, kc, :], start=(kc == 0), stop=(kc == OKC - 1))
                        return ins
                    B.op("pe", mm, reads=[wbuf, otb[p]], writes=[byb])
                    eng = "dve" if oc % 2 else "act"
                    B.op(eng, lambda e, by=by, oc=oc, p=p, eng=eng: (e.tensor_copy(out=yt[p][:, oc, :], in_=by[:, 0:T]) if eng == "dve"
                                                                     else e.copy(out=yt[p][:, oc, :], in_=by[:, 0:T])),
                         reads=[byb], writes=[ytb[p]])
                qd_, col = (t * T) // SQ, (t * T) % SQ
                B.dma("pool", part.rearrange("(r kc p) t -> r p kc t", r=4, p=128)[qd_][:, :, col:col + T], yt[p], reads=[ytb[p]])
            B.cc(ccsem, "ReduceScatter", ALU.add, RG, [part], [red[:, 0:SQ]])
            A.reset()
            xs = [A.get(F32, KC, T) for _ in range(2)]
            rt = [A.get(F32, KC, T) for _ in range(2)]
            xb, rb_ = [Buf(), Buf()], [Buf(), Buf()]
            for i, (s_, t) in enumerate([tl for tl in TILES if tl[0] == 1]):
                p = i % 2
                B.dma("sp", xs[p], xT_tile(1, t), writes=[xb[p]])
                B.dma("sp", rt[p], red.rearrange("(kc p) t -> p kc t", p=128)[:, :, t * T:(t + 1) * T], writes=[rb_[p]])
                B.op("dve", lambda e, p=p: e.tensor_tensor(out=xs[p], in0=xs[p], in1=rt[p], op=ALU.add), reads=[rb_[p]], writes=[xb[p]])
                B.dma("pool", xT_tile(1, t), xs[p], reads=[xb[p]])
            B.barrier()

        def gather_x():
            for k in range(NGP):
                B.cc(ccsem, "AllGather", ALU.bypass, RG, [xTs[k * GP:(k + 1) * GP, 0:SQ]], [xg[k][:, 0:SQ]])

        def final():
            A.reset()
            xs = [A.get(F32, KC, T) for _ in range(2)]
            xn = A.get(F32, KC, T)
            yo = [A.get(F32, TT, D) for _ in range(2)]
            sq2 = [A.get(BF16, min(2, KC), T) for _ in range(2)]
            rs = A.get(F32, T)
            xb, yb = [Buf(), Buf()], [Buf(), Buf()]
            xnb, rsb = Buf(), Buf()
            sqb = [Buf(), Buf()]
            g_ap = gcol(3 * c.DEPTH)
            G = min(4, KC)
            tiles = list(TILES)
            B.dma("sp", xs[0], xT_tile(*tiles[0]), writes=[xb[0]])
            for i, (s, t) in enumerate(tiles):
                p = i % 2
                if i + 1 < len(tiles):
                    B.dma("sp", xs[1 - p], xT_tile(*tiles[i + 1]), writes=[xb[1 - p]])
                rmsnorm(xs[p], xb[p], g_ap, xn, xnb, sq2, sqb, banks[6], bankb[6], rs, rsb)
                n = 0
                for tt in range(TT):
                    for g in range(KC // G):
                        bk = n % 4
                        n += 1

                        def tr(e, tt=tt, g=g, bk=bk):
                            for j in range(G):
                                kc = g * G + j
                                ins = e.transpose(banks[bk][:, j * 128:(j + 1) * 128], xn[:, kc, tt * 128:(tt + 1) * 128], ident)
                            return ins
                        B.op("pe", tr, reads=[xnb, cb], writes=[bankb[bk]])
                        eng = "dve" if n % 2 else "act"

                        def cp(e, tt=tt, g=g, bk=bk, p=p, eng=eng):
                            o = yo[p][:, tt, g * G * 128:(g + 1) * G * 128]
                            i_ = banks[bk][:, 0:G * 128]
                            return e.tensor_copy(out=o, in_=i_) if eng == "dve" else e.copy(out=o, in_=i_)
                        B.op(eng, cp, reads=[bankb[bk]], writes=[yb[p]])
                dsty = y_out[0] if s == 0 else ys_out
                B.dma("pool", dsty[t * T:(t + 1) * T, :].rearrange("(tt p) d -> p tt d", p=128), yo[p], reads=[yb[p]])
            B.barrier()

        ph = phases or ("precast", "etab", "in", "layers", "mix", "final")
        if "precast" in ph:
            precast()
        if "etab" in ph:
            etables()
        if "in" in ph:
            in_transpose()
        if "layers" in ph:
            for l in range(c.DEPTH):
                B.seg()
                if "noffn" not in ph:
                    ffn(l, 1)
                if "mix" in ph:
                    B.seg()
                    in_proj(l)
                    if c.BAL:
                        gather_x()
                        in_proj(l, samp=True)
                    B.seg()
                    if l % 2 == 0:
                        attn_a(l)
                        B.seg()
                        attn_b(l)
                    else:
                        attn_c(l)
                        B.seg()
                        attn_d(l)
                    B.seg()
                    out_proj(l)
                    if c.BAL:
                        out_proj_s(l)
                B.seg()
                if "noffn" not in ph:
                    ffn(l, 2)
        if "final" in ph:
            final()
        B.barrier()
        block = es.enter_context(nc.Block())
        B.emit(block)
    nc._nops = B.nops
    return nc


def _t5_bucket_np(rel):
    half, max_exact = 16, 8
    n = np.abs(rel)
    nf = np.maximum(n, 1).astype(np.float32)
    large = max_exact + (np.log(nf / np.float32(max_exact)) / np.float32(math.log(2048 / max_exact))
                         * np.float32(half - max_exact)).astype(np.int32)
    large = np.minimum(large, half - 1)
    return np.where(rel > 0, half, 0) + np.where(n < max_exact, n, large)


def make_consts(cfg):
    S = cfg.S
    out = {}
    n_pairs, n_freq = 64, 32
    pos = np.arange(S)
    row = (pos // GRID_W).astype(np.float32)
    col = (pos % GRID_W).astype(np.float32)
    omega = (np.float32(10000.0) ** (-(np.arange(n_freq, dtype=np.float32) * np.float32(2.0) / np.float32(n_pairs)))).astype(np.float32)
    ang = np.concatenate([row[:, None] * omega, col[:, None] * omega], axis=-1).astype(np.float32)
    cos = np.cos(ang).astype(np.float32)
    sin = np.sin(ang).astype(np.float32)
    out["cosT"] = np.ascontiguousarray(np.repeat(cos, 2, axis=1).T)
    out["sinT"] = np.ascontiguousarray(np.repeat(sin, 2, axis=1).T)
    pi = np.zeros((128, 128), np.float32)
    for i in range(64):
        pi[2 * i + 1, 2 * i] = -1.0
        pi[2 * i, 2 * i + 1] = 1.0
    out["piT"] = pi
    kk = np.arange(128)[:, None, None]
    cc = np.arange(3)[None, :, None]
    qq = np.arange(128)[None, None, :]
    rel = (cc - 1) * 128 + kk - qq
    bk = _t5_bucket_np(rel)
    m = (np.abs(rel) <= 128)
    ohA = np.zeros((128, 32, 3, 128), np.float32)
    for b in range(32):
        ohA[:, b] = ((bk == b) & m)
    out["ohA"] = ohA.reshape(128, 32, 384)
    out["mA"] = m.astype(np.float32).reshape(128, 384)
    kk = np.arange(64)[:, None, None]
    qq = np.arange(64)[None, None, :]
    rs = (cc - 1) * 64 + kk - qq
    m = (np.abs(rs) <= 64)
    ohD = np.zeros((64, 3, 32, 3, 64), np.float32)
    for g, dil in enumerate((1, 4, 16)):
        bk = _t5_bucket_np(rs * dil)
        for b in range(32):
            ohD[:, g, b] = ((bk == b) & m)
    out["ohD"] = ohD.reshape(64, 3, 32, 192)
    out["mD"] = m.astype(np.float32).reshape(64, 192)
    kc_ = np.arange(64)[:, None]
    qc_ = np.arange(64)[None, :]
    dci = np.clip(kc_ - qc_ + 15, 0, 30)
    cs = np.clip(qc_ - 8, 0, 48)
    m = (kc_ >= cs) & (kc_ < cs + 16)
    ohC = np.zeros((31, 64, 64), np.float32)
    for d_ in range(31):
        ohC[d_] = ((dci == d_) & m).T
    out["ohC"] = ohC
    out["mC"] = m.astype(np.float32)
    return out


def _chunks(W, kc_n, oc_n):
    return W.reshape(kc_n, 128, oc_n, 128).transpose(2, 1, 0, 3)


def pack_weights(cfg, inp):
    c = cfg
    KC, FFC, FF = c.KC, c.FFC, c.FF
    wp = np.empty((c.NW,), np.float32)

    def put(off, arr):
        a = np.ascontiguousarray(arr, dtype=np.float32).reshape(-1)
        wp[off:off + a.size] = a

    w_in = {1: inp["ffn1_w_in"], 2: inp["ffn2_w_in"]}
    w_out = {1: inp["ffn1_w_out"], 2: inp["ffn2_w_out"]}
    for l in range(c.DEPTH):
        i = l // 2
        d = {k: (v[0] + c.lbase[l], v[1], v[2]) for k, v in c.woff[l].items()}
        for which in (1, 2):
            Win = np.asarray(w_in[which][l])
            g = _chunks(Win[:, :FF], KC, FFC)
            u = _chunks(Win[:, FF:], KC, FFC)
            put(d["f%din" % which][0], np.concatenate([g, u], axis=3))
            Wout = np.asarray(w_out[which][l])
            put(d["f%dout" % which][0], _chunks(Wout, FFC, KC))
        if l % 2 == 0:
            put(d["min"][0], _chunks(np.asarray(inp["ab_w_in"][i]), KC, 24))
            put(d["mout"][0], _chunks(np.asarray(inp["ab_w_out"][i]), 16, KC))
        else:
            put(d["min"][0], _chunks(np.asarray(inp["cd_w_in"][i]), KC, 32))
            put(d["mout"][0], _chunks(np.asarray(inp["cd_w_out"][i]), 12, KC))
    return wp


def pack_weights_s(cfg, inp, qd):
    c = cfg
    KC = c.KC
    wp = np.empty((c.NWS,), np.float32)

    def put(off, arr):
        a = np.ascontiguousarray(arr, dtype=np.float32).reshape(-1)
        wp[off:off + a.size] = a

    for l in range(c.DEPTH):
        i = l // 2
        d = c.woff_s[l]
        if l % 2 == 0:
            cin = [2 * qd, 2 * qd + 1, 8 + qd // 2, 12 + 2 * qd, 13 + 2 * qd, 20 + qd // 2, 10 + qd // 2, 22 + qd // 2]
            rout = [2 * qd, 2 * qd + 1, 8 + 2 * qd, 9 + 2 * qd]
            Win, Wout = np.asarray(inp["ab_w_in"][i]), np.asarray(inp["ab_w_out"][i])
            ninc = 24
        else:
            cin = [2 * qd, 2 * qd + 1, 8 + qd // 2, 12 + qd, 16 + qd, 20 + qd, 24 + qd, 10 + qd // 2, 28 + qd]
            rout = [2 * qd, 2 * qd + 1, 8 + qd]
            Win, Wout = np.asarray(inp["cd_w_in"][i]), np.asarray(inp["cd_w_out"][i])
            ninc = 32
        put(d["min"][0], _chunks(Win, KC, ninc)[cin])
        Ws = np.concatenate([Wout[r * 128:(r + 1) * 128, :] for r in rout], axis=0)
        put(d["mout"][0], _chunks(Ws, len(rout), KC))
    return wp


def host_inputs(cfg, inp):
    c = cfg
    KC = c.KC
    shared = make_consts(c)
    shared["wpack"] = pack_weights(c, inp)

    def fm(v):
        return np.asarray(v, np.float32).reshape(KC, 128).T

    cols = [fm(inp["norm_ffn1"][l]) for l in range(c.DEPTH)] + [fm(inp["norm_mix"][l]) for l in range(c.DEPTH)] + \
           [fm(inp["norm_ffn2"][l]) for l in range(c.DEPTH)] + [fm(inp["final_norm"])]
    shared["gains"] = np.ascontiguousarray(np.concatenate(cols, axis=1))
    qkg = np.zeros((128, 2 * c.n_even), np.float32)
    for i in range(c.n_even):
        qkg[:, 2 * i] = np.asarray(inp["ab_q_gain"][i])
        qkg[:, 2 * i + 1] = np.asarray(inp["ab_k_gain"][i])
    shared["qkg"] = qkg
    shared["sink"] = np.ascontiguousarray(np.asarray(inp["ab_sink"], np.float32).reshape(-1))
    shared["t5"] = np.ascontiguousarray(np.asarray(inp["t5_table"], np.float32).reshape(-1))
    r = np.asarray(inp["cd_rpb"], np.float32)
    shared["rpb"] = (np.ascontiguousarray(r.reshape(r.shape[0], 120, 31).transpose(0, 2, 1)) if r.shape[0]
                     else np.zeros((1, 31, 120), np.float32))
    return shared


def core_inputs(cfg, inp, qd):
    c = cfg
    m = {"wpack_s": pack_weights_s(c, inp, qd)}
    t5 = np.asarray(inp["t5_table"], np.float32)
    m["t5m"] = np.ascontiguousarray(t5[:, [2 * qd, 2 * qd + 1, 8 + qd, 12 + qd, 16 + qd]].reshape(-1))
    sk = np.asarray(inp["ab_sink"], np.float32)
    m["sinkm"] = np.ascontiguousarray(sk[:, 2 * qd:2 * qd + 2].reshape(-1))
    r = np.asarray(inp["cd_rpb"], np.float32)
    m["rpbm"] = (np.ascontiguousarray(r[:, 2 * qd:2 * qd + 2].reshape(r.shape[0], 30, 31).transpose(0, 2, 1)) if r.shape[0]
                 else np.zeros((1, 31, 30), np.float32))
    return m


_NC_CACHE = {}


def kernel(**inputs):
    cfg = Cfg()
    shared = host_inputs(cfg, inputs)
    xp = np.asarray(inputs["x_prompt"], np.float32)
    xsm = np.asarray(inputs["x_sample"], np.float32)
    SQ = cfg.SQ
    percore = [core_inputs(cfg, inputs, qd) for qd in range(4)]
    in_maps = []
    for cid in range(N_CORES):
        j, qd = cid // 4, cid % 4
        m = dict(shared)
        m.update(percore[qd])
        m["x"] = xp[cid:cid + 1]
        m["xs"] = np.ascontiguousarray(xsm[j, qd * SQ:(qd + 1) * SQ])
        in_maps.append(m)
    if "nc" not in _NC_CACHE:
        _NC_CACHE["nc"] = build(cfg)
    res = run_bass_kernel_spmd(_NC_CACHE["nc"], in_maps, core_ids=list(range(N_CORES)))
    y_prompt = np.stack([res.results[cid]["y"][0] for cid in range(N_CORES)], axis=0)
    y_sample = np.stack([np.concatenate([res.results[4 * j + qd]["ys"] for qd in range(4)], axis=0)
                         for j in range(xsm.shape[0])], axis=0)
    return (y_prompt.astype(np.float32), y_sample.astype(np.float32))
```
